# Optimizing a Trainium2 kernel written in Bass

```python
import math
import jax, jax.numpy as jnp
from jax import lax
import numpy as np

D_MODEL = 1024
BATCH = 16
SEQ = 4096
DEPTH = 4

BLOCK_Q = 128
ROPE_THETA = 500000.0
ROPE_FRAC = 4
HA = 4
DA = 128
HB = 4
DB = 128
KV_LATENT = 256
HI = 8
DI = 64
TOPK_MAX = 256
HC = 4
DC = 64
DCV = 128
D_FF = 2816
CONV_W = 3
ALPHA = (2 * DEPTH) ** 0.25
BETA = (8 * DEPTH) ** -0.25
LN_EPS = 1e-5
RMS_EPS = 1e-6
IN_WIDTHS = (HA * DA, HA * DA, HA * DA, HA,
             HB * DB, KV_LATENT, HI * DI, DI, HI,
             HC * 2 * DC, HC * 2 * DC, HC * DCV)
D_IN = sum(IN_WIDTHS)

kernel_name = 'fox_dsa_diff_gated_hybrid'


def _layer_norm(x, g, b):
    xf = x.astype(jnp.float32)
    mu = jnp.mean(xf, axis=-1, keepdims=True)
    var = jnp.mean(jnp.square(xf - mu), axis=-1, keepdims=True)
    y = (xf - mu) * lax.rsqrt(var + LN_EPS) * g.astype(jnp.float32) + b.astype(jnp.float32)
    return y.astype(x.dtype)


def _rms_norm(x, g):
    xf = x.astype(jnp.float32)
    y = xf * lax.rsqrt(jnp.mean(xf * xf, axis=-1, keepdims=True) + RMS_EPS) * g.astype(jnp.float32)
    return y.astype(x.dtype)


def _partial_rope(x, pos):
    d = x.shape[-1]
    rot = d // ROPE_FRAC
    half = rot // 2
    inv_freq = ROPE_THETA ** (-jnp.arange(half, dtype=jnp.float32) / half)
    ang = pos.astype(jnp.float32)[..., None] * inv_freq
    cos = jnp.cos(ang)[:, :, None, :]
    sin = jnp.sin(ang)[:, :, None, :]
    xf = x.astype(jnp.float32)
    x1, x2 = xf[..., :half], xf[..., half:rot]
    out = jnp.concatenate([x1 * cos - x2 * sin, x2 * cos + x1 * sin, xf[..., rot:]], axis=-1)
    return out.astype(x.dtype)


def _causal_mask(q0, q1):
    return (q0 + jnp.arange(q1 - q0))[:, None] >= jnp.arange(q1)[None, :]


def _forgetting_attention(q, k, v, log_f):
    B, S, H, D = q.shape
    scale = D ** -0.5
    cum = jnp.cumsum(log_f, axis=1).transpose(0, 2, 1)
    outs = []
    for i in range(S // BLOCK_Q):
        q0, q1 = i * BLOCK_Q, (i + 1) * BLOCK_Q
        s = jnp.einsum('bqhd,bkhd->bhqk', q[:, q0:q1], k[:, :q1],
                       preferred_element_type=jnp.float32) * scale
        s = s + cum[:, :, q0:q1, None] - cum[:, :, None, :q1]
        s = jnp.where(_causal_mask(q0, q1), s, -jnp.inf)
        p = jax.nn.softmax(s, axis=-1).astype(v.dtype)
        outs.append(jnp.einsum('bhqk,bkhd->bqhd', p, v[:, :q1]))
    return jnp.concatenate(outs, axis=1)


def _indexed_sparse_attention(q, k, v, q_idx, k_idx, w_idx, top_k):
    B, S, H, D = q.shape
    scale = D ** -0.5
    gather = jax.vmap(lambda t, i: t[i])
    outs = []
    for i in range(S // BLOCK_Q):
        q0, q1 = i * BLOCK_Q, (i + 1) * BLOCK_Q
        qpos = q0 + jnp.arange(BLOCK_Q)
        r = jax.nn.relu(jnp.einsum('bqhd,bkd->bqhk', q_idx[:, q0:q1], k_idx[:, :q1],
                                   preferred_element_type=jnp.float32) * (DI ** -0.5))
        score = jnp.einsum('bqhk,bqh->bqk', r, w_idx[:, q0:q1].astype(jnp.float32))
        score = jnp.where(_causal_mask(q0, q1), score, -jnp.inf)
        kk = min(top_k, q1)
        _, sel = lax.top_k(score, kk)
        ks = gather(k[:, :q1], sel)
        vs = gather(v[:, :q1], sel)
        s = jnp.einsum('bqhd,bqkd->bhqk', q[:, q0:q1], ks,
                       preferred_element_type=jnp.float32) * scale
        valid = sel <= qpos[None, :, None]
        s = jnp.where(valid[:, None], s, -jnp.inf)
        p = jax.nn.softmax(s, axis=-1).astype(v.dtype)
        outs.append(jnp.einsum('bhqk,bqkd->bqhd', p, vs))
    return jnp.concatenate(outs, axis=1)


def _differential_attention(q, k, v, lam):
    B, S = q.shape[:2]
    scale = q.shape[-1] ** -0.5
    outs = []
    for i in range(S // BLOCK_Q):
        q0, q1 = i * BLOCK_Q, (i + 1) * BLOCK_Q
        s = jnp.einsum('bqhcd,bkhcd->bhcqk', q[:, q0:q1], k[:, :q1],
                       preferred_element_type=jnp.float32) * scale
        s = jnp.where(_causal_mask(q0, q1), s, -jnp.inf)
        p = jax.nn.softmax(s, axis=-1)
        a = (p[:, :, 0] - lam * p[:, :, 1]).astype(v.dtype)
        outs.append(jnp.einsum('bhqk,bkhd->bqhd', a, v[:, :q1]))
    return jnp.concatenate(outs, axis=1)


def _hybrid_mixer(x, pos, w_in, b_f, kv_norm_g, w_ukv, lam_qk, diff_norm_g, w_gate,
                  w_br_a, w_br_b, w_br_c, w_out, lam_init, top_k):
    B, S, _ = x.shape
    splits = np.cumsum(IN_WIDTHS)[:-1].tolist()
    (aq, ak, av, af, bq, bc, biq, bik, biw, cq, ck, cv) = jnp.split(x @ w_in, splits, axis=-1)
    log_f = jax.nn.log_sigmoid(af.astype(jnp.float32) + b_f.astype(jnp.float32))
    o_a = _forgetting_attention(aq.reshape(B, S, HA, DA), ak.reshape(B, S, HA, DA),
                                av.reshape(B, S, HA, DA), log_f).reshape(B, S, HA * DA)
    q_b = _partial_rope(bq.reshape(B, S, HB, DB), pos)
    kv = _rms_norm(bc, kv_norm_g) @ w_ukv
    k_b = _partial_rope(kv[:, :, None, :DB], pos)[:, :, 0]
    v_b = kv[..., DB:]
    qi = _partial_rope(biq.reshape(B, S, HI, DI), pos)
    ki = _partial_rope(bik[:, :, None, :], pos)[:, :, 0]
    o_b = _indexed_sparse_attention(q_b, k_b, v_b, qi, ki, biw * (HI ** -0.5),
                                    top_k).reshape(B, S, HB * DB)
    q_c = _partial_rope(cq.reshape(B, S, 2 * HC, DC), pos).reshape(B, S, HC, 2, DC)
    k_c = _partial_rope(ck.reshape(B, S, 2 * HC, DC), pos).reshape(B, S, HC, 2, DC)
    v_c = cv.reshape(B, S, HC, DCV)
    lq = lam_qk.astype(jnp.float32)
    lam = jnp.exp(jnp.sum(lq[0] * lq[1])) - jnp.exp(jnp.sum(lq[2] * lq[3])) + lam_init
    o_c = _differential_attention(q_c, k_c, v_c, lam)
    o_c = (_rms_norm(o_c, diff_norm_g) * (1.0 - lam_init)).reshape(B, S, HC * DCV)
    g_a, g_b, g_c = jnp.split(jax.nn.sigmoid(x @ w_gate), 3, axis=-1)
    merged = g_a * (o_a @ w_br_a) + g_b * (o_b @ w_br_b) + g_c * (o_c @ w_br_c)
    return merged @ w_out


def _conv_gated_mlp(x, w_up, conv_w, conv_b, w_down):
    gate, val = jnp.split(x @ w_up, 2, axis=-1)
    gate = lax.conv_general_dilated(
        gate, conv_w[:, None, :].astype(gate.dtype), window_strides=(1,),
        padding=((CONV_W - 1, 0),), dimension_numbers=('NWC', 'WIO', 'NWC'),
        feature_group_count=gate.shape[-1]) + conv_b
    return (jax.nn.silu(gate) * val) @ w_down


def setup_inputs(seed: int = 0) -> dict:
    key = jax.random.key(seed)
    ks = jax.random.split(key, 26)
    f32 = jnp.float32

    def nrm(k, shape, fan_in, gain=1.0):
        return jax.random.normal(k, shape, f32) * (gain * fan_in ** -0.5)

    def gain_vec(k, shape):
        return 1.0 + 0.02 * jax.random.normal(k, shape, f32)

    def small(k, shape):
        return 0.02 * jax.random.normal(k, shape, f32)

    x = jax.random.normal(ks[0], (BATCH, SEQ, D_MODEL), f32)
    offs = jax.random.randint(ks[1], (BATCH, 1), 0, SEQ, dtype=jnp.int32)
    positions = offs + jnp.arange(SEQ, dtype=jnp.int32)[None, :]
    b_f = jnp.linspace(2.0, 6.0, HA, dtype=f32)[None, :] + 0.1 * jax.random.normal(ks[4], (DEPTH, HA), f32)
    return {
        'x': x,
        'positions': positions,
        'ln_in_g': gain_vec(ks[2], (D_MODEL,)),
        'ln_in_b': small(ks[3], (D_MODEL,)),
        'w_in': nrm(ks[5], (DEPTH, D_MODEL, D_IN), D_MODEL),
        'b_f': b_f,
        'kv_norm_g': gain_vec(ks[6], (DEPTH, KV_LATENT)),
        'w_ukv': nrm(ks[7], (DEPTH, KV_LATENT, 2 * DB), KV_LATENT),
        'lam_qk': 0.1 * jax.random.normal(ks[8], (DEPTH, 4, DC), f32),
        'diff_norm_g': gain_vec(ks[9], (DEPTH, DCV)),
        'w_gate': nrm(ks[10], (DEPTH, D_MODEL, 3 * D_MODEL), D_MODEL),
        'w_br_a': nrm(ks[11], (DEPTH, HA * DA, D_MODEL), HA * DA, BETA),
        'w_br_b': nrm(ks[12], (DEPTH, HB * DB, D_MODEL), HB * DB, BETA),
        'w_br_c': nrm(ks[13], (DEPTH, HC * DCV, D_MODEL), HC * DCV, BETA),
        'w_out': nrm(ks[14], (DEPTH, D_MODEL, D_MODEL), D_MODEL, BETA),
        'ln1_g': gain_vec(ks[15], (DEPTH, D_MODEL)),
        'ln1_b': small(ks[16], (DEPTH, D_MODEL)),
        'w_up': nrm(ks[17], (DEPTH, D_MODEL, 2 * D_FF), D_MODEL),
        'conv_w': nrm(ks[18], (DEPTH, CONV_W, D_FF), CONV_W),
        'conv_b': small(ks[19], (DEPTH, D_FF)),
        'w_down': nrm(ks[20], (DEPTH, D_FF, D_MODEL), D_FF, BETA),
        'ln2_g': gain_vec(ks[21], (DEPTH, D_MODEL)),
        'ln2_b': small(ks[22], (DEPTH, D_MODEL)),
    }


def reference(x, positions, ln_in_g, ln_in_b, w_in, b_f, kv_norm_g, w_ukv, lam_qk, diff_norm_g,
              w_gate, w_br_a, w_br_b, w_br_c, w_out, ln1_g, ln1_b, w_up, conv_w, conv_b,
              w_down, ln2_g, ln2_b):
    top_k = min(TOPK_MAX, x.shape[1] // 4)
    h = _layer_norm(x, ln_in_g, ln_in_b)
    for l in range(DEPTH):
        lam_init = 0.8 - 0.6 * math.exp(-0.3 * l)
        mix = _hybrid_mixer(h, positions, w_in[l], b_f[l], kv_norm_g[l], w_ukv[l], lam_qk[l],
                            diff_norm_g[l], w_gate[l], w_br_a[l], w_br_b[l], w_br_c[l], w_out[l],
                            lam_init, top_k)
        h = _layer_norm(ALPHA * h + mix, ln1_g[l], ln1_b[l])
        ffn = _conv_gated_mlp(h, w_up[l], conv_w[l], conv_b[l], w_down[l])
        h = _layer_norm(ALPHA * h + ffn, ln2_g[l], ln2_b[l])
    return h
```

```python
import contextlib
import math
import numpy as np
import concourse.bass as bass
import concourse.mybir as mybir
from concourse.bass_utils import run_bass_kernel_spmd

F32 = mybir.dt.float32
BF16 = mybir.dt.bfloat16
I32 = mybir.dt.int32
AF = mybir.ActivationFunctionType
ALU = mybir.AluOpType
AX = mybir.AxisListType

D = 1024
DEPTH = 4
D_FF = 2816
NCF = D_FF // 128
ALPHA = (2 * DEPTH) ** 0.25
LN_EPS = 1e-5
RMS_EPS = 1e-6
TOPK = 256
NEG = -30000.0
NBIS = 16
TWO_PI = 6.283185307179586
C1 = 6.28125
C2 = TWO_PI - C1

SRC = dict(aq=0, ak=512, av=1024, af=1536, bq=1540, bc=2052, biq=2308, bik=2820, biw=2884,
           cq=2892, ck=3404, cv=3916)
WID = dict(aq=512, ak=512, av=512, af=4, bq=512, bc=256, biq=512, bik=64, biw=8,
           cq=512, ck=512, cv=512)
ORDER = ["aq", "ak", "av", "bq", "biq", "cq", "ck", "cv", "bc", "bik", "af", "biw"]
DST = {}
_o = 0
for _k in ORDER:
    DST[_k] = _o
    _o += WID[_k]
D_IN = _o
CH = dict(aq=0, ak=4, bq=8, biq=12, cq=16, ck=20, kb=24, ki=25)
NCH = 26


class Res:
    __slots__ = ("name", "ws", "rs", "prs", "sem", "dummy", "excl")

    def __init__(self, name="", dummy=False):
        self.name = name
        self.ws = {}
        self.rs = {}
        self.prs = {}
        self.sem = None
        self.dummy = dummy
        self.excl = False


def PRes(name):
    r = Res(name)
    r.excl = True
    return r


def _merge(a, b):
    d = dict(a)
    for k, v in b.items():
        if d.get(k, -1) < v:
            d[k] = v
    return d


class Sch:
    COMPUTE = ("pe", "act", "dve", "pool")

    def __init__(self, nc, semstack):
        self.nc = nc
        self.ops = []
        self.base = 0
        self.eng = {"pe": nc.tensor, "act": nc.scalar, "dve": nc.vector,
                    "pool": nc.gpsimd, "sp": nc.sync}
        self.esem = {e: semstack.enter_context(nc.semaphore("e_" + e)) for e in self.COMPUTE}
        self.ecnt = {e: 0 for e in self.COMPUTE}
        self.pool = [[semstack.enter_context(nc.semaphore("d%d" % i)), 0] for i in range(72)]
        self.free = {"pool": list(range(0, 24)), "sp": list(range(24, 72))}
        self.semkind = {}
        self.used = []
        self.waited = {e: {} for e in self.eng}
        self.n_ins = 0
        self.n_wait = 0

    def _key(self, idx):
        o = self.ops[idx - self.base]
        return ("d", id(o[4])) if o[4] is not None else o[0]

    def op(self, eng, fn, reads=(), writes=(), pwrites=(), slot=None):
        idx = self.base + len(self.ops)
        raw = set()
        oth = set()
        reads = [r for r in reads if not r.dummy]
        writes = [w for w in writes if not w.dummy]
        pwrites = [w for w in pwrites if not w.dummy]
        for r in reads:
            raw.update(r.ws.values())
            if r.excl:
                oth.update(v for k, v in r.rs.items() if k != eng)
        for w in writes:
            oth.update(w.ws.values())
            oth.update(w.rs.values())
        for w in pwrites:
            if w.rs:
                oth.update(w.rs.values())
            else:
                oth.update(w.prs.values())
        key = ("d", id(slot)) if slot is not None else eng
        for r in reads:
            r.rs[key] = idx
        for w in writes:
            w.prs = _merge(w.ws, w.rs)
            w.ws = {key: idx}
            w.rs = {}
        for w in pwrites:
            if w.rs:
                w.prs = _merge(w.ws, w.rs)
                w.ws = {key: idx}
                w.rs = {}
            else:
                w.ws[key] = idx
        self.ops.append([eng, fn, raw, oth, slot, False, 0])
        return idx

    def flush(self):
        ops = self.ops
        base = self.base
        last = {}
        for i, o in enumerate(ops):
            eng = o[0]
            for d in o[2]:
                if d >= base:
                    od = ops[d - base]
                    if od[4] is None:
                        od[5] = True
            for d in o[3]:
                if d >= base:
                    od = ops[d - base]
                    if od[4] is None and (od[0] != eng or eng != "pe"):
                        od[5] = True
            if o[4] is None and eng in self.ecnt:
                last[eng] = o
        for o in last.values():
            o[5] = True
        for o in ops:
            if o[4] is not None:
                s = o[4]
                if s.sem is None:
                    kind = "pool" if o[0] == "pool" else "sp"
                    s.sem = self.free[kind].pop()
                    self.semkind[s.sem] = kind
                    self.used.append(s)
                else:
                    assert self.semkind[s.sem] == ("pool" if o[0] == "pool" else "sp"), s.name
                p = self.pool[s.sem]
                p[1] += 16
                o[6] = p[1]
            elif o[5]:
                self.ecnt[o[0]] += 1
                o[6] = self.ecnt[o[0]]
        waited = self.waited
        for o in ops:
            eng = o[0]
            need = {}
            for kind, deps in ((0, o[2]), (1, o[3])):
                for d in deps:
                    if d < base:
                        continue
                    od = ops[d - base]
                    if od[4] is not None:
                        sem = self.pool[od[4].sem][0]
                        k = ("d", od[4].sem)
                    else:
                        if od[0] == eng and kind == 1 and eng == "pe":
                            continue
                        sem = self.esem[od[0]]
                        k = od[0]
                    v = od[6]
                    if waited[eng].get(k, 0) < v and need.get(k, (None, 0))[1] < v:
                        need[k] = (sem, v)
            e = self.eng[eng]
            for k, (sem, v) in need.items():
                e.wait_ge(sem, v)
                waited[eng][k] = v
                self.n_wait += 1
            ins = o[1]()
            self.n_ins += 1
            if o[4] is not None:
                ins.then_inc(self.pool[o[4].sem][0], 16)
            elif o[5]:
                ins.then_inc(self.esem[eng], 1)
        for eng, e in self.eng.items():
            for f in self.COMPUTE:
                if f != eng and waited[eng].get(f, 0) < self.ecnt[f]:
                    e.wait_ge(self.esem[f], self.ecnt[f])
                    waited[eng][f] = self.ecnt[f]
            for s in self.used:
                k = ("d", s.sem)
                v = self.pool[s.sem][1]
                if waited[eng].get(k, 0) < v:
                    e.wait_ge(self.pool[s.sem][0], v)
                    waited[eng][k] = v
        for s in self.used:
            self.free[self.semkind[s.sem]].append(s.sem)
            s.sem = None
        self.used = []
        self.base += len(ops)
        self.ops = []


class Builder:
    def __init__(self, S_len, nseq, nlayer, dbg=()):
        self.SL = S_len
        self.NSEQ = nseq
        self.L = nlayer
        self.NT = S_len * nseq
        self.TT = self.NT // 128
        self.NB = self.NT // 512
        self.TPS = S_len // 128
        self.GPS = S_len // 512
        self.dbg = set(dbg)
        self.nc = bass.Bass("TRN2", target_bir_lowering=False)
        self.gs = contextlib.ExitStack()
        self.S = None

    def un(self, n):
        self._uid = getattr(self, "_uid", 0) + 1
        return "%s_%d" % (n, self._uid)

    def I(self, eng, meth, rd=(), wr=(), pw=(), **kw):
        f = getattr(self.S.eng[eng], meth)
        self.S.op(eng, lambda: f(**kw), reads=rd, writes=wr, pwrites=pw)

    def dma(self, eng, out, in_, rd, slot, wr=(), pw=(), **kw):
        f = self.S.eng[eng].dma_start
        self.S.op(eng, lambda: f(out=out, in_=in_, **kw), reads=rd, writes=wr, pwrites=pw, slot=slot)

    def mm(self, out, lhsT, rhs, start, stop, rd, wr):
        f = self.nc.tensor.matmul
        self.S.op("pe", lambda: f(out, lhsT=lhsT, rhs=rhs, start=start, stop=stop),
                  reads=rd, writes=[wr] if start else (), pwrites=() if start else [wr])

    def tr(self, out, in_, rd, wr, first):
        f = self.nc.tensor.transpose
        idt = self.ident[:]
        self.S.op("pe", lambda: f(out, in_, idt), reads=list(rd) + [self.r_const],
                  writes=[wr] if first else (), pwrites=() if first else [wr])

    def din(self, name, shape, dt):
        return self.nc.dram_tensor(name, list(shape), dt, kind="ExternalInput").ap()

    def dscr(self, name, shape, dt):
        kind = "ExternalOutput" if name in self.dbg else "Internal"
        return self.nc.dram_tensor(name, list(shape), dt, kind=kind).ap()

    def build(self):
        nc = self.nc
        L, NT, TT = self.L, self.NT, self.TT
        with self.gs as gs:
            self.S = Sch(nc, gs)
            sbg = lambda n, s, d: gs.enter_context(nc.sbuf_tensor(self.un(n), s, d))
            self.x = self.din("x", [NT, D], F32)
            self.pos = self.din("positions", [NT], I32)
            self.ln_in_g = self.din("ln_in_g", [D], F32)
            self.ln_in_b = self.din("ln_in_b", [D], F32)
            self.w_in = self.din("w_in", [L, D, 4428], F32)
            self.b_f = self.din("b_f", [L, 4], F32)
            self.kv_norm_g = self.din("kv_norm_g", [L, 256], F32)
            self.w_ukv = self.din("w_ukv", [L, 256, 256], F32)
            self.lam_qk = self.din("lam_qk", [L, 256], F32)
            self.diff_norm_g = self.din("diff_norm_g", [L, 128], F32)
            self.w_gate = self.din("w_gate", [L, D, 3 * D], F32)
            self.w_br = [self.din("w_br_" + c, [L, 512, D], F32) for c in "abc"]
            self.w_out = self.din("w_out", [L, D, D], F32)
            self.ln1_g = self.din("ln1_g", [L, D], F32)
            self.ln1_b = self.din("ln1_b", [L, D], F32)
            self.w_up = self.din("w_up", [L, D, 2 * D_FF], F32)
            self.conv_w = self.din("conv_w", [L, 3, D_FF], F32)
            self.conv_b = self.din("conv_b", [L, D_FF], F32)
            self.w_down = self.din("w_down", [L, D_FF, D], F32)
            self.ln2_g = self.din("ln2_g", [L, D], F32)
            self.ln2_b = self.din("ln2_b", [L, D], F32)
            self.c_ident = self.din("c_ident", [128, 128], F32)
            self.c_trim = self.din("c_trim", [128, 128], F32)
            self.c_caus = self.din("c_caus", [128, 128], F32)
            self.c_utri = self.din("c_utri", [128, 128], F32)
            self.c_e127 = self.din("c_e127", [128, 128], F32)
            self.c_pw = self.din("c_pw", [NBIS], F32)
            self.c_invf = self.din("c_invf", [24], F32)
            self.out = self.nc.dram_tensor("out", [NT, D], F32, kind="ExternalOutput").ap()
            self.h_tok = self.dscr("h_tok", [NT, D], F32)
            self.hT = self.dscr("hT", [8, 128, NT], BF16)
            self.qkT = self.dscr("qkT", [NCH, 128, NT], BF16)
            self.vtok = self.dscr("vtok", [NT, 1152], BF16)
            self.lfw = self.dscr("lfw", [128, TT, 12], F32)
            self.mbT = self.dscr("mbT", [NT // 128, 128, self.SL], BF16)
            self.oT = self.dscr("oT", [12, 128, NT], BF16)
            self.actT = self.dscr("actT", [NCF, 128, NT], BF16)
            dm_ = lambda n: Res(n, dummy=True)
            self.r_h_tok, self.r_hT, self.r_qkT, self.r_vtok = dm_("h_tok"), dm_("hT"), dm_("qkT"), dm_("vtok")
            self.r_lfw, self.r_mb, self.r_oT, self.r_actT, self.r_out = dm_("lfw"), dm_("mb"), dm_("oT"), dm_("actT"), dm_("out")
            self.r_in = dm_("inputs")
            self.ident = sbg("ident", [128, 128], BF16)
            self.trim = sbg("trim", [128, 128], BF16)
            self.caus = sbg("caus", [128, 128], F32)
            self.utri = sbg("utri", [128, 128], F32)
            self.e127 = sbg("e127", [128, 128], F32)
            self.pw = sbg("pw", [128, NBIS], F32)
            self.cosT = sbg("cosT", [128, TT, 24], F32)
            self.sinT = sbg("sinT", [128, TT, 24], F32)
            self.r_const = Res("const")
            self.r_rope = Res("rope")
            rc = [Res("c%d" % i) for i in range(6)]
            self.dma("pool", self.ident[:], self.c_ident[:, :], [], rc[0], wr=[rc[0]])
            self.dma("pool", self.trim[:], self.c_trim[:, :], [], rc[1], wr=[rc[1]])
            self.dma("sp", self.caus[:], self.c_caus[:, :], [], rc[2], wr=[rc[2]])
            self.dma("sp", self.utri[:], self.c_utri[:, :], [], rc[3], wr=[rc[3]])
            self.dma("sp", self.e127[:], self.c_e127[:, :], [], rc[4], wr=[rc[4]])
            self.dma("sp", self.pw[:], self.c_pw.partition_broadcast(128), [], rc[5], wr=[rc[5]])
            plist = [(self.phase_rope, ()), (self.phase_ln_in, ())]
            for l in range(L):
                plist.append((self.phase_p1, (l,)))
                plist += [(self.phase_attn_a, (l, s)) for s in range(self.NSEQ)]
                plist += [(self.phase_idx, (l, s)) for s in range(self.NSEQ)]
                plist += [(self.phase_attn_b, (l, s)) for s in range(self.NSEQ)]
                plist += [(self.phase_attn_c, (l, s)) for s in range(self.NSEQ)]
                plist += [(self.phase_p3, (l,)), (self.phase_p4a, (l,)), (self.phase_p4b, (l, l == L - 1))]
            for fn, args in plist[:getattr(self, "maxph", 10 ** 9)]:
                fn(*args)
            self.I("sp", "nop")
            self.S.flush()
        return nc

    def phase_rope(self):
        nc, TT = self.nc, self.TT
        with contextlib.ExitStack() as ph:
            sb = lambda n, s, d: ph.enter_context(nc.sbuf_tensor(self.un(n), s, d))
            posi = sb("posi", [128, TT], I32)
            posf = sb("posf", [128, TT], F32)
            invf = sb("invf", [128, 24], F32)
            ang = sb("ang", [128, TT, 24], F32)
            ki = sb("rki", [128, TT, 24], I32)
            kf = sb("rkf", [128, TT, 24], F32)
            r1 = sb("rr1", [128, TT, 24], F32)
            r_p, r_i, r_a, r_k, r_r = Res("posi"), Res("invf"), Res("ang"), Res("rk"), Res("rr")
            self.dma("sp", posi[:], self.pos.rearrange("(j p) -> p j", p=128), [], r_p, wr=[r_p],
                     allow_slow_non_contiguous=True)
            self.dma("sp", invf[:], self.c_invf.partition_broadcast(128), [], r_i, wr=[r_i])
            r_pf = Res("posf")
            self.I("dve", "tensor_copy", rd=[r_p], wr=[r_pf], out=posf[:], in_=posi[:])
            self.I("dve", "tensor_tensor", rd=[r_pf, r_i], wr=[r_a], out=ang[:],
                   in0=posf[:].unsqueeze(2).to_broadcast([128, TT, 24]),
                   in1=invf[:].unsqueeze(1).to_broadcast([128, TT, 24]), op=ALU.mult)
            for which, dst in ((0, self.sinT), (1, self.cosT)):
                src = ang
                if which == 1:
                    self.I("dve", "tensor_scalar", rd=[r_a], wr=[r_r], out=r1[:], in0=ang[:],
                           scalar1=math.pi / 2, scalar2=None, op0=ALU.add)
                    src = r1
                rs_ = r_a if which == 0 else r_r
                r_kf = Res("kf")
                self.I("dve", "tensor_scalar", rd=[rs_], wr=[r_kf], out=kf[:], in0=src[:],
                       scalar1=1.0 / TWO_PI, scalar2=None, op0=ALU.mult)
                self.I("dve", "tensor_copy", rd=[r_kf], wr=[r_k], out=ki[:], in_=kf[:])
                self.I("dve", "tensor_copy", rd=[r_k], wr=[r_kf], out=kf[:], in_=ki[:])
                self.I("dve", "scalar_tensor_tensor", rd=[r_kf, rs_], wr=[r_r], out=r1[:], in0=kf[:],
                       scalar=-C1, in1=src[:], op0=ALU.mult, op1=ALU.add)
                self.I("dve", "scalar_tensor_tensor", rd=[r_kf, r_r], wr=[r_r], out=r1[:], in0=kf[:],
                       scalar=-C2, in1=r1[:], op0=ALU.mult, op1=ALU.add)
                self.I("dve", "tensor_scalar", rd=[r_r], wr=[r_r], out=r1[:], in0=r1[:],
                       scalar1=math.pi, scalar2=-math.pi, op0=ALU.min, op1=ALU.max)
                self.I("act", "activation", rd=[r_r], wr=[self.r_rope], out=dst[:], in_=r1[:], func=AF.Sin)
            self.S.flush()

    def ln_tile(self, env, R, r_R, gB, bB, r_gb, tt, dst, r_dst, stage, r_stage, pt, r_pt):
        st, mv, hb = env["st"], env["mv"], env["hb"]
        k = env["k"] = env.get("k", 0) + 1
        b2 = k % 2
        r_st, r_mv, r_hb = env["r_st"][b2], env["r_mv"][b2], env["r_hb"][b2]
        st_, mv_, hb_ = st[b2], mv[b2], hb[b2]
        self.I("dve", "bn_stats", rd=[r_R], wr=[r_st], out=st_[:, 0:6], in_=R[:, 0:512])
        self.I("dve", "bn_stats", rd=[r_R], pw=[r_st], out=st_[:, 6:12], in_=R[:, 512:1024])
        self.I("dve", "bn_aggr", rd=[r_st], wr=[r_mv], out=mv_[:, 0:2], in_=st_[:, 0:12])
        self.I("dve", "tensor_scalar", rd=[r_mv], wr=[r_mv], out=mv_[:, 2:3], in0=mv_[:, 1:2],
               scalar1=LN_EPS, scalar2=None, op0=ALU.add)
        self.I("act", "activation", rd=[r_mv], wr=[r_mv], out=mv_[:, 3:4], in_=mv_[:, 2:3], func=AF.Sqrt)
        self.I("dve", "reciprocal", rd=[r_mv], wr=[r_mv], out=mv_[:, 4:5], in_=mv_[:, 3:4])
        self.I("dve", "scalar_tensor_tensor", rd=[r_R, r_mv] + list(r_gb), wr=[r_R], out=R[:], in0=R[:],
               scalar=mv_[:, 0:1], in1=gB[:], op0=ALU.subtract, op1=ALU.mult)
        self.I("dve", "scalar_tensor_tensor", rd=[r_R, r_mv] + list(r_gb), wr=[r_R], out=R[:], in0=R[:],
               scalar=mv_[:, 4:5], in1=bB[:], op0=ALU.mult, op1=ALU.add)
        self.dma("sp", dst[tt * 128:(tt + 1) * 128, :], R[:], [r_R], r_R, pw=[r_dst])
        if stage is None:
            return
        self.I("act", "copy", rd=[r_R], wr=[r_hb], out=hb_[:], in_=R[:])
        for c in range(8):
            self.tr(pt[:, c * 128:(c + 1) * 128], hb_[:, c * 128:(c + 1) * 128], [r_hb], r_pt, c == 0)
        i = tt % 4
        self.I("act", "copy", rd=[r_pt], pw=[r_stage], out=stage[:, :, i * 128:(i + 1) * 128],
               in_=pt[:].rearrange("p (c t) -> p c t", c=8))

    def ln_env(self, sb):
        env = dict(st=[sb("ln_st%d" % i, [128, 12], F32) for i in range(2)],
                   mv=[sb("ln_mv%d" % i, [128, 8], F32) for i in range(2)],
                   hb=[sb("ln_hb%d" % i, [128, 1024], BF16) for i in range(2)],
                   r_st=[Res("st0"), Res("st1")], r_mv=[Res("mv0"), Res("mv1")],
                   r_hb=[Res("hb0"), Res("hb1")])
        return env

    def load_gb(self, sb, g_ap, b_ap, tag):
        gB = sb("gB" + tag, [128, 1024], F32)
        bB = sb("bB" + tag, [128, 1024], F32)
        r_g, r_b = Res("gB"), Res("bB")
        self.dma("sp", gB[:], g_ap.partition_broadcast(128), [self.r_in], r_g, wr=[r_g])
        self.dma("sp", bB[:], b_ap.partition_broadcast(128), [self.r_in], r_b, wr=[r_b])
        return gB, bB, [r_g, r_b]

    def phase_ln_in(self):
        nc, TT = self.nc, self.TT
        with contextlib.ExitStack() as ph:
            sb = lambda n, s, d: ph.enter_context(nc.sbuf_tensor(self.un(n), s, d))
            env = self.ln_env(sb)
            gB, bB, r_gb = self.load_gb(sb, self.ln_in_g, self.ln_in_b, "i")
            Rt = [sb("R%d" % i, [128, 1024], F32) for i in range(3)]
            r_Rt = [Res("R%d" % i) for i in range(3)]
            stage = [sb("hst%d" % i, [128, 8, 512], BF16) for i in range(2)]
            r_stage = [Res("hst0"), Res("hst1")]
            pt = [ph.enter_context(nc.psum_tensor("pt%d" % i, [128, 1024], BF16)) for i in range(2)]
            r_pt = [PRes("pt0"), PRes("pt1")]
            ldx = lambda t_: self.dma("sp", Rt[t_ % 3][:], self.x[t_ * 128:(t_ + 1) * 128, :], [self.r_in],
                                      r_Rt[t_ % 3], wr=[r_Rt[t_ % 3]])
            ldx(0)
            for tt in range(TT):
                R, r_R = Rt[tt % 3], r_Rt[tt % 3]
                b = tt // 4
                if tt + 1 < TT:
                    ldx(tt + 1)
                self.ln_tile(env, R, r_R, gB, bB, r_gb[0:2], tt, self.h_tok, self.r_h_tok,
                             stage[b % 2], r_stage[b % 2], pt[tt % 2], r_pt[tt % 2])
                if tt % 4 == 3:
                    self.dma("sp", self.hT[:, :, b * 512:(b + 1) * 512].rearrange("c p t -> p c t"),
                             stage[b % 2][:], [r_stage[b % 2]], r_stage[b % 2], pw=[self.r_hT])
            self.S.flush()

    def rope(self, X, Y, H, Dh, half, off, tt, tmp, r_tmp, r_X, r_Y):
        Xv = X.rearrange("p (h d) -> p h d", h=H)
        Yv = Y.rearrange("p (h d) -> p h d", h=H)
        cos = self.cosT[:, tt, off:off + half].unsqueeze(1).to_broadcast([128, H, half])
        sin = self.sinT[:, tt, off:off + half].unsqueeze(1).to_broadcast([128, H, half])
        x1, x2 = Xv[:, :, 0:half], Xv[:, :, half:2 * half]
        t = [tmp[i][:, 0:H * half].rearrange("p (h d) -> p h d", h=H) for i in range(4)]
        rr = [self.r_rope]
        TTm = lambda o, a, b, op, rd, wr=(), pw=(): self.I("dve", "tensor_tensor", rd=rd, wr=wr, pw=pw, out=o, in0=a, in1=b, op=op)
        RP = 9
        RO = 4
        TTm(t[0], x1, cos, ALU.mult, [r_X, r_Y] + rr, wr=[r_tmp[0]])
        if RO >= 2:
            TTm(t[1], x2, sin, ALU.mult, [r_X] + rr, wr=[r_tmp[1]])
        if RO >= 3:
            TTm(t[2], x2, cos, ALU.mult, [r_X] + rr, wr=[r_tmp[2]])
        if RO >= 4:
            TTm(t[3], x1, sin, ALU.mult, [r_X] + rr, wr=[r_tmp[3]])
        if RP == 1:
            return
        TTm(Yv[:, :, 0:half], t[0], t[1], ALU.subtract, [r_tmp[0], r_tmp[1], r_Y], pw=[r_Y])
        TTm(Yv[:, :, half:2 * half], t[2], t[3], ALU.add, [r_tmp[2], r_tmp[3], r_Y], pw=[r_Y])

    def phase_p1(self, l):
        nc, TT, NB = self.nc, self.TT, self.NB
        with contextlib.ExitStack() as ph:
            sb = lambda n, s, d: ph.enter_context(nc.sbuf_tensor(self.un(n), s, d))
            ps = lambda n, s, d: ph.enter_context(nc.psum_tensor(self.un(n), s, d))
            W = sb("W_in", [128, 8, D_IN], BF16)
            r_W = [Res("W%d" % k) for k in range(8)]
            Wk = sb("W_ukv", [128, 2, 256], BF16)
            r_Wk = Res("Wukv")
            wsrc = self.w_in[l].rearrange("(kc p) n -> p kc n", p=128)
            for kc in range(8):
                for j, name in enumerate(ORDER):
                    self.dma("pool", W[:, kc, DST[name]:DST[name] + WID[name]],
                             wsrc[:, kc, SRC[name]:SRC[name] + WID[name]], [self.r_in], r_W[kc],
                             wr=[r_W[kc]] if j == 0 else (), pw=() if j == 0 else [r_W[kc]])
            self.dma("pool", Wk[:], self.w_ukv[l].rearrange("(kc p) n -> p kc n", p=128), [self.r_in], r_Wk, wr=[r_Wk])
            kvg = sb("kvg", [128, 256], F32)
            bfb = sb("bfb", [128, 4], F32)
            r_kvg = Res("kvg")
            self.dma("sp", kvg[:], self.kv_norm_g[l].partition_broadcast(128), [self.r_in], r_kvg, wr=[r_kvg])
            r_bfb = Res("bfb")
            self.dma("sp", bfb[:], self.b_f[l].partition_broadcast(128), [self.r_in], r_bfb, wr=[r_bfb])
            hTb = [sb("hTb%d" % i, [128, 8, 512], BF16) for i in range(2)]
            r_hTb = [Res("hTb0"), Res("hTb1")]
            Y = [sb("Y%d" % i, [128, D_IN], BF16) for i in range(2)]
            r_Y = [Res("Y0"), Res("Y1")]
            VB = [sb("VB%d" % i, [128, 128], BF16) for i in range(2)]
            r_VB = [Res("VB0"), Res("VB1")]
            ST = [sb("ST%d" % i, [128, NCH, 512], BF16) for i in range(2)]
            r_ST = [Res("ST0"), Res("ST1")]
            LFW = sb("LFW", [128, TT, 12], F32)
            r_LFW = Res("LFW")
            tmp = [sb("rt%d" % i, [128, 64], F32) for i in range(8)]
            r_tmp = [Res("rt%d" % i) for i in range(8)]
            sm = [sb("sm%d" % i, [128, 8], F32) for i in range(2)]
            r_sm = [Res("sm0"), Res("sm1")]
            junk = sb("junk", [128, 256], F32)
            r_junk = Res("junk")
            bcn = [sb("bcn%d" % i, [128, 256], BF16) for i in range(2)]
            r_bcn = [Res("bcn0"), Res("bcn1")]
            bcnT = [sb("bcnT%d" % i, [128, 256], BF16) for i in range(2)]
            r_bcnT = [Res("bcnT0"), Res("bcnT1")]
            kid = [sb("kid%d" % i, [128, 256], BF16) for i in range(2)]
            r_kid = [Res("kid0"), Res("kid1")]
            pc = [ps("pc%d" % i, [128, 512], F32) for i in range(4)]
            r_pc = [PRes("pc%d" % i) for i in range(4)]
            ptr = [ps("ptr%d" % i, [128, 1024], BF16) for i in range(2)]
            r_ptr = [PRes("ptr0"), PRes("ptr1")]
            pkv = ps("pkv", [128, 256], F32)
            r_pkv = PRes("pkv")
            npc = 0
            ntr = 0
            LV = 9
            ldh = lambda b_: self.dma("sp", hTb[b_ % 2][:], self.hT[:, :, b_ * 512:(b_ + 1) * 512].rearrange("c p t -> p c t"),
                                      [self.r_hT], r_hTb[b_ % 2], wr=[r_hTb[b_ % 2]])
            ldh(0)
            for b in range(NB if LV > 0 else 0):
                hb_, r_hb_ = hTb[b % 2], r_hTb[b % 2]
                if b + 1 < NB:
                    ldh(b + 1)
                st_, r_st_ = ST[b % 2], r_ST[b % 2]
                for i in range(4):
                    tt = b * 4 + i
                    y, r_y = Y[tt % 2], r_Y[tt % 2]
                    t4 = tmp[0:4] if tt % 2 == 0 else tmp[4:8]
                    r_t4 = r_tmp[0:4] if tt % 2 == 0 else r_tmp[4:8]
                    sm_, r_sm_ = sm[tt % 2], r_sm[tt % 2]
                    for c in range(9):
                        n = 512 if c < 8 else D_IN - 4096
                        p, r_p = pc[npc % 4], r_pc[npc % 4]
                        npc += 1
                        for kc in range(8):
                            self.mm(p[:, 0:n], hb_[:, kc, i * 128:(i + 1) * 128], W[:, kc, c * 512:c * 512 + n],
                                    kc == 0, kc == 7, [r_hb_, r_W[kc]], r_p)
                        if c < 8:
                            self.I("act", "copy", rd=[r_p], wr=[r_y] if c == 0 else (), pw=() if c == 0 else [r_y],
                                   out=y[:, c * 512:(c + 1) * 512], in_=p[:, 0:512])
                            if LV < 2:
                                pass
                            elif c == 3:
                                self.rope(p[:, 0:512], y[:, 1536:2048], 4, 128, 16, 0, tt, t4, r_t4, r_p, r_y)
                            elif c in (4, 5, 6):
                                self.rope(p[:, 0:512], y[:, c * 512:(c + 1) * 512], 8, 64, 8, 16, tt, t4, r_t4, r_p, r_y)
                        elif LV >= 3:
                            bn_, r_bn_ = bcn[tt % 2], r_bcn[tt % 2]
                            kd_, r_kd_ = kid[tt % 2], r_kid[tt % 2]
                            self.I("dve", "memset", wr=[r_sm_], ap=sm_[:], constant=0.0)
                            self.I("act", "activation", rd=[r_p, r_sm_], wr=[r_junk], pw=[r_sm_], out=junk[:],
                                   in_=p[:, 0:256], func=AF.Square, accum_out=sm_[:, 0:1])
                            self.I("dve", "tensor_scalar", rd=[r_sm_], wr=[r_sm_], out=sm_[:, 1:2], in0=sm_[:, 0:1],
                                   scalar1=1.0 / 256, scalar2=RMS_EPS, op0=ALU.mult, op1=ALU.add)
                            self.I("act", "activation", rd=[r_sm_], wr=[r_sm_], out=sm_[:, 2:3], in_=sm_[:, 1:2], func=AF.Sqrt)
                            self.I("dve", "reciprocal", rd=[r_sm_], wr=[r_sm_], out=sm_[:, 3:4], in_=sm_[:, 2:3])
                            self.I("dve", "scalar_tensor_tensor", rd=[r_p, r_sm_, r_kvg], wr=[r_bn_], out=bn_[:],
                                   in0=p[:, 0:256], scalar=sm_[:, 3:4], in1=kvg[:], op0=ALU.mult, op1=ALU.mult)
                            self.I("act", "copy", rd=[r_p], wr=[r_kd_], out=kd_[:, 128:192], in_=p[:, 256:320])
                            self.rope(p[:, 256:320], kd_[:, 128:192], 1, 64, 8, 16, tt, t4, r_t4, r_p, r_kd_)
                            self.I("dve", "tensor_copy", rd=[r_kd_], pw=[r_kd_], out=kd_[:, 192:256], in_=kd_[:, 128:192])
                            lf = LFW[:, tt, 0:4]
                            self.I("dve", "tensor_tensor", rd=[r_p, r_bfb], wr=[r_sm_], out=sm_[:, 4:8], in0=p[:, 320:324],
                                   in1=bfb[:], op=ALU.add)
                            self.I("act", "activation", rd=[r_sm_], wr=[r_sm_], out=sm_[:, 4:8], in_=sm_[:, 4:8],
                                   func=AF.Exp, scale=-1.0)
                            self.I("act", "activation", rd=[r_sm_], wr=[r_sm_], out=sm_[:, 4:8], in_=sm_[:, 4:8],
                                   func=AF.Ln, bias=1.0, scale=1.0)
                            self.I("dve", "tensor_scalar", rd=[r_sm_], pw=[r_LFW], out=lf, in0=sm_[:, 4:8],
                                   scalar1=-1.0, scalar2=None, op0=ALU.mult)
                            self.I("act", "copy", rd=[r_p], pw=[r_LFW], out=LFW[:, tt, 4:12], in_=p[:, 324:332])
                            pt_, r_pt_ = ptr[ntr % 2], r_ptr[ntr % 2]
                            ntr += 1
                            bT, r_bT = bcnT[tt % 2], r_bcnT[tt % 2]
                            self.tr(pt_[:, 0:128], bn_[:, 0:128], [r_bn_], r_pt_, True)
                            self.tr(pt_[:, 128:256], bn_[:, 128:256], [r_bn_], r_pt_, False)
                            self.I("dve", "tensor_copy", rd=[r_pt_], wr=[r_bT], out=bT[:], in_=pt_[:, 0:256])
                            self.mm(pkv[:], bT[:, 0:128], Wk[:, 0, :], True, False, [r_bT, r_Wk], r_pkv)
                            self.mm(pkv[:], bT[:, 128:256], Wk[:, 1, :], False, True, [r_bT, r_Wk], r_pkv)
                            vb_, r_vb_ = VB[tt % 2], r_VB[tt % 2]
                            self.I("act", "copy", rd=[r_pkv], wr=[r_vb_], out=vb_[:], in_=pkv[:, 128:256])
                            self.I("act", "copy", rd=[r_pkv], pw=[r_kd_], out=kd_[:, 0:128], in_=pkv[:, 0:128])
                            self.rope(pkv[:, 0:128], kd_[:, 0:128], 1, 128, 16, 0, tt, t4, r_t4, r_pkv, r_kd_)
                    if LV < 4:
                        continue
                    groups = [("aq", 0), ("ak", 512), ("bq", 1536), ("biq", 2048), ("cq", 2560), ("ck", 3072)]
                    for gi in range(0, 6, 2):
                        pt_, r_pt_ = ptr[ntr % 2], r_ptr[ntr % 2]
                        ntr += 1
                        for g2 in range(2):
                            name, yoff = groups[gi + g2]
                            for j in range(4):
                                self.tr(pt_[:, (g2 * 4 + j) * 128:(g2 * 4 + j + 1) * 128],
                                        y[:, yoff + j * 128: yoff + (j + 1) * 128], [r_y], r_pt_, g2 == 0 and j == 0)
                        for g2 in range(2):
                            name, yoff = groups[gi + g2]
                            eng = "act" if g2 == 0 else "dve"
                            meth = "copy" if g2 == 0 else "tensor_copy"
                            self.I(eng, meth, rd=[r_pt_], pw=[r_st_],
                                   out=st_[:, CH[name]:CH[name] + 4, i * 128:(i + 1) * 128],
                                   in_=pt_[:, g2 * 512:(g2 + 1) * 512].rearrange("p (c t) -> p c t", c=4))
                    pt_, r_pt_ = ptr[ntr % 2], r_ptr[ntr % 2]
                    ntr += 1
                    kd_, r_kd_ = kid[tt % 2], r_kid[tt % 2]
                    self.tr(pt_[:, 0:128], kd_[:, 0:128], [r_kd_], r_pt_, True)
                    self.tr(pt_[:, 128:256], kd_[:, 128:256], [r_kd_], r_pt_, False)
                    self.I("dve", "tensor_copy", rd=[r_pt_], pw=[r_st_],
                           out=st_[:, CH["kb"]:CH["kb"] + 2, i * 128:(i + 1) * 128],
                           in_=pt_[:, 0:256].rearrange("p (c t) -> p c t", c=2))
                    if LV < 5:
                        continue
                    rows = slice(tt * 128, (tt + 1) * 128)
                    self.dma("sp", self.vtok[rows, 0:512], y[:, 1024:1536], [r_y], r_y, pw=[self.r_vtok])
                    self.dma("sp", self.vtok[rows, 640:1152], y[:, 3584:4096], [r_y], r_y, pw=[self.r_vtok])
                    self.dma("sp", self.vtok[rows, 512:640], VB[tt % 2][:], [r_VB[tt % 2]], r_VB[tt % 2], pw=[self.r_vtok])
                if LV >= 6:
                    self.dma("sp", self.qkT[:, :, b * 512:(b + 1) * 512].rearrange("c p t -> p c t"), st_[:],
                             [r_st_], r_st_, pw=[self.r_qkT])
            if LV >= 6:
                self.dma("sp", self.lfw[:, :, :], LFW[:], [r_LFW], r_LFW, pw=[self.r_lfw])
            self.S.flush()

    def attn_core(self, env, qT, kT, V, rd_q, rd_k, rd_v, scale, bias_fn, rd_bias, mask, fin, Gs=None):
        GPS = self.GPS
        pss, r_pss = env["pss"], env["r_pss"]
        oa, r_oa = env["OA"], env["r_OA"]
        PT, r_PT = env["PT"], env["r_PT"]
        q = env.setdefault("queue", [])
        NP = len(pss)
        NPT = len(PT)
        for G in (range(GPS) if Gs is None else Gs):
            for J in range(4 * G + 4):
                i0 = max(J - 4 * G, 0)
                n = env["n"] = env.get("n", 0) + 1
                p, r_p = pss[n % NP], r_pss[n % NP]
                pt, r_pt = PT[n % NPT], r_PT[n % NPT]
                c0 = i0 * 128
                diag = J >= 4 * G
                mul = env.get("mulmask")
                self.mm(p[:, c0:512], kT[:, J * 128:(J + 1) * 128], qT[:, G * 512 + c0:(G + 1) * 512], True,
                        (mul is not None) or (mask is None and not diag), rd_k + rd_q, r_p)
                if mul is not None:
                    pass
                elif mask is not None:
                    MB, r_MB = mask
                    for i in range(i0, 4):
                        self.mm(p[:, i * 128:(i + 1) * 128], MB[:, i, J * 128:(J + 1) * 128], self.ident[:],
                                False, i == 3, [r_MB, self.r_const], r_p)
                elif diag:
                    self.mm(p[:, c0:c0 + 128], self.ident[:], self.trim[:], False, True, [self.r_const], r_p)
                if bias_fn is None:
                    self.I("act", "activation", rd=[r_p], wr=[r_pt], out=pt[:, c0:512], in_=p[:, c0:512],
                           func=AF.Exp, scale=scale)
                else:
                    first = True
                    for ip in range(2):
                        lo, hi = max(c0, 256 * ip), 256 * (ip + 1)
                        if lo >= hi:
                            continue
                        self.I("act", "activation", rd=[r_p] + rd_bias, wr=[r_pt] if first else (),
                               pw=() if first else [r_pt], out=pt[:, lo:hi], in_=p[:, lo:hi], func=AF.Exp,
                               scale=scale, bias=bias_fn(4 * G + 2 * ip + 1, J))
                        first = False
                if mul is not None:
                    MT, r_MT = mul
                    self.I("dve", "tensor_tensor", rd=[r_pt, r_MT], wr=[r_pt], out=pt[:, c0:512], in0=pt[:, c0:512],
                           in1=MT[:, J, c0:512], op=ALU.mult)

                def stage2(G=G, J=J, i0=i0, pt=pt, r_pt=r_pt, V=V, rd_v=rd_v, fin=fin):
                    for i in range(i0, 4):
                        self.mm(oa[i][:, 0:129], pt[:, i * 128:(i + 1) * 128], V[:, J, :],
                                J == 0, J == 4 * G + i, [r_pt] + rd_v, r_oa[i])
                    if J == 4 * G + 3:
                        for i in range(4):
                            fin(G, i, oa[i][:, 0:129], r_oa[i])
                q.append(stage2)
                if len(q) > 2:
                    q.pop(0)()

    def attn_drain(self, env):
        q = env.setdefault("queue", [])
        while q:
            q.pop(0)()

    def attn_env(self, sb, ps):
        env = dict(pss=[ps("pss%d" % i, [128, 512], F32) for i in range(3)], r_pss=[PRes("pss%d" % i) for i in range(3)],
                   OA=[ps("OA%d" % i, [128, 512], F32) for i in range(4)], r_OA=[PRes("OA%d" % i) for i in range(4)],
                   PT=[sb("PT%d" % i, [128, 512], BF16) for i in range(4)], r_PT=[Res("PT%d" % i) for i in range(4)],
                   ptr=ps("aptr", [128, 1024], BF16), r_ptr=PRes("aptr"),
                   ob=[sb("ob%d" % i, [128, 512], BF16) for i in range(2)], r_ob=[Res("ob0"), Res("ob1")],
                   ost=[sb("ost%d" % i, [128, 512], BF16) for i in range(2)], r_ost=[Res("ost0"), Res("ost1")],
                   rec=[sb("rec%d" % i, [128, 8], F32) for i in range(4)], r_rec=[Res("rec%d" % i) for i in range(4)])
        return env

    def attn_store(self, env, ob, r_ob, chunk, tok0):
        k = env["k"] = env.get("k", 0) + 1
        ptr, r_ptr = env["ptr"], env["r_ptr"]
        ost, r_ost = env["ost"][k % 2], env["r_ost"][k % 2]
        for i in range(4):
            self.tr(ptr[:, i * 128:(i + 1) * 128], ob[:, i * 128:(i + 1) * 128], [r_ob], r_ptr, i == 0)
        self.I("dve", "tensor_copy", rd=[r_ptr], wr=[r_ost], out=ost[:], in_=ptr[:, 0:512])
        self.dma("sp", self.oT[chunk, :, tok0:tok0 + 512], ost[:], [r_ost], r_ost, pw=[self.r_oT])

    def load_v(self, Vt, r_V, s, col0):
        self.dma("sp", Vt[:, :, 0:128],
                 self.vtok[s * self.SL:(s + 1) * self.SL, col0:col0 + 128].rearrange("(j p) d -> p j d", p=128),
                 [self.r_vtok], r_V, wr=[r_V])

    def phase_attn_a(self, l, s):
        nc, SL, TPS = self.nc, self.SL, self.TPS
        t0 = s * SL
        with contextlib.ExitStack() as ph:
            sb = lambda n, s_, d: ph.enter_context(nc.sbuf_tensor(self.un(n), s_, d))
            ps = lambda n, s_, d: ph.enter_context(nc.psum_tensor(self.un(n), s_, d))
            LF = sb("LF", [128, TPS, 12], F32)
            r_LF = Res("LF")
            self.dma("sp", LF[:], self.lfw[:, s * TPS:(s + 1) * TPS, :], [self.r_lfw], r_LF, wr=[r_LF])
            lf = sb("lf4", [128, 4, TPS], F32)
            r_lf = Res("lf4")
            self.I("dve", "tensor_copy", rd=[r_LF], wr=[r_lf], out=lf[:], in_=LF[:, :, 0:4].rearrange("p j h -> p h j"))
            ph2 = contextlib.ExitStack()
            pcs = ph2.enter_context(nc.psum_tensor(self.un("pcs"), [128, 4 * TPS], F32))
            r_pcs = PRes("pcs")
            lf2 = lf[:].rearrange("p h j -> p (h j)")
            self.mm(pcs[:], self.utri[:], lf2, True, True, [r_lf, self.r_const], r_pcs)
            cs = sb("cs", [128, 4, TPS], F32)
            r_cs = Res("cs")
            self.I("dve", "tensor_copy", rd=[r_pcs], wr=[r_cs], out=cs[:].rearrange("p h j -> p (h j)"), in_=pcs[:])
            ex = sb("ex", [128, 4, TPS], F32)
            r_ex = Res("ex")
            self.I("dve", "memset", wr=[r_ex], ap=ex[:, :, 0:1], constant=0.0)
            for j in range(1, TPS):
                self.I("dve", "tensor_tensor", rd=[r_ex, r_cs], wr=[r_ex], out=ex[:, :, j:j + 1], in0=ex[:, :, j - 1:j],
                       in1=cs[:, :, j - 1:j], op=ALU.add)
            self.mm(pcs[:], self.e127[:], ex[:].rearrange("p h j -> p (h j)"), True, True, [r_ex, self.r_const], r_pcs)
            cum = sb("cum", [128, 4, TPS], F32)
            r_cum = Res("cum")
            self.I("dve", "tensor_tensor", rd=[r_pcs, r_cs], wr=[r_cum], out=cum[:].rearrange("p h j -> p (h j)"),
                   in0=pcs[:], in1=cs[:].rearrange("p h j -> p (h j)"), op=ALU.add)
            self.mm(pcs[:], self.e127[:], cum[:].rearrange("p h j -> p (h j)"), True, True, [r_cum, self.r_const], r_pcs)
            cend = sb("cend", [128, 4, TPS], F32)
            r_cend = Res("cend")
            self.I("dve", "tensor_copy", rd=[r_pcs], wr=[r_cend], out=cend[:].rearrange("p h j -> p (h j)"), in_=pcs[:])
            bias = sb("biasA", [128, 4, TPS, TPS], F32)
            r_bias = Res("biasA")
            for h in range(4):
                for I_ in range(TPS):
                    self.I("dve", "tensor_scalar", rd=[r_cum, r_cend], pw=[r_bias], out=bias[:, h, I_, 0:I_ + 1],
                           in0=cum[:, h, 0:I_ + 1], scalar1=-1.0, scalar2=cend[:, h, I_:I_ + 1], op0=ALU.mult, op1=ALU.add)
            self.S.flush()
            ph2.close()
            env = self.attn_env(sb, ps)
            qT = [sb("qTa%d" % i, [128, SL], BF16) for i in range(2)]
            kT = [sb("kTa%d" % i, [128, SL], BF16) for i in range(2)]
            Vt = [sb("Va%d" % i, [128, TPS, 129], BF16) for i in range(2)]
            r_q, r_k, r_v = [Res("q0"), Res("q1")], [Res("k0"), Res("k1")], [Res("v0"), Res("v1")]
            r_v1 = [Res("v10"), Res("v11")]
            for i in range(2):
                self.I("pool", "memset", wr=[r_v1[i]], ap=Vt[i][:, :, 128:129], constant=1.0)
            for h in range(4):
                b2 = h % 2
                self.dma("sp", qT[b2][:], self.qkT[CH["aq"] + h, :, t0:t0 + SL], [self.r_qkT], r_q[b2], wr=[r_q[b2]])
                self.dma("sp", kT[b2][:], self.qkT[CH["ak"] + h, :, t0:t0 + SL], [self.r_qkT], r_k[b2], wr=[r_k[b2]])
                self.load_v(Vt[b2], r_v[b2], s, h * 128)

                def fin(G, i, O, r_O, h=h):
                    k = env["fk"] = env.get("fk", 0) + 1
                    rec, r_rec = env["rec"][k % 4], env["r_rec"][k % 4]
                    ob, r_ob = env["ob"][(k - 1) // 4 % 2], env["r_ob"][(k - 1) // 4 % 2]
                    self.I("dve", "reciprocal", rd=[r_O], wr=[r_rec], out=rec[:, 0:1], in_=O[:, 128:129])
                    self.I("act", "activation", rd=[r_O, r_rec], wr=[r_ob] if i == 0 else (), pw=() if i == 0 else [r_ob],
                           out=ob[:, i * 128:(i + 1) * 128], in_=O[:, 0:128], func=AF.Identity, scale=rec[:, 0:1])
                    if i == 3:
                        self.attn_store(env, ob, r_ob, h, t0 + G * 512)

                self.attn_core(env, qT[b2][:], kT[b2][:], Vt[b2], [r_q[b2]], [r_k[b2]], [r_v[b2], r_v1[b2]],
                               128 ** -0.5, lambda I_, J, h=h: bias[:, h, I_, J:J + 1], [r_bias], None, fin)
            self.attn_drain(env)
            self.S.flush()

    def phase_attn_c(self, l, s):
        nc, SL, TPS = self.nc, self.SL, self.TPS
        t0 = s * SL
        lam_init = 0.8 - 0.6 * math.exp(-0.3 * l)
        with contextlib.ExitStack() as ph:
            sb = lambda n, s_, d: ph.enter_context(nc.sbuf_tensor(self.un(n), s_, d))
            ps = lambda n, s_, d: ph.enter_context(nc.psum_tensor(self.un(n), s_, d))
            env = self.attn_env(sb, ps)
            lq = sb("lq", [128, 256], F32)
            r_lq = Res("lq")
            self.dma("sp", lq[:], self.lam_qk[l].partition_broadcast(128), [self.r_in], r_lq, wr=[r_lq])
            lt = sb("lt", [128, 128], F32)
            r_lt = Res("lt")
            lv = sb("lv", [128, 8], F32)
            r_lv = Res("lv")
            self.I("dve", "memset", wr=[r_lv], ap=lv[:], constant=0.0)
            self.I("dve", "tensor_tensor", rd=[r_lq], wr=[r_lt], out=lt[:, 0:64], in0=lq[:, 0:64], in1=lq[:, 64:128], op=ALU.mult)
            self.I("dve", "tensor_tensor", rd=[r_lq], pw=[r_lt], out=lt[:, 64:128], in0=lq[:, 128:192], in1=lq[:, 192:256], op=ALU.mult)
            self.I("dve", "reduce_sum", rd=[r_lt], wr=[r_lv], out=lv[:, 0:1], in_=lt[:, 0:64], axis=AX.X)
            self.I("dve", "reduce_sum", rd=[r_lt, r_lv], wr=[r_lv], out=lv[:, 1:2], in_=lt[:, 64:128], axis=AX.X)
            self.I("act", "activation", rd=[r_lv], wr=[r_lv], out=lv[:, 2:4], in_=lv[:, 0:2], func=AF.Exp)
            self.I("dve", "tensor_tensor", rd=[r_lv], wr=[r_lv], out=lv[:, 4:5], in0=lv[:, 3:4], in1=lv[:, 2:3], op=ALU.subtract)
            self.I("dve", "tensor_scalar", rd=[r_lv], wr=[r_lv], out=lv[:, 5:6], in0=lv[:, 4:5], scalar1=-lam_init,
                   scalar2=None, op0=ALU.add)
            dg = sb("dg", [128, 128], F32)
            r_dg = Res("dg")
            self.dma("sp", dg[:], self.diff_norm_g[l].partition_broadcast(128), [self.r_in], r_dg, wr=[r_dg])
            self.I("dve", "tensor_scalar", rd=[r_dg], wr=[r_dg], out=dg[:], in0=dg[:], scalar1=1.0 - lam_init,
                   scalar2=None, op0=ALU.mult)
            qT = [sb("qTc%d" % i, [128, SL], BF16) for i in range(2)]
            kz = [[sb("kz%d_%d" % (c, i), [128, SL], BF16) for i in range(2)] for c in range(2)]
            r_kz = [[Res("kz"), Res("kz")] for c in range(2)]
            r_kzz = [[Res("kzz"), Res("kzz")] for c in range(2)]
            for c in range(2):
                for i in range(2):
                    for j0 in range(0, SL, 1024):
                        self.I("dve", "memset", wr=[r_kzz[c][i]] if j0 == 0 else (), pw=() if j0 == 0 else [r_kzz[c][i]],
                               ap=kz[c][i][64 * (1 - c):64 * (1 - c) + 64, j0:j0 + 1024], constant=0.0)
            Vt = [sb("Vc%d" % i, [128, TPS, 129], BF16) for i in range(2)]
            r_q, r_v = [Res("q0"), Res("q1")], [Res("v0"), Res("v1")]
            r_v1 = [Res("v10"), Res("v11")]
            on0 = [sb("on0_%d" % i, [128, 4, 128], F32) for i in range(2)]
            r_on0 = [Res("on0_0"), Res("on0_1")]
            dd = [sb("dd%d" % i, [128, 128], F32) for i in range(2)]
            r_dd = [Res("dd0"), Res("dd1")]
            jk = sb("jkc", [128, 128], F32)
            r_jk = Res("jkc")
            for i in range(2):
                self.I("pool", "memset", wr=[r_v1[i]], ap=Vt[i][:, :, 128:129], constant=1.0)
            for h in range(4):
                b2 = h % 2
                self.dma("sp", qT[b2][:], self.qkT[CH["cq"] + h, :, t0:t0 + SL], [self.r_qkT], r_q[b2], wr=[r_q[b2]])
                for c in range(2):
                    self.dma("sp", kz[c][b2][64 * c:64 * c + 64, :], self.qkT[CH["ck"] + h, 64 * c:64 * c + 64, t0:t0 + SL],
                             [self.r_qkT], r_kz[c][b2], wr=[r_kz[c][b2]])
                self.load_v(Vt[b2], r_v[b2], s, 640 + h * 128)
                for G in range(self.GPS):
                    gk = env["gk"] = env.get("gk", 0) + 1
                    o0, r_o0 = on0[gk % 2], r_on0[gk % 2]

                    def fin0(G, i, O, r_O, o0=o0, r_o0=r_o0):
                        k = env["fk"] = env.get("fk", 0) + 1
                        rec, r_rec = env["rec"][k % 4], env["r_rec"][k % 4]
                        self.I("dve", "reciprocal", rd=[r_O], wr=[r_rec], out=rec[:, 0:1], in_=O[:, 128:129])
                        self.I("act", "activation", rd=[r_O, r_rec], wr=[r_o0] if i == 0 else (), pw=() if i == 0 else [r_o0],
                               out=o0[:, i, :], in_=O[:, 0:128], func=AF.Identity, scale=rec[:, 0:1])

                    def fin1(G, i, O, r_O, o0=o0, r_o0=r_o0, h=h):
                        k = env["fk"] = env.get("fk", 0) + 1
                        rec, r_rec = env["rec"][k % 4], env["r_rec"][k % 4]
                        d_, r_d = dd[k % 2], r_dd[k % 2]
                        kk = env["ck"] = env.get("ck", 0) + 1
                        ob, r_ob = env["ob"][(kk - 1) // 4 % 2], env["r_ob"][(kk - 1) // 4 % 2]
                        self.I("dve", "memset", wr=[r_rec], ap=rec[:], constant=0.0)
                        self.I("dve", "reciprocal", rd=[r_O, r_rec], wr=[r_rec], out=rec[:, 0:1], in_=O[:, 128:129])
                        self.I("dve", "tensor_tensor", rd=[r_rec, r_lv], wr=[r_rec], out=rec[:, 1:2], in0=rec[:, 0:1],
                               in1=lv[:, 5:6], op=ALU.mult)
                        self.I("dve", "scalar_tensor_tensor", rd=[r_O, r_rec, r_o0], wr=[r_d], out=d_[:], in0=O[:, 0:128],
                               scalar=rec[:, 1:2], in1=o0[:, i, :], op0=ALU.mult, op1=ALU.add)
                        self.I("act", "activation", rd=[r_d, r_rec], wr=[r_jk], pw=[r_rec], out=jk[:], in_=d_[:],
                               func=AF.Square, accum_out=rec[:, 2:3])
                        self.I("dve", "tensor_scalar", rd=[r_rec], wr=[r_rec], out=rec[:, 3:4], in0=rec[:, 2:3],
                               scalar1=1.0 / 128, scalar2=RMS_EPS, op0=ALU.mult, op1=ALU.add)
                        self.I("act", "activation", rd=[r_rec], wr=[r_rec], out=rec[:, 4:5], in_=rec[:, 3:4], func=AF.Sqrt)
                        self.I("dve", "reciprocal", rd=[r_rec], wr=[r_rec], out=rec[:, 5:6], in_=rec[:, 4:5])
                        self.I("dve", "scalar_tensor_tensor", rd=[r_d, r_rec, r_dg], wr=[r_ob] if i == 0 else (),
                               pw=() if i == 0 else [r_ob], out=ob[:, i * 128:(i + 1) * 128], in0=d_[:],
                               scalar=rec[:, 5:6], in1=dg[:], op0=ALU.mult, op1=ALU.mult)
                        if i == 3:
                            self.attn_store(env, ob, r_ob, 8 + h, t0 + G * 512)

                    for c in range(2):
                        self.attn_core(env, qT[b2][:], kz[c][b2][:], Vt[b2],
                                       [r_q[b2]], [r_kz[c][b2], r_kzz[c][b2]], [r_v[b2], r_v1[b2]], 64 ** -0.5, None, [], None,
                                       fin0 if c == 0 else fin1, Gs=[G])
            self.attn_drain(env)
            self.S.flush()

    def phase_idx(self, l, s):
        nc, SL, TPS = self.nc, self.SL, self.TPS
        t0 = s * SL
        with contextlib.ExitStack() as ph:
            sb = lambda n, s_, d: ph.enter_context(nc.sbuf_tensor(self.un(n), s_, d))
            ps = lambda n, s_, d: ph.enter_context(nc.psum_tensor(self.un(n), s_, d))
            kiT = sb("kiT", [128, SL], BF16)
            qiT = sb("qiT", [128, 4, SL], BF16)
            wi = sb("wi", [128, TPS, 12], F32)
            r_ki, r_qi, r_wi = Res("kiT"), Res("qiT"), Res("wi")
            self.dma("sp", kiT[:], self.qkT[CH["ki"], :, t0:t0 + SL], [self.r_qkT], r_ki, wr=[r_ki])
            self.dma("sp", qiT[:], self.qkT[CH["biq"]:CH["biq"] + 4, :, t0:t0 + SL].rearrange("c p t -> p c t"),
                     [self.r_qkT], r_qi, wr=[r_qi])
            self.dma("sp", wi[:], self.lfw[:, s * TPS:(s + 1) * TPS, :], [self.r_lfw], r_wi, wr=[r_wi])
            SC = [sb("SC%d" % i, [128, SL], F32) for i in range(2)]
            r_SC = [Res("SC0"), Res("SC1")]
            MBt = [sb("MBt%d" % i, [128, SL], BF16) for i in range(2)]
            r_MBt = [Res("MBt0"), Res("MBt1")]
            Rl = [sb("Rl%d" % i, [128, 1024], F32) for i in range(3)]
            r_Rl = [Res("Rl%d" % i) for i in range(3)]
            jk = sb("jki", [128, SL], BF16)
            r_jk = Res("jki")
            bs = [sb("bs%d" % i, [128, 8 + 2 * NBIS], F32) for i in range(2)]
            r_bs = [Res("bs0"), Res("bs1")]
            bsA = [sb("bsA%d" % i, [128, NBIS], F32) for i in range(2)]
            r_bsA = [Res("bsA0"), Res("bsA1")]
            bsD = [sb("bsD%d" % i, [128, NBIS], F32) for i in range(2)]
            r_bsD = [Res("bsD0"), Res("bsD1")]
            jk2 = sb("jki2", [128, SL], BF16)
            r_jk2 = Res("jki2")
            pp = [ps("pi%d" % i, [128, 1024], F32) for i in range(3)]
            r_pp = [PRes("pi%d" % i) for i in range(3)]
            ptm = [ps("ptm%d" % i, [128, 1024], BF16) for i in range(2)]
            r_ptm = [PRes("ptm0"), PRes("ptm1")]
            MTs = [sb("MTs%d" % i, [128, TPS, 128], BF16) for i in range(2)]
            r_MTs = [Res("MTs0"), Res("MTs1")]
            cnt = {"n": 0}

            def units(I_):
                q1 = (I_ + 1) * 128
                return [(I_, c, hh, min(1024, q1 - c * 1024)) for c in range((q1 + 1023) // 1024) for hh in range(8)]

            def unit(I_, c, hh, w):
                sc, r_sc = SC[I_ % 2], r_SC[I_ % 2]
                n = cnt["n"] = cnt["n"] + 1
                p, r_p = pp[n % 3], r_pp[n % 3]
                r0 = 64 * (hh % 2)
                for j0 in range(0, w, 512):
                    w2 = min(512, w - j0)
                    self.mm(p[:, j0:j0 + w2], qiT[r0:r0 + 64, hh // 2, I_ * 128:(I_ + 1) * 128],
                            kiT[r0:r0 + 64, c * 1024 + j0:c * 1024 + j0 + w2], True, True, [r_qi, r_ki], r_p)
                if hh == 0:
                    self.I("dve", "tensor_scalar", rd=[r_p, r_wi], wr=[r_sc] if c == 0 else (),
                           pw=() if c == 0 else [r_sc], out=sc[:, c * 1024:c * 1024 + w], in0=p[:, 0:w],
                           scalar1=0.0, scalar2=wi[:, I_, 4:5], op0=ALU.max, op1=ALU.mult)
                else:
                    rl, r_rl = Rl[n % 3], r_Rl[n % 3]
                    self.I("act", "activation", rd=[r_p], wr=[r_rl], out=rl[:, 0:w], in_=p[:, 0:w], func=AF.Relu)
                    self.I("dve", "scalar_tensor_tensor", rd=[r_rl, r_wi, r_sc], pw=[r_sc],
                           out=sc[:, c * 1024:c * 1024 + w], in0=rl[:, 0:w], scalar=wi[:, I_, 4 + hh:5 + hh],
                           in1=sc[:, c * 1024:c * 1024 + w], op0=ALU.mult, op1=ALU.add)

            def final(I_):
                sc, r_sc = SC[I_ % 2], r_SC[I_ % 2]
                q1 = (I_ + 1) * 128
                self.I("dve", "tensor_tensor", rd=[r_sc, self.r_const], wr=[r_sc], out=sc[:, I_ * 128:q1],
                       in0=sc[:, I_ * 128:q1], in1=self.caus[:], op=ALU.add)

            def mb_out(I_):
                sc, r_sc = SC[I_ % 2], r_SC[I_ % 2]
                b_, r_b = bs[I_ % 2], r_bs[I_ % 2]
                q1 = (I_ + 1) * 128
                m01, r_m01 = MBt[I_ % 2], r_MBt[I_ % 2]
                self.I("dve", "tensor_scalar", rd=[r_sc, r_b], wr=[r_m01], out=m01[:, 0:q1], in0=sc[:, 0:q1],
                       scalar1=b_[:, 5:6], scalar2=None, op0=ALU.is_ge)
                st, r_st = MTs[I_ % 2], r_MTs[I_ % 2]
                for j0 in range(0, I_ + 1, 8):
                    nj = min(8, I_ + 1 - j0)
                    k = cnt["t"] = cnt.get("t", 0) + 1
                    pt_, r_pt_ = ptm[k % 2], r_ptm[k % 2]
                    for j in range(nj):
                        self.tr(pt_[:, j * 128:(j + 1) * 128], m01[:, (j0 + j) * 128:(j0 + j + 1) * 128], [r_m01], r_pt_, j == 0)
                    self.I("act", "copy", rd=[r_pt_], wr=[r_st] if j0 == 0 else (), pw=() if j0 == 0 else [r_st],
                           out=st[:, j0:j0 + nj, :], in_=pt_[:, 0:nj * 128].rearrange("p (j t) -> p j t", j=nj))
                self.dma("sp", self.mbT[s * TPS:s * TPS + I_ + 1, :, I_ * 128:(I_ + 1) * 128].rearrange("j s t -> s j t"),
                         st[:, 0:I_ + 1, :], [r_st], r_st, pw=[self.r_mb])

            for I_ in range(min(2, TPS)):
                for u in units(I_):
                    unit(*u)
                final(I_)
                self.I("dve", "memset", wr=[r_bs[I_ % 2]], ap=bs[I_ % 2][:, 5:6], constant=-1e29)
                mb_out(I_)
            if TPS > 2:
                for u in units(2):
                    unit(*u)
                final(2)
            for I_ in range(2, TPS):
                q1 = (I_ + 1) * 128
                sc, r_sc = SC[I_ % 2], r_SC[I_ % 2]
                b_, r_b = bs[I_ % 2], r_bs[I_ % 2]
                nxt = units(I_ + 1) if I_ + 1 < TPS else []
                bA, r_bA = bsA[I_ % 2], r_bsA[I_ % 2]
                bD, r_bD = bsD[I_ % 2], r_bsD[I_ % 2]
                self.I("dve", "memset", wr=[r_b], ap=b_[:], constant=0.0)
                self.I("dve", "memset", wr=[r_bA], ap=bA[:], constant=0.0)
                self.I("dve", "memset", wr=[r_bD], ap=bD[:], constant=0.0)
                self.I("dve", "reduce_max", rd=[r_sc, r_b], wr=[r_b], out=b_[:, 0:1], in_=sc[:, 0:q1], axis=AX.X)
                self.I("dve", "tensor_reduce", rd=[r_sc, r_b], wr=[r_b], out=b_[:, 1:2], in_=sc[:, 0:I_ * 128],
                       axis=AX.X, op=ALU.min)
                self.I("dve", "tensor_tensor", rd=[r_b], wr=[r_b], out=b_[:, 2:3], in0=b_[:, 0:1], in1=b_[:, 1:2],
                       op=ALU.subtract)
                self.I("dve", "tensor_scalar", rd=[r_b, self.r_const], wr=[r_b], out=b_[:, 8:8 + NBIS], in0=self.pw[:],
                       scalar1=b_[:, 2:3], scalar2=None, op0=ALU.mult)
                self.I("dve", "scalar_tensor_tensor", rd=[r_b], wr=[r_b], out=b_[:, 3:4], in0=b_[:, 2:3], scalar=0.5,
                       in1=b_[:, 1:2], op0=ALU.mult, op1=ALU.add)
                done = 0
                a = min(q1, max(128, int(0.85 * q1 / 128) * 128))
                for k in range(NBIS):
                    cc = 8 + NBIS + k
                    self.I("act", "activation", rd=[r_sc, r_b], wr=[r_jk], pw=[r_bA], out=jk[:, 0:a], in_=sc[:, 0:a],
                           func=AF.Sign, scale=-1.0, bias=b_[:, 3:4], accum_out=bA[:, k:k + 1])
                    if a < q1:
                        self.I("dve", "tensor_scalar", rd=[r_sc, r_b], wr=[r_jk2], pw=[r_bD], out=jk2[:, a:q1], in0=sc[:, a:q1],
                               scalar1=b_[:, 3:4], scalar2=0.0, op0=ALU.is_ge, op1=ALU.add, accum_out=bD[:, k:k + 1])
                    self.I("dve", "scalar_tensor_tensor", rd=[r_bA, r_bD], wr=[r_b], out=b_[:, 6:7], in0=bD[:, k:k + 1],
                           scalar=2.0, in1=bA[:, k:k + 1], op0=ALU.mult, op1=ALU.subtract)
                    self.I("dve", "tensor_scalar", rd=[r_b], wr=[r_b], out=b_[:, 4:5], in0=b_[:, 6:7],
                           scalar1=float(511.5 - a), scalar2=0.5, op0=ALU.is_ge, op1=ALU.subtract)
                    self.I("dve", "scalar_tensor_tensor", rd=[r_b], wr=[r_b], out=b_[:, 3:4], in0=b_[:, 4:5],
                           scalar=b_[:, 8 + k:9 + k], in1=b_[:, 3:4], op0=ALU.mult, op1=ALU.add)
                    upto = (len(nxt) * (k + 1)) // NBIS
                    for u in nxt[done:upto]:
                        unit(*u)
                    done = upto
                self.I("dve", "tensor_tensor", rd=[r_b], wr=[r_b], out=b_[:, 5:6], in0=b_[:, 3:4],
                       in1=b_[:, 8 + NBIS - 1:8 + NBIS], op=ALU.subtract)
                if I_ + 1 < TPS:
                    final(I_ + 1)
                mb_out(I_)
            self.S.flush()

    def phase_attn_b(self, l, s):
        nc, SL, TPS = self.nc, self.SL, self.TPS
        t0 = s * SL
        with contextlib.ExitStack() as ph:
            sb = lambda n, s_, d: ph.enter_context(nc.sbuf_tensor(self.un(n), s_, d))
            ps = lambda n, s_, d: ph.enter_context(nc.psum_tensor(self.un(n), s_, d))
            env = self.attn_env(sb, ps)
            qT = sb("qTb", [128, 4, SL], BF16)
            kT = sb("kTb", [128, SL], BF16)
            Vt = sb("Vb", [128, TPS, 129], BF16)
            r_q, r_k, r_v, r_v1 = Res("qb"), Res("kb"), Res("vb"), Res("vb1")
            self.I("pool", "memset", wr=[r_v1], ap=Vt[:, :, 128:129], constant=1.0)
            self.dma("sp", qT[:], self.qkT[CH["bq"]:CH["bq"] + 4, :, t0:t0 + SL].rearrange("c p t -> p c t"),
                     [self.r_qkT], r_q, wr=[r_q])
            self.dma("sp", kT[:], self.qkT[CH["kb"], :, t0:t0 + SL], [self.r_qkT], r_k, wr=[r_k])
            self.load_v(Vt, r_v, s, 512)
            MT = [sb("MTall%d" % i, [128, TPS, 512], BF16) for i in range(2)]
            r_MT = [Res("MTall0"), Res("MTall1")]
            for G in range(self.GPS):
                mt, r_mt = MT[G % 2], r_MT[G % 2]
                if G > 0:
                    self.dma("sp", mt[:, 0:4 * G, :],
                             self.mbT[s * TPS:s * TPS + 4 * G, :, G * 512:(G + 1) * 512].rearrange("j s t -> s j t"),
                             [self.r_mb], r_mt, wr=[r_mt])
                for j in range(4):
                    J = 4 * G + j
                    self.dma("sp", mt[:, J, j * 128:512], self.mbT[s * TPS + J, :, G * 512 + j * 128:(G + 1) * 512],
                             [self.r_mb], r_mt, wr=[r_mt] if (G == 0 and j == 0) else (),
                             pw=() if (G == 0 and j == 0) else [r_mt])
                env["mulmask"] = (mt, r_mt)
                for h in range(4):
                    def fin(G, i, O, r_O, h=h):
                        k = env["fk"] = env.get("fk", 0) + 1
                        rec, r_rec = env["rec"][k % 4], env["r_rec"][k % 4]
                        ob, r_ob = env["ob"][(k - 1) // 4 % 2], env["r_ob"][(k - 1) // 4 % 2]
                        self.I("dve", "reciprocal", rd=[r_O], wr=[r_rec], out=rec[:, 0:1], in_=O[:, 128:129])
                        self.I("act", "activation", rd=[r_O, r_rec], wr=[r_ob] if i == 0 else (), pw=() if i == 0 else [r_ob],
                               out=ob[:, i * 128:(i + 1) * 128], in_=O[:, 0:128], func=AF.Identity, scale=rec[:, 0:1])
                        if i == 3:
                            self.attn_store(env, ob, r_ob, 4 + h, t0 + G * 512)
                    self.attn_core(env, qT[:, h, :], kT[:], Vt, [r_q], [r_k], [r_v, r_v1], 128 ** -0.5, None, [],
                                   None, fin, Gs=[G])
            self.attn_drain(env)
            self.S.flush()

    def load_w(self, dst, src, r, nparts=1):
        n1 = dst.shape[1]
        step = (n1 + nparts - 1) // nparts
        for j, a in enumerate(range(0, n1, step)):
            b = min(n1, a + step)
            self.dma("pool", dst[:, a:b], src[:, a:b], [self.r_in], r[j], wr=[r[j]])

    def phase_p3(self, l):
        nc, NB = self.nc, self.NB
        with contextlib.ExitStack() as ph:
            sb = lambda n, s_, d: ph.enter_context(nc.sbuf_tensor(self.un(n), s_, d))
            ps = lambda n, s_, d: ph.enter_context(nc.psum_tensor(self.un(n), s_, d))
            Wg = sb("Wg", [128, 8, 3 * D], BF16)
            r_Wg = [Res("Wg%d" % i) for i in range(8)]
            self.load_w(Wg, self.w_gate[l].rearrange("(kc p) n -> p kc n", p=128), r_Wg, 8)
            Wb = [sb("Wb%d" % i, [128, 4, D], BF16) for i in range(3)]
            r_Wb = [[Res("Wb%d" % i)] for i in range(3)]
            for i in range(3):
                self.load_w(Wb[i], self.w_br[i][l].rearrange("(kc p) n -> p kc n", p=128), r_Wb[i], 1)
            Wo = sb("Wo", [128, 8, D], BF16)
            r_Wo = [Res("Wo0"), Res("Wo1")]
            self.load_w(Wo, self.w_out[l].rearrange("(kc p) n -> p kc n", p=128), r_Wo, 2)
            env = self.ln_env(sb)
            gB, bB, r_gb = self.load_gb(sb, self.ln1_g[l], self.ln1_b[l], "1")
            hTb = [sb("hTb0", [128, 8, 512], BF16)]
            r_hTb = [Res("hTb0")]
            oTb = [sb("oTb0", [128, 12, 512], BF16)]
            r_oTb = [Res("oTb0")]
            gx = [sb("gx%d" % i, [128, 512], F32) for i in range(3)]
            r_gx = [Res("gx%d" % i) for i in range(3)]
            tm = [sb("tm%d" % i, [128, 512], F32) for i in range(3)]
            r_tm = [Res("tm%d" % i) for i in range(3)]
            mT = sb("mT", [128, 8, 512], BF16)
            r_mT = [Res("mT%d" % i) for i in range(8)]
            Rt = [sb("R%d" % i, [128, 1024], F32) for i in range(2)]
            r_Rt = [Res("R0"), Res("R1")]
            Ht = [sb("H%d" % i, [128, 1024], F32) for i in range(2)]
            r_Ht = [Res("H0"), Res("H1")]
            stage = [sb("hst%d" % i, [128, 8, 512], BF16) for i in range(2)]
            r_stage = [Res("hst0"), Res("hst1")]
            pg = [ps("pg%d" % i, [128, 512], F32) for i in range(2)]
            r_pg = [PRes("pg0"), PRes("pg1")]
            pb = [ps("pb%d" % i, [128, 512], F32) for i in range(2)]
            r_pb = [PRes("pb0"), PRes("pb1")]
            po = [ps("po%d" % i, [128, 512], F32) for i in range(2)]
            r_po = [PRes("po0"), PRes("po1")]
            pt = [ps("pt%d" % i, [128, 1024], BF16) for i in range(2)]
            r_pt = [PRes("pt0"), PRes("pt1")]
            n = 0

            def ldb(b_):
                self.dma("sp", hTb[0][:], self.hT[:, :, b_ * 512:(b_ + 1) * 512].rearrange("c p t -> p c t"),
                         [self.r_hT], r_hTb[0], wr=[r_hTb[0]])
                self.dma("sp", oTb[0][:], self.oT[:, :, b_ * 512:(b_ + 1) * 512].rearrange("c p t -> p c t"),
                         [self.r_oT], r_oTb[0], wr=[r_oTb[0]])
            ldH = lambda t_: self.dma("sp", Ht[t_ % 2][:], self.h_tok[t_ * 128:(t_ + 1) * 128, :], [self.r_h_tok],
                                      r_Ht[t_ % 2], wr=[r_Ht[t_ % 2]])
            ldb(0)
            ldH(0)
            for b in range(NB):
                hb_, r_hb_ = hTb[0], r_hTb[0]
                ob_, r_ob_ = oTb[0], r_oTb[0]
                for dm in range(8):
                    for x in range(3):
                        n += 1
                        g_, r_g_ = pg[n % 2], r_pg[n % 2]
                        b_, r_b_ = pb[n % 2], r_pb[n % 2]
                        gs_, r_gs_ = gx[n % 3], r_gx[n % 3]
                        for kc in range(8):
                            self.mm(g_[:], Wg[:, kc, x * D + dm * 128:x * D + (dm + 1) * 128], hb_[:, kc, :],
                                    kc == 0, kc == 7, [r_Wg[kc], r_hb_], r_g_)
                        self.I("act", "activation", rd=[r_g_], wr=[r_gs_], out=gs_[:], in_=g_[:], func=AF.Sigmoid)
                        for kc in range(4):
                            self.mm(b_[:], Wb[x][:, kc, dm * 128:(dm + 1) * 128], ob_[:, 4 * x + kc, :],
                                    kc == 0, kc == 3, [r_Wb[x][0], r_ob_], r_b_)
                        t_, r_t_ = tm[x], r_tm[x]
                        self.I("dve", "tensor_tensor", rd=[r_gs_, r_b_], wr=[r_t_], out=t_[:], in0=gs_[:], in1=b_[:], op=ALU.mult)
                    self.I("pool", "tensor_tensor", rd=[r_tm[0], r_tm[1]], wr=[r_tm[0]], out=tm[0][:], in0=tm[0][:],
                           in1=tm[1][:], op=ALU.add)
                    self.I("pool", "tensor_tensor", rd=[r_tm[0], r_tm[2]], wr=[r_mT[dm]], out=mT[:, dm, :], in0=tm[0][:],
                           in1=tm[2][:], op=ALU.add)
                if b + 1 < NB:
                    ldb(b + 1)
                for i in range(4):
                    tt = b * 4 + i
                    R, r_R = Rt[tt % 2], r_Rt[tt % 2]
                    H, r_H = Ht[tt % 2], r_Ht[tt % 2]
                    for half in range(2):
                        n += 1
                        o_, r_o_ = po[n % 2], r_po[n % 2]
                        for dm in range(8):
                            self.mm(o_[:], mT[:, dm, i * 128:(i + 1) * 128], Wo[:, dm, half * 512:(half + 1) * 512],
                                    dm == 0, dm == 7, [r_mT[dm], r_Wo[dm // 4]], r_o_)
                        self.I("dve", "scalar_tensor_tensor", rd=[r_H, r_o_], wr=[r_R] if half == 0 else (),
                               pw=() if half == 0 else [r_R], out=R[:, half * 512:(half + 1) * 512],
                               in0=H[:, half * 512:(half + 1) * 512], scalar=ALPHA, in1=o_[:], op0=ALU.mult, op1=ALU.add)
                    if tt + 1 < self.TT:
                        ldH(tt + 1)
                    self.ln_tile(env, R, r_R, gB, bB, r_gb, tt, self.h_tok, self.r_h_tok, stage[b % 2], r_stage[b % 2],
                                 pt[tt % 2], r_pt[tt % 2])
                self.dma("sp", self.hT[:, :, b * 512:(b + 1) * 512].rearrange("c p t -> p c t"),
                         stage[b % 2][:], [r_stage[b % 2]], r_stage[b % 2], pw=[self.r_hT])
            self.S.flush()

    def phase_p4a(self, l):
        nc, NB = self.nc, self.NB
        BPS = self.SL // 512
        with contextlib.ExitStack() as ph:
            sb = lambda n, s_, d: ph.enter_context(nc.sbuf_tensor(self.un(n), s_, d))
            ps = lambda n, s_, d: ph.enter_context(nc.psum_tensor(self.un(n), s_, d))
            Wu = sb("Wu", [128, 8, 2 * D_FF], BF16)
            r_Wu = [Res("Wu%d" % i) for i in range(8)]
            self.load_w(Wu, self.w_up[l].rearrange("(kc p) n -> p kc n", p=128), r_Wu, 8)
            cw = sb("cw", [128, NCF, 4], F32)
            r_cw = Res("cw")
            for j in range(3):
                self.dma("sp", cw[:, :, j:j + 1], self.conv_w[l, j].rearrange("(c p o) -> p c o", p=128, o=1), [self.r_in], r_cw,
                         wr=[r_cw] if j == 0 else (), pw=() if j == 0 else [r_cw], allow_slow_non_contiguous=True)
            self.dma("sp", cw[:, :, 3:4], self.conv_b[l].rearrange("(c p o) -> p c o", p=128, o=1), [self.r_in], r_cw, pw=[r_cw],
                     allow_slow_non_contiguous=True)
            hTb = [sb("hTb%d" % i, [128, 8, 512], BF16) for i in range(2)]
            r_hTb = [Res("hTb0"), Res("hTb1")]
            aT = [sb("aT%d" % i, [128, NCF, 512], BF16) for i in range(2)]
            r_aT = [Res("aT0"), Res("aT1")]
            Gt = [sb("Gt%d" % i, [128, 514], F32) for i in range(3)]
            r_Gt = [Res("Gt%d" % i) for i in range(3)]
            cv = [sb("cv%d" % i, [128, 512], F32) for i in range(3)]
            r_cv = [Res("cv%d" % i) for i in range(3)]
            sl = [sb("sl%d" % i, [128, 512], F32) for i in range(3)]
            r_sl = [Res("sl%d" % i) for i in range(3)]
            halo = sb("halo", [128, NCF, 2], F32)
            r_halo = [Res("halo%d" % c) for c in range(NCF)]
            pg = [ps("pg%d" % i, [128, 512], F32) for i in range(3)]
            r_pg = [PRes("pg%d" % i) for i in range(3)]
            pv = [ps("pv%d" % i, [128, 512], F32) for i in range(3)]
            r_pv = [PRes("pv%d" % i) for i in range(3)]
            n = 0
            ldh = lambda b_: self.dma("sp", hTb[b_ % 2][:], self.hT[:, :, b_ * 512:(b_ + 1) * 512].rearrange("c p t -> p c t"),
                                      [self.r_hT], r_hTb[b_ % 2], wr=[r_hTb[b_ % 2]])
            ldh(0)
            for b in range(NB):
                hb_, r_hb_ = hTb[b % 2], r_hTb[b % 2]
                a_, r_a_ = aT[b % 2], r_aT[b % 2]
                if b + 1 < NB:
                    ldh(b + 1)
                for c in range(NCF):
                    n += 1
                    g_, r_g_ = pg[n % 3], r_pg[n % 3]
                    v_, r_v_ = pv[n % 3], r_pv[n % 3]
                    G_, r_G_ = Gt[n % 3], r_Gt[n % 3]
                    c_, r_c_ = cv[n % 3], r_cv[n % 3]
                    s_, r_s_ = sl[n % 3], r_sl[n % 3]
                    for kc in range(8):
                        self.mm(g_[:], Wu[:, kc, c * 128:(c + 1) * 128], hb_[:, kc, :], kc == 0, kc == 7, [r_Wu[kc], r_hb_], r_g_)
                    for kc in range(8):
                        self.mm(v_[:], Wu[:, kc, D_FF + c * 128:D_FF + (c + 1) * 128], hb_[:, kc, :], kc == 0, kc == 7,
                                [r_Wu[kc], r_hb_], r_v_)
                    self.I("act", "copy", rd=[r_g_], wr=[r_G_], out=G_[:, 2:514], in_=g_[:])
                    if b % BPS == 0:
                        self.I("pool", "memset", pw=[r_G_], ap=G_[:, 0:2], constant=0.0)
                    else:
                        self.I("pool", "tensor_copy", rd=[r_halo[c]], pw=[r_G_], out=G_[:, 0:2], in_=halo[:, c, :])
                    self.I("pool", "tensor_copy", rd=[r_G_], wr=[r_halo[c]], out=halo[:, c, :], in_=G_[:, 512:514])
                    self.I("dve", "tensor_scalar", rd=[r_G_, r_cw], wr=[r_c_], out=c_[:], in0=G_[:, 2:514],
                           scalar1=cw[:, c, 2:3], scalar2=cw[:, c, 3:4], op0=ALU.mult, op1=ALU.add)
                    self.I("dve", "scalar_tensor_tensor", rd=[r_G_, r_cw, r_c_], wr=[r_c_], out=c_[:], in0=G_[:, 1:513],
                           scalar=cw[:, c, 1:2], in1=c_[:], op0=ALU.mult, op1=ALU.add)
                    self.I("dve", "scalar_tensor_tensor", rd=[r_G_, r_cw, r_c_], wr=[r_c_], out=c_[:], in0=G_[:, 0:512],
                           scalar=cw[:, c, 0:1], in1=c_[:], op0=ALU.mult, op1=ALU.add)
                    self.I("act", "activation", rd=[r_c_], wr=[r_s_], out=s_[:], in_=c_[:], func=AF.Silu)
                    self.I("dve", "tensor_tensor", rd=[r_s_, r_v_], wr=[r_a_] if c == 0 else (), pw=() if c == 0 else [r_a_],
                           out=a_[:, c, :], in0=s_[:], in1=v_[:], op=ALU.mult)
                self.dma("sp", self.actT[:, :, b * 512:(b + 1) * 512].rearrange("c p t -> p c t"), a_[:], [r_a_], r_a_,
                         pw=[self.r_actT])
            self.S.flush()

    def phase_p4b(self, l, last):
        nc, NB = self.nc, self.NB
        with contextlib.ExitStack() as ph:
            sb = lambda n, s_, d: ph.enter_context(nc.sbuf_tensor(self.un(n), s_, d))
            ps = lambda n, s_, d: ph.enter_context(nc.psum_tensor(self.un(n), s_, d))
            Wd = sb("Wd", [128, NCF, D], BF16)
            r_Wd = [Res("Wd%d" % i) for i in range(11)]
            self.load_w(Wd, self.w_down[l].rearrange("(kc p) n -> p kc n", p=128), r_Wd, 11)
            env = self.ln_env(sb)
            gB, bB, r_gb = self.load_gb(sb, self.ln2_g[l], self.ln2_b[l], "2")
            aT = [sb("aT%d" % i, [128, NCF, 512], BF16) for i in range(2)]
            r_aT = [Res("aT0"), Res("aT1")]
            Rt = [sb("R%d" % i, [128, 1024], F32) for i in range(2)]
            r_Rt = [Res("R0"), Res("R1")]
            Ht = [sb("H%d" % i, [128, 1024], F32) for i in range(2)]
            r_Ht = [Res("H0"), Res("H1")]
            stage = [sb("hst%d" % i, [128, 8, 512], BF16) for i in range(2)]
            r_stage = [Res("hst0"), Res("hst1")]
            po = [ps("po%d" % i, [128, 512], F32) for i in range(3)]
            r_po = [PRes("po%d" % i) for i in range(3)]
            pt = [ps("pt%d" % i, [128, 1024], BF16) for i in range(2)]
            r_pt = [PRes("pt0"), PRes("pt1")]
            n = 0
            dst, r_dst = (self.out, self.r_out) if last else (self.h_tok, self.r_h_tok)
            lda = lambda b_: self.dma("sp", aT[b_ % 2][:], self.actT[:, :, b_ * 512:(b_ + 1) * 512].rearrange("c p t -> p c t"),
                                      [self.r_actT], r_aT[b_ % 2], wr=[r_aT[b_ % 2]])
            ldH = lambda t_: self.dma("sp", Ht[t_ % 2][:], self.h_tok[t_ * 128:(t_ + 1) * 128, :], [self.r_h_tok],
                                      r_Ht[t_ % 2], wr=[r_Ht[t_ % 2]])
            lda(0)
            ldH(0)
            for b in range(NB):
                a_, r_a_ = aT[b % 2], r_aT[b % 2]
                if b + 1 < NB:
                    lda(b + 1)
                for i in range(4):
                    tt = b * 4 + i
                    R, r_R = Rt[tt % 2], r_Rt[tt % 2]
                    H, r_H = Ht[tt % 2], r_Ht[tt % 2]
                    for half in range(2):
                        n += 1
                        o_, r_o_ = po[n % 3], r_po[n % 3]
                        for c in range(NCF):
                            self.mm(o_[:], a_[:, c, i * 128:(i + 1) * 128], Wd[:, c, half * 512:(half + 1) * 512],
                                    c == 0, c == NCF - 1, [r_a_, r_Wd[c // 2]], r_o_)
                        self.I("dve", "scalar_tensor_tensor", rd=[r_H, r_o_], wr=[r_R] if half == 0 else (),
                               pw=() if half == 0 else [r_R], out=R[:, half * 512:(half + 1) * 512],
                               in0=H[:, half * 512:(half + 1) * 512], scalar=ALPHA, in1=o_[:], op0=ALU.mult, op1=ALU.add)
                    if tt + 1 < self.TT:
                        ldH(tt + 1)
                    if last:
                        self.ln_tile(env, R, r_R, gB, bB, r_gb, tt, dst, r_dst, None, None, None, None)
                    else:
                        self.ln_tile(env, R, r_R, gB, bB, r_gb, tt, dst, r_dst, stage[b % 2], r_stage[b % 2],
                                     pt[tt % 2], r_pt[tt % 2])
                if not last:
                    self.dma("sp", self.hT[:, :, b * 512:(b + 1) * 512].rearrange("c p t -> p c t"),
                             stage[b % 2][:], [r_stage[b % 2]], r_stage[b % 2], pw=[self.r_hT])
            self.S.flush()


def host_consts():
    idx = np.arange(128)
    c = {}
    c["c_ident"] = np.eye(128, dtype=np.float32)
    c["c_trim"] = np.where(idx[:, None] > idx[None, :], NEG, 0.0).astype(np.float32)
    c["c_caus"] = np.where(idx[None, :] > idx[:, None], -1e30, 0.0).astype(np.float32)
    c["c_utri"] = (idx[:, None] <= idx[None, :]).astype(np.float32)
    e = np.zeros((128, 128), np.float32)
    e[127, :] = 1.0
    c["c_e127"] = e
    c["c_pw"] = (0.5 ** np.arange(1, NBIS + 1)).astype(np.float32)
    theta = np.float32(500000.0)
    f16 = theta ** (-np.arange(16, dtype=np.float32) / np.float32(16))
    f8 = theta ** (-np.arange(8, dtype=np.float32) / np.float32(8))
    c["c_invf"] = np.concatenate([f16, f8]).astype(np.float32)
    return c


WEIGHT_KEYS = ["ln_in_g", "ln_in_b", "w_in", "b_f", "kv_norm_g", "w_ukv", "lam_qk", "diff_norm_g", "w_gate",
               "w_br_a", "w_br_b", "w_br_c", "w_out", "ln1_g", "ln1_b", "w_up", "conv_w", "conv_b", "w_down",
               "ln2_g", "ln2_b"]


def make_in_maps(inputs, n_cores, nseq, nlayer):
    consts = host_consts()
    shared = {}
    for k in WEIGHT_KEYS:
        a = np.ascontiguousarray(np.asarray(inputs[k], dtype=np.float32))
        if a.ndim >= 2 or k in ("b_f",):
            pass
        if k not in ("ln_in_g", "ln_in_b"):
            a = a[:nlayer]
        if k == "lam_qk":
            a = a.reshape(a.shape[0], 256)
        shared[k] = np.ascontiguousarray(a)
    shared.update(consts)
    x = np.asarray(inputs["x"], dtype=np.float32)
    pos = np.asarray(inputs["positions"], dtype=np.int32)
    maps = []
    for c in range(n_cores):
        m = dict(shared)
        m["x"] = np.ascontiguousarray(x[c * nseq:(c + 1) * nseq].reshape(-1, D))
        m["positions"] = np.ascontiguousarray(pos[c * nseq:(c + 1) * nseq].reshape(-1))
        maps.append(m)
    return maps


def kernel(**inputs):
    x = np.asarray(inputs["x"])
    B, SL, _ = x.shape
    n_cores = 8
    nseq = B // n_cores
    bld = Builder(SL, nseq, DEPTH)
    nc = bld.build()
    maps = make_in_maps(inputs, n_cores, nseq, DEPTH)
    res = run_bass_kernel_spmd(nc, maps, core_ids=list(range(n_cores)))
    outs = [np.asarray(r["out"], dtype=np.float32).reshape(nseq, SL, D) for r in res.results]
    return np.concatenate(outs, axis=0)
```

```python
import contextlib
import math
import numpy as np
import concourse.bass as bass
import concourse.mybir as mybir
from concourse.bass_utils import run_bass_kernel_spmd

F32 = mybir.dt.float32
BF16 = mybir.dt.bfloat16
I32 = mybir.dt.int32
AF = mybir.ActivationFunctionType
ALU = mybir.AluOpType
AX = mybir.AxisListType

D = 1024
DEPTH = 4
D_FF = 2816
NCF = D_FF // 128
ALPHA = (2 * DEPTH) ** 0.25
LN_EPS = 1e-5
RMS_EPS = 1e-6
TOPK = 256
NEG = -30000.0
NBIS = 16
TWO_PI = 6.283185307179586
C1 = 6.28125
C2 = TWO_PI - C1

SRC = dict(aq=0, ak=512, av=1024, af=1536, bq=1540, bc=2052, biq=2308, bik=2820, biw=2884,
           cq=2892, ck=3404, cv=3916)
WID = dict(aq=512, ak=512, av=512, af=4, bq=512, bc=256, biq=512, bik=64, biw=8,
           cq=512, ck=512, cv=512)
ORDER = ["aq", "ak", "av", "bq", "biq", "cq", "ck", "cv", "bc", "bik", "af", "biw"]
DST = {}
_o = 0
for _k in ORDER:
    DST[_k] = _o
    _o += WID[_k]
D_IN = _o
CH = dict(aq=0, ak=4, bq=8, biq=12, cq=16, ck=20, kb=24, ki=25)
NCH = 26


class Res:
    __slots__ = ("name", "ws", "rs", "prs", "sem", "dummy", "excl")

    def __init__(self, name="", dummy=False):
        self.name = name
        self.ws = {}
        self.rs = {}
        self.prs = {}
        self.sem = None
        self.dummy = dummy
        self.excl = False


def PRes(name):
    r = Res(name)
    r.excl = True
    return r


def _merge(a, b):
    d = dict(a)
    for k, v in b.items():
        if d.get(k, -1) < v:
            d[k] = v
    return d


class Sch:
    COMPUTE = ("pe", "act", "dve", "pool")

    def __init__(self, nc, semstack):
        self.nc = nc
        self.ops = []
        self.base = 0
        self.eng = {"pe": nc.tensor, "act": nc.scalar, "dve": nc.vector,
                    "pool": nc.gpsimd, "sp": nc.sync}
        self.esem = {e: semstack.enter_context(nc.semaphore("e_" + e)) for e in self.COMPUTE}
        self.ecnt = {e: 0 for e in self.COMPUTE}
        self.pool = [[semstack.enter_context(nc.semaphore("d%d" % i)), 0] for i in range(72)]
        self.free = {"pool": list(range(0, 24)), "sp": list(range(24, 72))}
        self.semkind = {}
        self.used = []
        self.waited = {e: {} for e in self.eng}
        self.n_ins = 0
        self.n_wait = 0

    def _key(self, idx):
        o = self.ops[idx - self.base]
        return ("d", id(o[4])) if o[4] is not None else o[0]

    def op(self, eng, fn, reads=(), writes=(), pwrites=(), slot=None):
        idx = self.base + len(self.ops)
        raw = set()
        oth = set()
        reads = [r for r in reads if not r.dummy]
        writes = [w for w in writes if not w.dummy]
        pwrites = [w for w in pwrites if not w.dummy]
        for r in reads:
            raw.update(r.ws.values())
            if r.excl:
                oth.update(v for k, v in r.rs.items() if k != eng)
        for w in writes:
            oth.update(w.ws.values())
            oth.update(w.rs.values())
        for w in pwrites:
            if w.rs:
                oth.update(w.rs.values())
            else:
                oth.update(w.prs.values())
        key = ("d", id(slot)) if slot is not None else eng
        for r in reads:
            r.rs[key] = idx
        for w in writes:
            w.prs = _merge(w.ws, w.rs)
            w.ws = {key: idx}
            w.rs = {}
        for w in pwrites:
            if w.rs:
                w.prs = _merge(w.ws, w.rs)
                w.ws = {key: idx}
                w.rs = {}
            else:
                w.ws[key] = idx
        self.ops.append([eng, fn, raw, oth, slot, False, 0])
        return idx

    def flush(self):
        ops = self.ops
        base = self.base
        last = {}
        for i, o in enumerate(ops):
            eng = o[0]
            for d in o[2]:
                if d >= base:
                    od = ops[d - base]
                    if od[4] is None:
                        od[5] = True
            for d in o[3]:
                if d >= base:
                    od = ops[d - base]
                    if od[4] is None and (od[0] != eng or eng != "pe"):
                        od[5] = True
            if o[4] is None and eng in self.ecnt:
                last[eng] = o
        for o in last.values():
            o[5] = True
        for o in ops:
            if o[4] is not None:
                s = o[4]
                if s.sem is None:
                    kind = "pool" if o[0] == "pool" else "sp"
                    s.sem = self.free[kind].pop()
                    self.semkind[s.sem] = kind
                    self.used.append(s)
                else:
                    assert self.semkind[s.sem] == ("pool" if o[0] == "pool" else "sp"), s.name
                p = self.pool[s.sem]
                p[1] += 16
                o[6] = p[1]
            elif o[5]:
                self.ecnt[o[0]] += 1
                o[6] = self.ecnt[o[0]]
        waited = self.waited
        for o in ops:
            eng = o[0]
            need = {}
            for kind, deps in ((0, o[2]), (1, o[3])):
                for d in deps:
                    if d < base:
                        continue
                    od = ops[d - base]
                    if od[4] is not None:
                        sem = self.pool[od[4].sem][0]
                        k = ("d", od[4].sem)
                    else:
                        if od[0] == eng and kind == 1 and eng == "pe":
                            continue
                        sem = self.esem[od[0]]
                        k = od[0]
                    v = od[6]
                    if waited[eng].get(k, 0) < v and need.get(k, (None, 0))[1] < v:
                        need[k] = (sem, v)
            e = self.eng[eng]
            for k, (sem, v) in need.items():
                e.wait_ge(sem, v)
                waited[eng][k] = v
                self.n_wait += 1
            ins = o[1]()
            self.n_ins += 1
            if o[4] is not None:
                ins.then_inc(self.pool[o[4].sem][0], 16)
            elif o[5]:
                ins.then_inc(self.esem[eng], 1)
        for eng, e in self.eng.items():
            for f in self.COMPUTE:
                if f != eng and waited[eng].get(f, 0) < self.ecnt[f]:
                    e.wait_ge(self.esem[f], self.ecnt[f])
                    waited[eng][f] = self.ecnt[f]
            for s in self.used:
                k = ("d", s.sem)
                v = self.pool[s.sem][1]
                if waited[eng].get(k, 0) < v:
                    e.wait_ge(self.pool[s.sem][0], v)
                    waited[eng][k] = v
        for s in self.used:
            self.free[self.semkind[s.sem]].append(s.sem)
            s.sem = None
        self.used = []
        self.base += len(ops)
        self.ops = []


class Builder:
    def __init__(self, S_len, nseq, nlayer, dbg=()):
        self.SL = S_len
        self.NSEQ = nseq
        self.L = nlayer
        self.NT = S_len * nseq
        self.TT = self.NT // 128
        self.NB = self.NT // 512
        self.TPS = S_len // 128
        self.GPS = S_len // 512
        self.dbg = set(dbg)
        self.nc = bass.Bass("TRN2", target_bir_lowering=False)
        self.gs = contextlib.ExitStack()
        self.S = None

    def un(self, n):
        self._uid = getattr(self, "_uid", 0) + 1
        return "%s_%d" % (n, self._uid)

    def I(self, eng, meth, rd=(), wr=(), pw=(), **kw):
        f = getattr(self.S.eng[eng], meth)
        self.S.op(eng, lambda: f(**kw), reads=rd, writes=wr, pwrites=pw)

    def dma(self, eng, out, in_, rd, slot, wr=(), pw=(), **kw):
        f = self.S.eng[eng].dma_start
        self.S.op(eng, lambda: f(out=out, in_=in_, **kw), reads=rd, writes=wr, pwrites=pw, slot=slot)

    def mm(self, out, lhsT, rhs, start, stop, rd, wr):
        f = self.nc.tensor.matmul
        self.S.op("pe", lambda: f(out, lhsT=lhsT, rhs=rhs, start=start, stop=stop),
                  reads=rd, writes=[wr] if start else (), pwrites=() if start else [wr])

    def tr(self, out, in_, rd, wr, first):
        f = self.nc.tensor.transpose
        idt = self.ident[:]
        self.S.op("pe", lambda: f(out, in_, idt), reads=list(rd) + [self.r_const],
                  writes=[wr] if first else (), pwrites=() if first else [wr])

    def din(self, name, shape, dt):
        return self.nc.dram_tensor(name, list(shape), dt, kind="ExternalInput").ap()

    def dscr(self, name, shape, dt):
        kind = "ExternalOutput" if name in self.dbg else "Internal"
        return self.nc.dram_tensor(name, list(shape), dt, kind=kind).ap()

    def build(self):
        nc = self.nc
        L, NT, TT = self.L, self.NT, self.TT
        with self.gs as gs:
            self.S = Sch(nc, gs)
            sbg = lambda n, s, d: gs.enter_context(nc.sbuf_tensor(self.un(n), s, d))
            self.x = self.din("x", [NT, D], F32)
            self.pos = self.din("positions", [NT], I32)
            self.ln_in_g = self.din("ln_in_g", [D], F32)
            self.ln_in_b = self.din("ln_in_b", [D], F32)
            self.w_in = self.din("w_in", [L, D, 4428], F32)
            self.b_f = self.din("b_f", [L, 4], F32)
            self.kv_norm_g = self.din("kv_norm_g", [L, 256], F32)
            self.w_ukv = self.din("w_ukv", [L, 256, 256], F32)
            self.lam_qk = self.din("lam_qk", [L, 256], F32)
            self.diff_norm_g = self.din("diff_norm_g", [L, 128], F32)
            self.w_gate = self.din("w_gate", [L, D, 3 * D], F32)
            self.w_br = [self.din("w_br_" + c, [L, 512, D], F32) for c in "abc"]
            self.w_out = self.din("w_out", [L, D, D], F32)
            self.ln1_g = self.din("ln1_g", [L, D], F32)
            self.ln1_b = self.din("ln1_b", [L, D], F32)
            self.w_up = self.din("w_up", [L, D, 2 * D_FF], F32)
            self.conv_w = self.din("conv_w", [L, 3, D_FF], F32)
            self.conv_b = self.din("conv_b", [L, D_FF], F32)
            self.w_down = self.din("w_down", [L, D_FF, D], F32)
            self.ln2_g = self.din("ln2_g", [L, D], F32)
            self.ln2_b = self.din("ln2_b", [L, D], F32)
            self.c_ident = self.din("c_ident", [128, 128], F32)
            self.c_trim = self.din("c_trim", [128, 128], F32)
            self.c_caus = self.din("c_caus", [128, 128], F32)
            self.c_utri = self.din("c_utri", [128, 128], F32)
            self.c_e127 = self.din("c_e127", [128, 128], F32)
            self.c_pw = self.din("c_pw", [NBIS], F32)
            self.c_invf = self.din("c_invf", [24], F32)
            self.out = self.nc.dram_tensor("out", [NT, D], F32, kind="ExternalOutput").ap()
            self.h_tok = self.dscr("h_tok", [NT, D], F32)
            self.hT = self.dscr("hT", [8, 128, NT], BF16)
            self.qkT = self.dscr("qkT", [NCH, 128, NT], BF16)
            self.vtok = self.dscr("vtok", [NT, 1152], BF16)
            self.lfw = self.dscr("lfw", [128, TT, 12], F32)
            self.mb = self.dscr("mb", [NT, self.SL], BF16)
            self.oT = self.dscr("oT", [12, 128, NT], BF16)
            self.actT = self.dscr("actT", [NCF, 128, NT], BF16)
            dm_ = lambda n: Res(n, dummy=True)
            self.r_h_tok, self.r_hT, self.r_qkT, self.r_vtok = dm_("h_tok"), dm_("hT"), dm_("qkT"), dm_("vtok")
            self.r_lfw, self.r_mb, self.r_oT, self.r_actT, self.r_out = dm_("lfw"), dm_("mb"), dm_("oT"), dm_("actT"), dm_("out")
            self.r_in = dm_("inputs")
            self.ident = sbg("ident", [128, 128], BF16)
            self.trim = sbg("trim", [128, 128], BF16)
            self.caus = sbg("caus", [128, 128], F32)
            self.utri = sbg("utri", [128, 128], F32)
            self.e127 = sbg("e127", [128, 128], F32)
            self.pw = sbg("pw", [128, NBIS], F32)
            self.cosT = sbg("cosT", [128, TT, 24], F32)
            self.sinT = sbg("sinT", [128, TT, 24], F32)
            self.r_const = Res("const")
            self.r_rope = Res("rope")
            rc = [Res("c%d" % i) for i in range(6)]
            self.dma("pool", self.ident[:], self.c_ident[:, :], [], rc[0], wr=[rc[0]])
            self.dma("pool", self.trim[:], self.c_trim[:, :], [], rc[1], wr=[rc[1]])
            self.dma("sp", self.caus[:], self.c_caus[:, :], [], rc[2], wr=[rc[2]])
            self.dma("sp", self.utri[:], self.c_utri[:, :], [], rc[3], wr=[rc[3]])
            self.dma("sp", self.e127[:], self.c_e127[:, :], [], rc[4], wr=[rc[4]])
            self.dma("sp", self.pw[:], self.c_pw.partition_broadcast(128), [], rc[5], wr=[rc[5]])
            plist = [(self.phase_rope, ()), (self.phase_ln_in, ())]
            for l in range(L):
                plist.append((self.phase_p1, (l,)))
                plist += [(self.phase_attn_a, (l, s)) for s in range(self.NSEQ)]
                plist += [(self.phase_idx, (l, s)) for s in range(self.NSEQ)]
                plist += [(self.phase_attn_b, (l, s)) for s in range(self.NSEQ)]
                plist += [(self.phase_attn_c, (l, s)) for s in range(self.NSEQ)]
                plist += [(self.phase_p3, (l,)), (self.phase_p4a, (l,)), (self.phase_p4b, (l, l == L - 1))]
            for fn, args in plist[:getattr(self, "maxph", 10 ** 9)]:
                fn(*args)
            self.I("sp", "nop")
            self.S.flush()
        return nc

    def phase_rope(self):
        nc, TT = self.nc, self.TT
        with contextlib.ExitStack() as ph:
            sb = lambda n, s, d: ph.enter_context(nc.sbuf_tensor(self.un(n), s, d))
            posi = sb("posi", [128, TT], I32)
            posf = sb("posf", [128, TT], F32)
            invf = sb("invf", [128, 24], F32)
            ang = sb("ang", [128, TT, 24], F32)
            ki = sb("rki", [128, TT, 24], I32)
            kf = sb("rkf", [128, TT, 24], F32)
            r1 = sb("rr1", [128, TT, 24], F32)
            r_p, r_i, r_a, r_k, r_r = Res("posi"), Res("invf"), Res("ang"), Res("rk"), Res("rr")
            self.dma("sp", posi[:], self.pos.rearrange("(j p) -> p j", p=128), [], r_p, wr=[r_p],
                     allow_slow_non_contiguous=True)
            self.dma("sp", invf[:], self.c_invf.partition_broadcast(128), [], r_i, wr=[r_i])
            r_pf = Res("posf")
            self.I("dve", "tensor_copy", rd=[r_p], wr=[r_pf], out=posf[:], in_=posi[:])
            self.I("dve", "tensor_tensor", rd=[r_pf, r_i], wr=[r_a], out=ang[:],
                   in0=posf[:].unsqueeze(2).to_broadcast([128, TT, 24]),
                   in1=invf[:].unsqueeze(1).to_broadcast([128, TT, 24]), op=ALU.mult)
            for which, dst in ((0, self.sinT), (1, self.cosT)):
                src = ang
                if which == 1:
                    self.I("dve", "tensor_scalar", rd=[r_a], wr=[r_r], out=r1[:], in0=ang[:],
                           scalar1=math.pi / 2, scalar2=None, op0=ALU.add)
                    src = r1
                rs_ = r_a if which == 0 else r_r
                r_kf = Res("kf")
                self.I("dve", "tensor_scalar", rd=[rs_], wr=[r_kf], out=kf[:], in0=src[:],
                       scalar1=1.0 / TWO_PI, scalar2=None, op0=ALU.mult)
                self.I("dve", "tensor_copy", rd=[r_kf], wr=[r_k], out=ki[:], in_=kf[:])
                self.I("dve", "tensor_copy", rd=[r_k], wr=[r_kf], out=kf[:], in_=ki[:])
                self.I("dve", "scalar_tensor_tensor", rd=[r_kf, rs_], wr=[r_r], out=r1[:], in0=kf[:],
                       scalar=-C1, in1=src[:], op0=ALU.mult, op1=ALU.add)
                self.I("dve", "scalar_tensor_tensor", rd=[r_kf, r_r], wr=[r_r], out=r1[:], in0=kf[:],
                       scalar=-C2, in1=r1[:], op0=ALU.mult, op1=ALU.add)
                self.I("dve", "tensor_scalar", rd=[r_r], wr=[r_r], out=r1[:], in0=r1[:],
                       scalar1=math.pi, scalar2=-math.pi, op0=ALU.min, op1=ALU.max)
                self.I("act", "activation", rd=[r_r], wr=[self.r_rope], out=dst[:], in_=r1[:], func=AF.Sin)
            self.S.flush()

    def ln_tile(self, env, R, r_R, gB, bB, r_gb, tt, dst, r_dst, stage, r_stage, pt, r_pt):
        st, mv, hb = env["st"], env["mv"], env["hb"]
        k = env["k"] = env.get("k", 0) + 1
        b2 = k % 2
        r_st, r_mv, r_hb = env["r_st"][b2], env["r_mv"][b2], env["r_hb"][b2]
        st_, mv_, hb_ = st[b2], mv[b2], hb[b2]
        self.I("dve", "bn_stats", rd=[r_R], wr=[r_st], out=st_[:, 0:6], in_=R[:, 0:512])
        self.I("dve", "bn_stats", rd=[r_R], pw=[r_st], out=st_[:, 6:12], in_=R[:, 512:1024])
        self.I("dve", "bn_aggr", rd=[r_st], wr=[r_mv], out=mv_[:, 0:2], in_=st_[:, 0:12])
        self.I("dve", "tensor_scalar", rd=[r_mv], wr=[r_mv], out=mv_[:, 2:3], in0=mv_[:, 1:2],
               scalar1=LN_EPS, scalar2=None, op0=ALU.add)
        self.I("act", "activation", rd=[r_mv], wr=[r_mv], out=mv_[:, 3:4], in_=mv_[:, 2:3], func=AF.Sqrt)
        self.I("dve", "reciprocal", rd=[r_mv], wr=[r_mv], out=mv_[:, 4:5], in_=mv_[:, 3:4])
        self.I("dve", "scalar_tensor_tensor", rd=[r_R, r_mv] + list(r_gb), wr=[r_R], out=R[:], in0=R[:],
               scalar=mv_[:, 0:1], in1=gB[:], op0=ALU.subtract, op1=ALU.mult)
        self.I("dve", "scalar_tensor_tensor", rd=[r_R, r_mv] + list(r_gb), wr=[r_R], out=R[:], in0=R[:],
               scalar=mv_[:, 4:5], in1=bB[:], op0=ALU.mult, op1=ALU.add)
        self.dma("sp", dst[tt * 128:(tt + 1) * 128, :], R[:], [r_R], r_R, pw=[r_dst])
        if stage is None:
            return
        self.I("act", "copy", rd=[r_R], wr=[r_hb], out=hb_[:], in_=R[:])
        for c in range(8):
            self.tr(pt[:, c * 128:(c + 1) * 128], hb_[:, c * 128:(c + 1) * 128], [r_hb], r_pt, c == 0)
        i = tt % 4
        self.I("act", "copy", rd=[r_pt], pw=[r_stage], out=stage[:, :, i * 128:(i + 1) * 128],
               in_=pt[:].rearrange("p (c t) -> p c t", c=8))

    def ln_env(self, sb):
        env = dict(st=[sb("ln_st%d" % i, [128, 12], F32) for i in range(2)],
                   mv=[sb("ln_mv%d" % i, [128, 8], F32) for i in range(2)],
                   hb=[sb("ln_hb%d" % i, [128, 1024], BF16) for i in range(2)],
                   r_st=[Res("st0"), Res("st1")], r_mv=[Res("mv0"), Res("mv1")],
                   r_hb=[Res("hb0"), Res("hb1")])
        return env

    def load_gb(self, sb, g_ap, b_ap, tag):
        gB = sb("gB" + tag, [128, 1024], F32)
        bB = sb("bB" + tag, [128, 1024], F32)
        r_g, r_b = Res("gB"), Res("bB")
        self.dma("sp", gB[:], g_ap.partition_broadcast(128), [self.r_in], r_g, wr=[r_g])
        self.dma("sp", bB[:], b_ap.partition_broadcast(128), [self.r_in], r_b, wr=[r_b])
        return gB, bB, [r_g, r_b]

    def phase_ln_in(self):
        nc, TT = self.nc, self.TT
        with contextlib.ExitStack() as ph:
            sb = lambda n, s, d: ph.enter_context(nc.sbuf_tensor(self.un(n), s, d))
            env = self.ln_env(sb)
            gB, bB, r_gb = self.load_gb(sb, self.ln_in_g, self.ln_in_b, "i")
            Rt = [sb("R%d" % i, [128, 1024], F32) for i in range(3)]
            r_Rt = [Res("R%d" % i) for i in range(3)]
            stage = [sb("hst%d" % i, [128, 8, 512], BF16) for i in range(2)]
            r_stage = [Res("hst0"), Res("hst1")]
            pt = [ph.enter_context(nc.psum_tensor("pt%d" % i, [128, 1024], BF16)) for i in range(2)]
            r_pt = [PRes("pt0"), PRes("pt1")]
            ldx = lambda t_: self.dma("sp", Rt[t_ % 3][:], self.x[t_ * 128:(t_ + 1) * 128, :], [self.r_in],
                                      r_Rt[t_ % 3], wr=[r_Rt[t_ % 3]])
            ldx(0)
            for tt in range(TT):
                R, r_R = Rt[tt % 3], r_Rt[tt % 3]
                b = tt // 4
                if tt + 1 < TT:
                    ldx(tt + 1)
                self.ln_tile(env, R, r_R, gB, bB, r_gb[0:2], tt, self.h_tok, self.r_h_tok,
                             stage[b % 2], r_stage[b % 2], pt[tt % 2], r_pt[tt % 2])
                if tt % 4 == 3:
                    self.dma("sp", self.hT[:, :, b * 512:(b + 1) * 512].rearrange("c p t -> p c t"),
                             stage[b % 2][:], [r_stage[b % 2]], r_stage[b % 2], pw=[self.r_hT])
            self.S.flush()

    def rope(self, X, Y, H, Dh, half, off, tt, tmp, r_tmp, r_X, r_Y):
        Xv = X.rearrange("p (h d) -> p h d", h=H)
        Yv = Y.rearrange("p (h d) -> p h d", h=H)
        cos = self.cosT[:, tt, off:off + half].unsqueeze(1).to_broadcast([128, H, half])
        sin = self.sinT[:, tt, off:off + half].unsqueeze(1).to_broadcast([128, H, half])
        x1, x2 = Xv[:, :, 0:half], Xv[:, :, half:2 * half]
        t = [tmp[i][:, 0:H * half].rearrange("p (h d) -> p h d", h=H) for i in range(4)]
        rr = [self.r_rope]
        TTm = lambda o, a, b, op, rd, wr=(), pw=(): self.I("dve", "tensor_tensor", rd=rd, wr=wr, pw=pw, out=o, in0=a, in1=b, op=op)
        RP = 9
        RO = 4
        TTm(t[0], x1, cos, ALU.mult, [r_X, r_Y] + rr, wr=[r_tmp[0]])
        if RO >= 2:
            TTm(t[1], x2, sin, ALU.mult, [r_X] + rr, wr=[r_tmp[1]])
        if RO >= 3:
            TTm(t[2], x2, cos, ALU.mult, [r_X] + rr, wr=[r_tmp[2]])
        if RO >= 4:
            TTm(t[3], x1, sin, ALU.mult, [r_X] + rr, wr=[r_tmp[3]])
        if RP == 1:
            return
        TTm(Yv[:, :, 0:half], t[0], t[1], ALU.subtract, [r_tmp[0], r_tmp[1], r_Y], pw=[r_Y])
        TTm(Yv[:, :, half:2 * half], t[2], t[3], ALU.add, [r_tmp[2], r_tmp[3], r_Y], pw=[r_Y])

    def phase_p1(self, l):
        nc, TT, NB = self.nc, self.TT, self.NB
        with contextlib.ExitStack() as ph:
            sb = lambda n, s, d: ph.enter_context(nc.sbuf_tensor(self.un(n), s, d))
            ps = lambda n, s, d: ph.enter_context(nc.psum_tensor(self.un(n), s, d))
            W = sb("W_in", [128, 8, D_IN], BF16)
            r_W = [Res("W%d" % k) for k in range(8)]
            Wk = sb("W_ukv", [128, 2, 256], BF16)
            r_Wk = Res("Wukv")
            wsrc = self.w_in[l].rearrange("(kc p) n -> p kc n", p=128)
            for kc in range(8):
                for j, name in enumerate(ORDER):
                    self.dma("pool", W[:, kc, DST[name]:DST[name] + WID[name]],
                             wsrc[:, kc, SRC[name]:SRC[name] + WID[name]], [self.r_in], r_W[kc],
                             wr=[r_W[kc]] if j == 0 else (), pw=() if j == 0 else [r_W[kc]])
            self.dma("pool", Wk[:], self.w_ukv[l].rearrange("(kc p) n -> p kc n", p=128), [self.r_in], r_Wk, wr=[r_Wk])
            kvg = sb("kvg", [128, 256], F32)
            bfb = sb("bfb", [128, 4], F32)
            r_kvg = Res("kvg")
            self.dma("sp", kvg[:], self.kv_norm_g[l].partition_broadcast(128), [self.r_in], r_kvg, wr=[r_kvg])
            r_bfb = Res("bfb")
            self.dma("sp", bfb[:], self.b_f[l].partition_broadcast(128), [self.r_in], r_bfb, wr=[r_bfb])
            hTb = [sb("hTb%d" % i, [128, 8, 512], BF16) for i in range(2)]
            r_hTb = [Res("hTb0"), Res("hTb1")]
            Y = [sb("Y%d" % i, [128, D_IN], BF16) for i in range(2)]
            r_Y = [Res("Y0"), Res("Y1")]
            VB = [sb("VB%d" % i, [128, 128], BF16) for i in range(2)]
            r_VB = [Res("VB0"), Res("VB1")]
            ST = [sb("ST%d" % i, [128, NCH, 512], BF16) for i in range(2)]
            r_ST = [Res("ST0"), Res("ST1")]
            LFW = sb("LFW", [128, TT, 12], F32)
            r_LFW = Res("LFW")
            tmp = [sb("rt%d" % i, [128, 64], F32) for i in range(8)]
            r_tmp = [Res("rt%d" % i) for i in range(8)]
            sm = [sb("sm%d" % i, [128, 8], F32) for i in range(2)]
            r_sm = [Res("sm0"), Res("sm1")]
            junk = sb("junk", [128, 256], F32)
            r_junk = Res("junk")
            bcn = [sb("bcn%d" % i, [128, 256], BF16) for i in range(2)]
            r_bcn = [Res("bcn0"), Res("bcn1")]
            bcnT = [sb("bcnT%d" % i, [128, 256], BF16) for i in range(2)]
            r_bcnT = [Res("bcnT0"), Res("bcnT1")]
            kid = [sb("kid%d" % i, [128, 256], BF16) for i in range(2)]
            r_kid = [Res("kid0"), Res("kid1")]
            pc = [ps("pc%d" % i, [128, 512], F32) for i in range(4)]
            r_pc = [PRes("pc%d" % i) for i in range(4)]
            ptr = [ps("ptr%d" % i, [128, 1024], BF16) for i in range(2)]
            r_ptr = [PRes("ptr0"), PRes("ptr1")]
            pkv = ps("pkv", [128, 256], F32)
            r_pkv = PRes("pkv")
            npc = 0
            ntr = 0
            LV = 9
            ldh = lambda b_: self.dma("sp", hTb[b_ % 2][:], self.hT[:, :, b_ * 512:(b_ + 1) * 512].rearrange("c p t -> p c t"),
                                      [self.r_hT], r_hTb[b_ % 2], wr=[r_hTb[b_ % 2]])
            ldh(0)
            for b in range(NB if LV > 0 else 0):
                hb_, r_hb_ = hTb[b % 2], r_hTb[b % 2]
                if b + 1 < NB:
                    ldh(b + 1)
                st_, r_st_ = ST[b % 2], r_ST[b % 2]
                for i in range(4):
                    tt = b * 4 + i
                    y, r_y = Y[tt % 2], r_Y[tt % 2]
                    t4 = tmp[0:4] if tt % 2 == 0 else tmp[4:8]
                    r_t4 = r_tmp[0:4] if tt % 2 == 0 else r_tmp[4:8]
                    sm_, r_sm_ = sm[tt % 2], r_sm[tt % 2]
                    for c in range(9):
                        n = 512 if c < 8 else D_IN - 4096
                        p, r_p = pc[npc % 4], r_pc[npc % 4]
                        npc += 1
                        for kc in range(8):
                            self.mm(p[:, 0:n], hb_[:, kc, i * 128:(i + 1) * 128], W[:, kc, c * 512:c * 512 + n],
                                    kc == 0, kc == 7, [r_hb_, r_W[kc]], r_p)
                        if c < 8:
                            self.I("act", "copy", rd=[r_p], wr=[r_y] if c == 0 else (), pw=() if c == 0 else [r_y],
                                   out=y[:, c * 512:(c + 1) * 512], in_=p[:, 0:512])
                            if LV < 2:
                                pass
                            elif c == 3:
                                self.rope(p[:, 0:512], y[:, 1536:2048], 4, 128, 16, 0, tt, t4, r_t4, r_p, r_y)
                            elif c in (4, 5, 6):
                                self.rope(p[:, 0:512], y[:, c * 512:(c + 1) * 512], 8, 64, 8, 16, tt, t4, r_t4, r_p, r_y)
                        elif LV >= 3:
                            bn_, r_bn_ = bcn[tt % 2], r_bcn[tt % 2]
                            kd_, r_kd_ = kid[tt % 2], r_kid[tt % 2]
                            self.I("dve", "memset", wr=[r_sm_], ap=sm_[:], constant=0.0)
                            self.I("act", "activation", rd=[r_p, r_sm_], wr=[r_junk], pw=[r_sm_], out=junk[:],
                                   in_=p[:, 0:256], func=AF.Square, accum_out=sm_[:, 0:1])
                            self.I("dve", "tensor_scalar", rd=[r_sm_], wr=[r_sm_], out=sm_[:, 1:2], in0=sm_[:, 0:1],
                                   scalar1=1.0 / 256, scalar2=RMS_EPS, op0=ALU.mult, op1=ALU.add)
                            self.I("act", "activation", rd=[r_sm_], wr=[r_sm_], out=sm_[:, 2:3], in_=sm_[:, 1:2], func=AF.Sqrt)
                            self.I("dve", "reciprocal", rd=[r_sm_], wr=[r_sm_], out=sm_[:, 3:4], in_=sm_[:, 2:3])
                            self.I("dve", "scalar_tensor_tensor", rd=[r_p, r_sm_, r_kvg], wr=[r_bn_], out=bn_[:],
                                   in0=p[:, 0:256], scalar=sm_[:, 3:4], in1=kvg[:], op0=ALU.mult, op1=ALU.mult)
                            self.I("act", "copy", rd=[r_p], wr=[r_kd_], out=kd_[:, 128:192], in_=p[:, 256:320])
                            self.rope(p[:, 256:320], kd_[:, 128:192], 1, 64, 8, 16, tt, t4, r_t4, r_p, r_kd_)
                            self.I("dve", "tensor_copy", rd=[r_kd_], pw=[r_kd_], out=kd_[:, 192:256], in_=kd_[:, 128:192])
                            lf = LFW[:, tt, 0:4]
                            self.I("dve", "tensor_tensor", rd=[r_p, r_bfb], wr=[r_sm_], out=sm_[:, 4:8], in0=p[:, 320:324],
                                   in1=bfb[:], op=ALU.add)
                            self.I("act", "activation", rd=[r_sm_], wr=[r_sm_], out=sm_[:, 4:8], in_=sm_[:, 4:8],
                                   func=AF.Exp, scale=-1.0)
                            self.I("act", "activation", rd=[r_sm_], wr=[r_sm_], out=sm_[:, 4:8], in_=sm_[:, 4:8],
                                   func=AF.Ln, bias=1.0, scale=1.0)
                            self.I("dve", "tensor_scalar", rd=[r_sm_], pw=[r_LFW], out=lf, in0=sm_[:, 4:8],
                                   scalar1=-1.0, scalar2=None, op0=ALU.mult)
                            self.I("act", "copy", rd=[r_p], pw=[r_LFW], out=LFW[:, tt, 4:12], in_=p[:, 324:332])
                            pt_, r_pt_ = ptr[ntr % 2], r_ptr[ntr % 2]
                            ntr += 1
                            bT, r_bT = bcnT[tt % 2], r_bcnT[tt % 2]
                            self.tr(pt_[:, 0:128], bn_[:, 0:128], [r_bn_], r_pt_, True)
                            self.tr(pt_[:, 128:256], bn_[:, 128:256], [r_bn_], r_pt_, False)
                            self.I("dve", "tensor_copy", rd=[r_pt_], wr=[r_bT], out=bT[:], in_=pt_[:, 0:256])
                            self.mm(pkv[:], bT[:, 0:128], Wk[:, 0, :], True, False, [r_bT, r_Wk], r_pkv)
                            self.mm(pkv[:], bT[:, 128:256], Wk[:, 1, :], False, True, [r_bT, r_Wk], r_pkv)
                            vb_, r_vb_ = VB[tt % 2], r_VB[tt % 2]
                            self.I("act", "copy", rd=[r_pkv], wr=[r_vb_], out=vb_[:], in_=pkv[:, 128:256])
                            self.I("act", "copy", rd=[r_pkv], pw=[r_kd_], out=kd_[:, 0:128], in_=pkv[:, 0:128])
                            self.rope(pkv[:, 0:128], kd_[:, 0:128], 1, 128, 16, 0, tt, t4, r_t4, r_pkv, r_kd_)
                    if LV < 4:
                        continue
                    groups = [("aq", 0), ("ak", 512), ("bq", 1536), ("biq", 2048), ("cq", 2560), ("ck", 3072)]
                    for gi in range(0, 6, 2):
                        pt_, r_pt_ = ptr[ntr % 2], r_ptr[ntr % 2]
                        ntr += 1
                        for g2 in range(2):
                            name, yoff = groups[gi + g2]
                            for j in range(4):
                                self.tr(pt_[:, (g2 * 4 + j) * 128:(g2 * 4 + j + 1) * 128],
                                        y[:, yoff + j * 128: yoff + (j + 1) * 128], [r_y], r_pt_, g2 == 0 and j == 0)
                        for g2 in range(2):
                            name, yoff = groups[gi + g2]
                            eng = "act" if g2 == 0 else "dve"
                            meth = "copy" if g2 == 0 else "tensor_copy"
                            self.I(eng, meth, rd=[r_pt_], pw=[r_st_],
                                   out=st_[:, CH[name]:CH[name] + 4, i * 128:(i + 1) * 128],
                                   in_=pt_[:, g2 * 512:(g2 + 1) * 512].rearrange("p (c t) -> p c t", c=4))
                    pt_, r_pt_ = ptr[ntr % 2], r_ptr[ntr % 2]
                    ntr += 1
                    kd_, r_kd_ = kid[tt % 2], r_kid[tt % 2]
                    self.tr(pt_[:, 0:128], kd_[:, 0:128], [r_kd_], r_pt_, True)
                    self.tr(pt_[:, 128:256], kd_[:, 128:256], [r_kd_], r_pt_, False)
                    self.I("dve", "tensor_copy", rd=[r_pt_], pw=[r_st_],
                           out=st_[:, CH["kb"]:CH["kb"] + 2, i * 128:(i + 1) * 128],
                           in_=pt_[:, 0:256].rearrange("p (c t) -> p c t", c=2))
                    if LV < 5:
                        continue
                    rows = slice(tt * 128, (tt + 1) * 128)
                    self.dma("sp", self.vtok[rows, 0:512], y[:, 1024:1536], [r_y], r_y, pw=[self.r_vtok])
                    self.dma("sp", self.vtok[rows, 640:1152], y[:, 3584:4096], [r_y], r_y, pw=[self.r_vtok])
                    self.dma("sp", self.vtok[rows, 512:640], VB[tt % 2][:], [r_VB[tt % 2]], r_VB[tt % 2], pw=[self.r_vtok])
                if LV >= 6:
                    self.dma("sp", self.qkT[:, :, b * 512:(b + 1) * 512].rearrange("c p t -> p c t"), st_[:],
                             [r_st_], r_st_, pw=[self.r_qkT])
            if LV >= 6:
                self.dma("sp", self.lfw[:, :, :], LFW[:], [r_LFW], r_LFW, pw=[self.r_lfw])
            self.S.flush()

    def attn_core(self, env, qT, kT, V, rd_q, rd_k, rd_v, scale, bias_fn, rd_bias, mask, fin, Gs=None):
        GPS = self.GPS
        pss, r_pss = env["pss"], env["r_pss"]
        oa, r_oa = env["OA"], env["r_OA"]
        PT, r_PT = env["PT"], env["r_PT"]
        q = env.setdefault("queue", [])
        NP = len(pss)
        NPT = len(PT)
        for G in (range(GPS) if Gs is None else Gs):
            for J in range(4 * G + 4):
                i0 = max(J - 4 * G, 0)
                n = env["n"] = env.get("n", 0) + 1
                p, r_p = pss[n % NP], r_pss[n % NP]
                pt, r_pt = PT[n % NPT], r_PT[n % NPT]
                c0 = i0 * 128
                diag = J >= 4 * G
                self.mm(p[:, c0:512], kT[:, J * 128:(J + 1) * 128], qT[:, G * 512 + c0:(G + 1) * 512], True,
                        (mask is None and not diag), rd_k + rd_q, r_p)
                if mask is not None:
                    MB, r_MB = mask
                    for i in range(i0, 4):
                        self.mm(p[:, i * 128:(i + 1) * 128], MB[:, i, J * 128:(J + 1) * 128], self.ident[:],
                                False, i == 3, [r_MB, self.r_const], r_p)
                elif diag:
                    self.mm(p[:, c0:c0 + 128], self.ident[:], self.trim[:], False, True, [self.r_const], r_p)
                if bias_fn is None:
                    self.I("act", "activation", rd=[r_p], wr=[r_pt], out=pt[:, c0:512], in_=p[:, c0:512],
                           func=AF.Exp, scale=scale)
                else:
                    first = True
                    for ip in range(2):
                        lo, hi = max(c0, 256 * ip), 256 * (ip + 1)
                        if lo >= hi:
                            continue
                        self.I("act", "activation", rd=[r_p] + rd_bias, wr=[r_pt] if first else (),
                               pw=() if first else [r_pt], out=pt[:, lo:hi], in_=p[:, lo:hi], func=AF.Exp,
                               scale=scale, bias=bias_fn(4 * G + 2 * ip + 1, J))
                        first = False

                def stage2(G=G, J=J, i0=i0, pt=pt, r_pt=r_pt, V=V, rd_v=rd_v, fin=fin):
                    for i in range(i0, 4):
                        self.mm(oa[i][:, 0:129], pt[:, i * 128:(i + 1) * 128], V[:, J, :],
                                J == 0, J == 4 * G + i, [r_pt] + rd_v, r_oa[i])
                    if J == 4 * G + 3:
                        for i in range(4):
                            fin(G, i, oa[i][:, 0:129], r_oa[i])
                q.append(stage2)
                if len(q) > 2:
                    q.pop(0)()

    def attn_drain(self, env):
        q = env.setdefault("queue", [])
        while q:
            q.pop(0)()

    def attn_env(self, sb, ps):
        env = dict(pss=[ps("pss%d" % i, [128, 512], F32) for i in range(3)], r_pss=[PRes("pss%d" % i) for i in range(3)],
                   OA=[ps("OA%d" % i, [128, 512], F32) for i in range(4)], r_OA=[PRes("OA%d" % i) for i in range(4)],
                   PT=[sb("PT%d" % i, [128, 512], BF16) for i in range(4)], r_PT=[Res("PT%d" % i) for i in range(4)],
                   ptr=ps("aptr", [128, 1024], BF16), r_ptr=PRes("aptr"),
                   ob=[sb("ob%d" % i, [128, 512], BF16) for i in range(2)], r_ob=[Res("ob0"), Res("ob1")],
                   ost=[sb("ost%d" % i, [128, 512], BF16) for i in range(2)], r_ost=[Res("ost0"), Res("ost1")],
                   rec=[sb("rec%d" % i, [128, 8], F32) for i in range(4)], r_rec=[Res("rec%d" % i) for i in range(4)])
        return env

    def attn_store(self, env, ob, r_ob, chunk, tok0):
        k = env["k"] = env.get("k", 0) + 1
        ptr, r_ptr = env["ptr"], env["r_ptr"]
        ost, r_ost = env["ost"][k % 2], env["r_ost"][k % 2]
        for i in range(4):
            self.tr(ptr[:, i * 128:(i + 1) * 128], ob[:, i * 128:(i + 1) * 128], [r_ob], r_ptr, i == 0)
        self.I("dve", "tensor_copy", rd=[r_ptr], wr=[r_ost], out=ost[:], in_=ptr[:, 0:512])
        self.dma("sp", self.oT[chunk, :, tok0:tok0 + 512], ost[:], [r_ost], r_ost, pw=[self.r_oT])

    def load_v(self, Vt, r_V, s, col0):
        self.dma("sp", Vt[:, :, 0:128],
                 self.vtok[s * self.SL:(s + 1) * self.SL, col0:col0 + 128].rearrange("(j p) d -> p j d", p=128),
                 [self.r_vtok], r_V, wr=[r_V])

    def phase_attn_a(self, l, s):
        nc, SL, TPS = self.nc, self.SL, self.TPS
        t0 = s * SL
        with contextlib.ExitStack() as ph:
            sb = lambda n, s_, d: ph.enter_context(nc.sbuf_tensor(self.un(n), s_, d))
            ps = lambda n, s_, d: ph.enter_context(nc.psum_tensor(self.un(n), s_, d))
            LF = sb("LF", [128, TPS, 12], F32)
            r_LF = Res("LF")
            self.dma("sp", LF[:], self.lfw[:, s * TPS:(s + 1) * TPS, :], [self.r_lfw], r_LF, wr=[r_LF])
            lf = sb("lf4", [128, 4, TPS], F32)
            r_lf = Res("lf4")
            self.I("dve", "tensor_copy", rd=[r_LF], wr=[r_lf], out=lf[:], in_=LF[:, :, 0:4].rearrange("p j h -> p h j"))
            ph2 = contextlib.ExitStack()
            pcs = ph2.enter_context(nc.psum_tensor(self.un("pcs"), [128, 4 * TPS], F32))
            r_pcs = PRes("pcs")
            lf2 = lf[:].rearrange("p h j -> p (h j)")
            self.mm(pcs[:], self.utri[:], lf2, True, True, [r_lf, self.r_const], r_pcs)
            cs = sb("cs", [128, 4, TPS], F32)
            r_cs = Res("cs")
            self.I("dve", "tensor_copy", rd=[r_pcs], wr=[r_cs], out=cs[:].rearrange("p h j -> p (h j)"), in_=pcs[:])
            ex = sb("ex", [128, 4, TPS], F32)
            r_ex = Res("ex")
            self.I("dve", "memset", wr=[r_ex], ap=ex[:, :, 0:1], constant=0.0)
            for j in range(1, TPS):
                self.I("dve", "tensor_tensor", rd=[r_ex, r_cs], wr=[r_ex], out=ex[:, :, j:j + 1], in0=ex[:, :, j - 1:j],
                       in1=cs[:, :, j - 1:j], op=ALU.add)
            self.mm(pcs[:], self.e127[:], ex[:].rearrange("p h j -> p (h j)"), True, True, [r_ex, self.r_const], r_pcs)
            cum = sb("cum", [128, 4, TPS], F32)
            r_cum = Res("cum")
            self.I("dve", "tensor_tensor", rd=[r_pcs, r_cs], wr=[r_cum], out=cum[:].rearrange("p h j -> p (h j)"),
                   in0=pcs[:], in1=cs[:].rearrange("p h j -> p (h j)"), op=ALU.add)
            self.mm(pcs[:], self.e127[:], cum[:].rearrange("p h j -> p (h j)"), True, True, [r_cum, self.r_const], r_pcs)
            cend = sb("cend", [128, 4, TPS], F32)
            r_cend = Res("cend")
            self.I("dve", "tensor_copy", rd=[r_pcs], wr=[r_cend], out=cend[:].rearrange("p h j -> p (h j)"), in_=pcs[:])
            bias = sb("biasA", [128, 4, TPS, TPS], F32)
            r_bias = Res("biasA")
            for h in range(4):
                for I_ in range(TPS):
                    self.I("dve", "tensor_scalar", rd=[r_cum, r_cend], pw=[r_bias], out=bias[:, h, I_, 0:I_ + 1],
                           in0=cum[:, h, 0:I_ + 1], scalar1=-1.0, scalar2=cend[:, h, I_:I_ + 1], op0=ALU.mult, op1=ALU.add)
            self.S.flush()
            ph2.close()
            env = self.attn_env(sb, ps)
            qT = [sb("qTa%d" % i, [128, SL], BF16) for i in range(2)]
            kT = [sb("kTa%d" % i, [128, SL], BF16) for i in range(2)]
            Vt = [sb("Va%d" % i, [128, TPS, 129], BF16) for i in range(2)]
            r_q, r_k, r_v = [Res("q0"), Res("q1")], [Res("k0"), Res("k1")], [Res("v0"), Res("v1")]
            r_v1 = [Res("v10"), Res("v11")]
            for i in range(2):
                self.I("pool", "memset", wr=[r_v1[i]], ap=Vt[i][:, :, 128:129], constant=1.0)
            for h in range(4):
                b2 = h % 2
                self.dma("sp", qT[b2][:], self.qkT[CH["aq"] + h, :, t0:t0 + SL], [self.r_qkT], r_q[b2], wr=[r_q[b2]])
                self.dma("sp", kT[b2][:], self.qkT[CH["ak"] + h, :, t0:t0 + SL], [self.r_qkT], r_k[b2], wr=[r_k[b2]])
                self.load_v(Vt[b2], r_v[b2], s, h * 128)

                def fin(G, i, O, r_O, h=h):
                    k = env["fk"] = env.get("fk", 0) + 1
                    rec, r_rec = env["rec"][k % 4], env["r_rec"][k % 4]
                    ob, r_ob = env["ob"][(k - 1) // 4 % 2], env["r_ob"][(k - 1) // 4 % 2]
                    self.I("dve", "reciprocal", rd=[r_O], wr=[r_rec], out=rec[:, 0:1], in_=O[:, 128:129])
                    self.I("dve", "tensor_scalar", rd=[r_O, r_rec], wr=[r_ob] if i == 0 else (), pw=() if i == 0 else [r_ob],
                               out=ob[:, i * 128:(i + 1) * 128], in0=O[:, 0:128], scalar1=rec[:, 0:1], scalar2=None, op0=ALU.mult)
                    if i == 3:
                        self.attn_store(env, ob, r_ob, h, t0 + G * 512)

                self.attn_core(env, qT[b2][:], kT[b2][:], Vt[b2], [r_q[b2]], [r_k[b2]], [r_v[b2], r_v1[b2]],
                               128 ** -0.5, lambda I_, J, h=h: bias[:, h, I_, J:J + 1], [r_bias], None, fin)
            self.attn_drain(env)
            self.S.flush()

    def phase_attn_c(self, l, s):
        nc, SL, TPS = self.nc, self.SL, self.TPS
        t0 = s * SL
        lam_init = 0.8 - 0.6 * math.exp(-0.3 * l)
        with contextlib.ExitStack() as ph:
            sb = lambda n, s_, d: ph.enter_context(nc.sbuf_tensor(self.un(n), s_, d))
            ps = lambda n, s_, d: ph.enter_context(nc.psum_tensor(self.un(n), s_, d))
            env = self.attn_env(sb, ps)
            lq = sb("lq", [128, 256], F32)
            r_lq = Res("lq")
            self.dma("sp", lq[:], self.lam_qk[l].partition_broadcast(128), [self.r_in], r_lq, wr=[r_lq])
            lt = sb("lt", [128, 128], F32)
            r_lt = Res("lt")
            lv = sb("lv", [128, 8], F32)
            r_lv = Res("lv")
            self.I("dve", "memset", wr=[r_lv], ap=lv[:], constant=0.0)
            self.I("dve", "tensor_tensor", rd=[r_lq], wr=[r_lt], out=lt[:, 0:64], in0=lq[:, 0:64], in1=lq[:, 64:128], op=ALU.mult)
            self.I("dve", "tensor_tensor", rd=[r_lq], pw=[r_lt], out=lt[:, 64:128], in0=lq[:, 128:192], in1=lq[:, 192:256], op=ALU.mult)
            self.I("dve", "reduce_sum", rd=[r_lt], wr=[r_lv], out=lv[:, 0:1], in_=lt[:, 0:64], axis=AX.X)
            self.I("dve", "reduce_sum", rd=[r_lt, r_lv], wr=[r_lv], out=lv[:, 1:2], in_=lt[:, 64:128], axis=AX.X)
            self.I("act", "activation", rd=[r_lv], wr=[r_lv], out=lv[:, 2:4], in_=lv[:, 0:2], func=AF.Exp)
            self.I("dve", "tensor_tensor", rd=[r_lv], wr=[r_lv], out=lv[:, 4:5], in0=lv[:, 3:4], in1=lv[:, 2:3], op=ALU.subtract)
            self.I("dve", "tensor_scalar", rd=[r_lv], wr=[r_lv], out=lv[:, 5:6], in0=lv[:, 4:5], scalar1=-lam_init,
                   scalar2=None, op0=ALU.add)
            dg = sb("dg", [128, 128], F32)
            r_dg = Res("dg")
            self.dma("sp", dg[:], self.diff_norm_g[l].partition_broadcast(128), [self.r_in], r_dg, wr=[r_dg])
            self.I("dve", "tensor_scalar", rd=[r_dg], wr=[r_dg], out=dg[:], in0=dg[:], scalar1=1.0 - lam_init,
                   scalar2=None, op0=ALU.mult)
            qT = [sb("qTc%d" % i, [128, SL], BF16) for i in range(2)]
            kz = [[sb("kz%d_%d" % (c, i), [128, SL], BF16) for i in range(2)] for c in range(2)]
            r_kz = [[Res("kz"), Res("kz")] for c in range(2)]
            r_kzz = [[Res("kzz"), Res("kzz")] for c in range(2)]
            for c in range(2):
                for i in range(2):
                    for j0 in range(0, SL, 1024):
                        self.I("dve", "memset", wr=[r_kzz[c][i]] if j0 == 0 else (), pw=() if j0 == 0 else [r_kzz[c][i]],
                               ap=kz[c][i][64 * (1 - c):64 * (1 - c) + 64, j0:j0 + 1024], constant=0.0)
            Vt = [sb("Vc%d" % i, [128, TPS, 129], BF16) for i in range(2)]
            r_q, r_v = [Res("q0"), Res("q1")], [Res("v0"), Res("v1")]
            r_v1 = [Res("v10"), Res("v11")]
            on0 = [sb("on0_%d" % i, [128, 4, 128], F32) for i in range(2)]
            r_on0 = [Res("on0_0"), Res("on0_1")]
            dd = [sb("dd%d" % i, [128, 128], F32) for i in range(2)]
            r_dd = [Res("dd0"), Res("dd1")]
            jk = sb("jkc", [128, 128], F32)
            r_jk = Res("jkc")
            for i in range(2):
                self.I("pool", "memset", wr=[r_v1[i]], ap=Vt[i][:, :, 128:129], constant=1.0)
            for h in range(4):
                b2 = h % 2
                self.dma("sp", qT[b2][:], self.qkT[CH["cq"] + h, :, t0:t0 + SL], [self.r_qkT], r_q[b2], wr=[r_q[b2]])
                for c in range(2):
                    self.dma("sp", kz[c][b2][64 * c:64 * c + 64, :], self.qkT[CH["ck"] + h, 64 * c:64 * c + 64, t0:t0 + SL],
                             [self.r_qkT], r_kz[c][b2], wr=[r_kz[c][b2]])
                self.load_v(Vt[b2], r_v[b2], s, 640 + h * 128)
                for G in range(self.GPS):
                    gk = env["gk"] = env.get("gk", 0) + 1
                    o0, r_o0 = on0[gk % 2], r_on0[gk % 2]

                    def fin0(G, i, O, r_O, o0=o0, r_o0=r_o0):
                        k = env["fk"] = env.get("fk", 0) + 1
                        rec, r_rec = env["rec"][k % 4], env["r_rec"][k % 4]
                        self.I("dve", "reciprocal", rd=[r_O], wr=[r_rec], out=rec[:, 0:1], in_=O[:, 128:129])
                        self.I("dve", "tensor_scalar", rd=[r_O, r_rec], wr=[r_o0] if i == 0 else (), pw=() if i == 0 else [r_o0],
                               out=o0[:, i, :], in0=O[:, 0:128], scalar1=rec[:, 0:1], scalar2=None, op0=ALU.mult)

                    def fin1(G, i, O, r_O, o0=o0, r_o0=r_o0, h=h):
                        k = env["fk"] = env.get("fk", 0) + 1
                        rec, r_rec = env["rec"][k % 4], env["r_rec"][k % 4]
                        d_, r_d = dd[k % 2], r_dd[k % 2]
                        kk = env["ck"] = env.get("ck", 0) + 1
                        ob, r_ob = env["ob"][(kk - 1) // 4 % 2], env["r_ob"][(kk - 1) // 4 % 2]
                        self.I("dve", "memset", wr=[r_rec], ap=rec[:], constant=0.0)
                        self.I("dve", "reciprocal", rd=[r_O, r_rec], wr=[r_rec], out=rec[:, 0:1], in_=O[:, 128:129])
                        self.I("dve", "tensor_tensor", rd=[r_rec, r_lv], wr=[r_rec], out=rec[:, 1:2], in0=rec[:, 0:1],
                               in1=lv[:, 5:6], op=ALU.mult)
                        self.I("dve", "scalar_tensor_tensor", rd=[r_O, r_rec, r_o0], wr=[r_d], out=d_[:], in0=O[:, 0:128],
                               scalar=rec[:, 1:2], in1=o0[:, i, :], op0=ALU.mult, op1=ALU.add)
                        self.I("act", "activation", rd=[r_d, r_rec], wr=[r_jk], pw=[r_rec], out=jk[:], in_=d_[:],
                               func=AF.Square, accum_out=rec[:, 2:3])
                        self.I("dve", "tensor_scalar", rd=[r_rec], wr=[r_rec], out=rec[:, 3:4], in0=rec[:, 2:3],
                               scalar1=1.0 / 128, scalar2=RMS_EPS, op0=ALU.mult, op1=ALU.add)
                        self.I("act", "activation", rd=[r_rec], wr=[r_rec], out=rec[:, 4:5], in_=rec[:, 3:4], func=AF.Sqrt)
                        self.I("dve", "reciprocal", rd=[r_rec], wr=[r_rec], out=rec[:, 5:6], in_=rec[:, 4:5])
                        self.I("dve", "scalar_tensor_tensor", rd=[r_d, r_rec, r_dg], wr=[r_ob] if i == 0 else (),
                               pw=() if i == 0 else [r_ob], out=ob[:, i * 128:(i + 1) * 128], in0=d_[:],
                               scalar=rec[:, 5:6], in1=dg[:], op0=ALU.mult, op1=ALU.mult)
                        if i == 3:
                            self.attn_store(env, ob, r_ob, 8 + h, t0 + G * 512)

                    for c in range(2):
                        self.attn_core(env, qT[b2][:], kz[c][b2][:], Vt[b2],
                                       [r_q[b2]], [r_kz[c][b2], r_kzz[c][b2]], [r_v[b2], r_v1[b2]], 64 ** -0.5, None, [], None,
                                       fin0 if c == 0 else fin1, Gs=[G])
            self.attn_drain(env)
            self.S.flush()

    def phase_idx(self, l, s):
        nc, SL, TPS = self.nc, self.SL, self.TPS
        t0 = s * SL
        with contextlib.ExitStack() as ph:
            sb = lambda n, s_, d: ph.enter_context(nc.sbuf_tensor(self.un(n), s_, d))
            ps = lambda n, s_, d: ph.enter_context(nc.psum_tensor(self.un(n), s_, d))
            kiT = sb("kiT", [128, SL], BF16)
            qiT = sb("qiT", [128, 4, SL], BF16)
            wi = sb("wi", [128, TPS, 12], F32)
            r_ki, r_qi, r_wi = Res("kiT"), Res("qiT"), Res("wi")
            self.dma("sp", kiT[:], self.qkT[CH["ki"], :, t0:t0 + SL], [self.r_qkT], r_ki, wr=[r_ki])
            self.dma("sp", qiT[:], self.qkT[CH["biq"]:CH["biq"] + 4, :, t0:t0 + SL].rearrange("c p t -> p c t"),
                     [self.r_qkT], r_qi, wr=[r_qi])
            self.dma("sp", wi[:], self.lfw[:, s * TPS:(s + 1) * TPS, :], [self.r_lfw], r_wi, wr=[r_wi])
            SC = [sb("SC%d" % i, [128, SL], F32) for i in range(2)]
            r_SC = [Res("SC0"), Res("SC1")]
            MBt = [sb("MBt%d" % i, [128, SL], BF16) for i in range(2)]
            r_MBt = [Res("MBt0"), Res("MBt1")]
            Rl = [sb("Rl%d" % i, [128, 1024], F32) for i in range(3)]
            r_Rl = [Res("Rl%d" % i) for i in range(3)]
            jk = sb("jki", [128, SL], BF16)
            r_jk = Res("jki")
            bs = [sb("bs%d" % i, [128, 8 + 2 * NBIS], F32) for i in range(2)]
            r_bs = [Res("bs0"), Res("bs1")]
            bsA = [sb("bsA%d" % i, [128, NBIS], F32) for i in range(2)]
            r_bsA = [Res("bsA0"), Res("bsA1")]
            bsD = [sb("bsD%d" % i, [128, NBIS], F32) for i in range(2)]
            r_bsD = [Res("bsD0"), Res("bsD1")]
            jk2 = sb("jki2", [128, SL], BF16)
            r_jk2 = Res("jki2")
            pp = [ps("pi%d" % i, [128, 1024], F32) for i in range(3)]
            r_pp = [PRes("pi%d" % i) for i in range(3)]
            cnt = {"n": 0}

            def units(I_):
                q1 = (I_ + 1) * 128
                return [(I_, c, hh, min(1024, q1 - c * 1024)) for c in range((q1 + 1023) // 1024) for hh in range(8)]

            def unit(I_, c, hh, w):
                sc, r_sc = SC[I_ % 2], r_SC[I_ % 2]
                n = cnt["n"] = cnt["n"] + 1
                p, r_p = pp[n % 3], r_pp[n % 3]
                r0 = 64 * (hh % 2)
                for j0 in range(0, w, 512):
                    w2 = min(512, w - j0)
                    self.mm(p[:, j0:j0 + w2], qiT[r0:r0 + 64, hh // 2, I_ * 128:(I_ + 1) * 128],
                            kiT[r0:r0 + 64, c * 1024 + j0:c * 1024 + j0 + w2], True, True, [r_qi, r_ki], r_p)
                if hh == 0:
                    self.I("dve", "tensor_scalar", rd=[r_p, r_wi], wr=[r_sc] if c == 0 else (),
                           pw=() if c == 0 else [r_sc], out=sc[:, c * 1024:c * 1024 + w], in0=p[:, 0:w],
                           scalar1=0.0, scalar2=wi[:, I_, 4:5], op0=ALU.max, op1=ALU.mult)
                else:
                    rl, r_rl = Rl[n % 3], r_Rl[n % 3]
                    self.I("act", "activation", rd=[r_p], wr=[r_rl], out=rl[:, 0:w], in_=p[:, 0:w], func=AF.Relu)
                    self.I("dve", "scalar_tensor_tensor", rd=[r_rl, r_wi, r_sc], pw=[r_sc],
                           out=sc[:, c * 1024:c * 1024 + w], in0=rl[:, 0:w], scalar=wi[:, I_, 4 + hh:5 + hh],
                           in1=sc[:, c * 1024:c * 1024 + w], op0=ALU.mult, op1=ALU.add)

            def final(I_):
                sc, r_sc = SC[I_ % 2], r_SC[I_ % 2]
                q1 = (I_ + 1) * 128
                self.I("dve", "tensor_tensor", rd=[r_sc, self.r_const], wr=[r_sc], out=sc[:, I_ * 128:q1],
                       in0=sc[:, I_ * 128:q1], in1=self.caus[:], op=ALU.add)

            def mb_out(I_):
                sc, r_sc = SC[I_ % 2], r_SC[I_ % 2]
                b_, r_b = bs[I_ % 2], r_bs[I_ % 2]
                q1 = (I_ + 1) * 128
                mbt, r_mbt = MBt[I_ % 2], r_MBt[I_ % 2]
                self.I("dve", "tensor_scalar", rd=[r_sc, r_b], wr=[r_mbt], out=mbt[:, 0:q1], in0=sc[:, 0:q1],
                       scalar1=b_[:, 5:6], scalar2=NEG, op0=ALU.is_lt, op1=ALU.mult)
                self.dma("sp", self.mb[t0 + I_ * 128:t0 + q1, 0:q1], mbt[:, 0:q1], [r_mbt], r_mbt, pw=[self.r_mb])

            for I_ in range(min(2, TPS)):
                for u in units(I_):
                    unit(*u)
                final(I_)
                self.I("dve", "memset", wr=[r_bs[I_ % 2]], ap=bs[I_ % 2][:, 5:6], constant=-1e29)
                mb_out(I_)
            if TPS > 2:
                for u in units(2):
                    unit(*u)
                final(2)
            for I_ in range(2, TPS):
                q1 = (I_ + 1) * 128
                sc, r_sc = SC[I_ % 2], r_SC[I_ % 2]
                b_, r_b = bs[I_ % 2], r_bs[I_ % 2]
                nxt = units(I_ + 1) if I_ + 1 < TPS else []
                bA, r_bA = bsA[I_ % 2], r_bsA[I_ % 2]
                bD, r_bD = bsD[I_ % 2], r_bsD[I_ % 2]
                self.I("dve", "memset", wr=[r_b], ap=b_[:], constant=0.0)
                self.I("dve", "memset", wr=[r_bA], ap=bA[:], constant=0.0)
                self.I("dve", "memset", wr=[r_bD], ap=bD[:], constant=0.0)
                self.I("dve", "reduce_max", rd=[r_sc, r_b], wr=[r_b], out=b_[:, 0:1], in_=sc[:, 0:q1], axis=AX.X)
                self.I("dve", "tensor_reduce", rd=[r_sc, r_b], wr=[r_b], out=b_[:, 1:2], in_=sc[:, 0:I_ * 128],
                       axis=AX.X, op=ALU.min)
                self.I("dve", "tensor_tensor", rd=[r_b], wr=[r_b], out=b_[:, 2:3], in0=b_[:, 0:1], in1=b_[:, 1:2],
                       op=ALU.subtract)
                self.I("dve", "tensor_scalar", rd=[r_b, self.r_const], wr=[r_b], out=b_[:, 8:8 + NBIS], in0=self.pw[:],
                       scalar1=b_[:, 2:3], scalar2=None, op0=ALU.mult)
                self.I("dve", "scalar_tensor_tensor", rd=[r_b], wr=[r_b], out=b_[:, 3:4], in0=b_[:, 2:3], scalar=0.5,
                       in1=b_[:, 1:2], op0=ALU.mult, op1=ALU.add)
                done = 0
                a = min(q1, max(128, int(0.85 * q1 / 128) * 128))
                for k in range(NBIS):
                    cc = 8 + NBIS + k
                    self.I("act", "activation", rd=[r_sc, r_b], wr=[r_jk], pw=[r_bA], out=jk[:, 0:a], in_=sc[:, 0:a],
                           func=AF.Sign, scale=-1.0, bias=b_[:, 3:4], accum_out=bA[:, k:k + 1])
                    if a < q1:
                        self.I("dve", "tensor_scalar", rd=[r_sc, r_b], wr=[r_jk2], pw=[r_bD], out=jk2[:, a:q1], in0=sc[:, a:q1],
                               scalar1=b_[:, 3:4], scalar2=0.0, op0=ALU.is_ge, op1=ALU.add, accum_out=bD[:, k:k + 1])
                    self.I("dve", "scalar_tensor_tensor", rd=[r_bA, r_bD], wr=[r_b], out=b_[:, 6:7], in0=bD[:, k:k + 1],
                           scalar=2.0, in1=bA[:, k:k + 1], op0=ALU.mult, op1=ALU.subtract)
                    self.I("dve", "tensor_scalar", rd=[r_b], wr=[r_b], out=b_[:, 4:5], in0=b_[:, 6:7],
                           scalar1=float(511.5 - a), scalar2=0.5, op0=ALU.is_ge, op1=ALU.subtract)
                    self.I("dve", "scalar_tensor_tensor", rd=[r_b], wr=[r_b], out=b_[:, 3:4], in0=b_[:, 4:5],
                           scalar=b_[:, 8 + k:9 + k], in1=b_[:, 3:4], op0=ALU.mult, op1=ALU.add)
                    upto = (len(nxt) * (k + 1)) // NBIS
                    for u in nxt[done:upto]:
                        unit(*u)
                    done = upto
                self.I("dve", "tensor_tensor", rd=[r_b], wr=[r_b], out=b_[:, 5:6], in0=b_[:, 3:4],
                       in1=b_[:, 8 + NBIS - 1:8 + NBIS], op=ALU.subtract)
                if I_ + 1 < TPS:
                    final(I_ + 1)
                mb_out(I_)
            self.S.flush()

    def phase_attn_b(self, l, s):
        nc, SL, TPS = self.nc, self.SL, self.TPS
        t0 = s * SL
        with contextlib.ExitStack() as ph:
            sb = lambda n, s_, d: ph.enter_context(nc.sbuf_tensor(self.un(n), s_, d))
            ps = lambda n, s_, d: ph.enter_context(nc.psum_tensor(self.un(n), s_, d))
            env = self.attn_env(sb, ps)
            qT = sb("qTb", [128, 4, SL], BF16)
            kT = sb("kTb", [128, SL], BF16)
            Vt = sb("Vb", [128, TPS, 129], BF16)
            r_q, r_k, r_v, r_v1 = Res("qb"), Res("kb"), Res("vb"), Res("vb1")
            self.I("pool", "memset", wr=[r_v1], ap=Vt[:, :, 128:129], constant=1.0)
            self.dma("sp", qT[:], self.qkT[CH["bq"]:CH["bq"] + 4, :, t0:t0 + SL].rearrange("c p t -> p c t"),
                     [self.r_qkT], r_q, wr=[r_q])
            self.dma("sp", kT[:], self.qkT[CH["kb"], :, t0:t0 + SL], [self.r_qkT], r_k, wr=[r_k])
            self.load_v(Vt, r_v, s, 512)
            MB = [sb("MB%d" % i, [128, 4, SL], BF16) for i in range(2)]
            r_MB = [Res("MB0"), Res("MB1")]
            for G in range(self.GPS):
                mbt, r_mbt = MB[G % 2], r_MB[G % 2]
                for i in range(4):
                    q1 = (4 * G + i + 1) * 128
                    self.dma("sp", mbt[:, i, 0:q1], self.mb[t0 + q1 - 128:t0 + q1, 0:q1],
                             [self.r_mb], r_mbt, wr=[r_mbt] if i == 0 else (), pw=() if i == 0 else [r_mbt])
                for h in range(4):
                    def fin(G, i, O, r_O, h=h):
                        k = env["fk"] = env.get("fk", 0) + 1
                        rec, r_rec = env["rec"][k % 4], env["r_rec"][k % 4]
                        ob, r_ob = env["ob"][(k - 1) // 4 % 2], env["r_ob"][(k - 1) // 4 % 2]
                        self.I("dve", "reciprocal", rd=[r_O], wr=[r_rec], out=rec[:, 0:1], in_=O[:, 128:129])
                        self.I("dve", "tensor_scalar", rd=[r_O, r_rec], wr=[r_ob] if i == 0 else (), pw=() if i == 0 else [r_ob],
                               out=ob[:, i * 128:(i + 1) * 128], in0=O[:, 0:128], scalar1=rec[:, 0:1], scalar2=None, op0=ALU.mult)
                        if i == 3:
                            self.attn_store(env, ob, r_ob, 4 + h, t0 + G * 512)
                    self.attn_core(env, qT[:, h, :], kT[:], Vt, [r_q], [r_k], [r_v, r_v1], 128 ** -0.5, None, [],
                                   (mbt, r_mbt), fin, Gs=[G])
            self.attn_drain(env)
            self.S.flush()

    def load_w(self, dst, src, r, nparts=1):
        n1 = dst.shape[1]
        step = (n1 + nparts - 1) // nparts
        for j, a in enumerate(range(0, n1, step)):
            b = min(n1, a + step)
            self.dma("pool", dst[:, a:b], src[:, a:b], [self.r_in], r[j], wr=[r[j]])

    def phase_p3(self, l):
        nc, NB = self.nc, self.NB
        with contextlib.ExitStack() as ph:
            sb = lambda n, s_, d: ph.enter_context(nc.sbuf_tensor(self.un(n), s_, d))
            ps = lambda n, s_, d: ph.enter_context(nc.psum_tensor(self.un(n), s_, d))
            Wg = sb("Wg", [128, 8, 3 * D], BF16)
            r_Wg = [Res("Wg%d" % i) for i in range(8)]
            self.load_w(Wg, self.w_gate[l].rearrange("(kc p) n -> p kc n", p=128), r_Wg, 8)
            Wb = [sb("Wb%d" % i, [128, 4, D], BF16) for i in range(3)]
            r_Wb = [[Res("Wb%d" % i)] for i in range(3)]
            for i in range(3):
                self.load_w(Wb[i], self.w_br[i][l].rearrange("(kc p) n -> p kc n", p=128), r_Wb[i], 1)
            Wo = sb("Wo", [128, 8, D], BF16)
            r_Wo = [Res("Wo0"), Res("Wo1")]
            self.load_w(Wo, self.w_out[l].rearrange("(kc p) n -> p kc n", p=128), r_Wo, 2)
            env = self.ln_env(sb)
            gB, bB, r_gb = self.load_gb(sb, self.ln1_g[l], self.ln1_b[l], "1")
            hTb = [sb("hTb0", [128, 8, 512], BF16)]
            r_hTb = [Res("hTb0")]
            oTb = [sb("oTb0", [128, 12, 512], BF16)]
            r_oTb = [Res("oTb0")]
            gx = [sb("gx%d" % i, [128, 512], F32) for i in range(3)]
            r_gx = [Res("gx%d" % i) for i in range(3)]
            tm = [sb("tm%d" % i, [128, 512], F32) for i in range(3)]
            r_tm = [Res("tm%d" % i) for i in range(3)]
            mT = sb("mT", [128, 8, 512], BF16)
            r_mT = [Res("mT%d" % i) for i in range(8)]
            Rt = [sb("R%d" % i, [128, 1024], F32) for i in range(2)]
            r_Rt = [Res("R0"), Res("R1")]
            Ht = [sb("H%d" % i, [128, 1024], F32) for i in range(2)]
            r_Ht = [Res("H0"), Res("H1")]
            stage = [sb("hst%d" % i, [128, 8, 512], BF16) for i in range(2)]
            r_stage = [Res("hst0"), Res("hst1")]
            pg = [ps("pg%d" % i, [128, 512], F32) for i in range(2)]
            r_pg = [PRes("pg0"), PRes("pg1")]
            pb = [ps("pb%d" % i, [128, 512], F32) for i in range(2)]
            r_pb = [PRes("pb0"), PRes("pb1")]
            po = [ps("po%d" % i, [128, 512], F32) for i in range(2)]
            r_po = [PRes("po0"), PRes("po1")]
            pt = [ps("pt%d" % i, [128, 1024], BF16) for i in range(2)]
            r_pt = [PRes("pt0"), PRes("pt1")]
            n = 0

            def ldb(b_):
                self.dma("sp", hTb[0][:], self.hT[:, :, b_ * 512:(b_ + 1) * 512].rearrange("c p t -> p c t"),
                         [self.r_hT], r_hTb[0], wr=[r_hTb[0]])
                self.dma("sp", oTb[0][:], self.oT[:, :, b_ * 512:(b_ + 1) * 512].rearrange("c p t -> p c t"),
                         [self.r_oT], r_oTb[0], wr=[r_oTb[0]])
            ldH = lambda t_: self.dma("sp", Ht[t_ % 2][:], self.h_tok[t_ * 128:(t_ + 1) * 128, :], [self.r_h_tok],
                                      r_Ht[t_ % 2], wr=[r_Ht[t_ % 2]])
            ldb(0)
            ldH(0)
            for b in range(NB):
                hb_, r_hb_ = hTb[0], r_hTb[0]
                ob_, r_ob_ = oTb[0], r_oTb[0]
                for dm in range(8):
                    for x in range(3):
                        n += 1
                        g_, r_g_ = pg[n % 2], r_pg[n % 2]
                        b_, r_b_ = pb[n % 2], r_pb[n % 2]
                        gs_, r_gs_ = gx[n % 3], r_gx[n % 3]
                        for kc in range(8):
                            self.mm(g_[:], Wg[:, kc, x * D + dm * 128:x * D + (dm + 1) * 128], hb_[:, kc, :],
                                    kc == 0, kc == 7, [r_Wg[kc], r_hb_], r_g_)
                        self.I("act", "activation", rd=[r_g_], wr=[r_gs_], out=gs_[:], in_=g_[:], func=AF.Sigmoid)
                        for kc in range(4):
                            self.mm(b_[:], Wb[x][:, kc, dm * 128:(dm + 1) * 128], ob_[:, 4 * x + kc, :],
                                    kc == 0, kc == 3, [r_Wb[x][0], r_ob_], r_b_)
                        t_, r_t_ = tm[x], r_tm[x]
                        self.I("dve", "tensor_tensor", rd=[r_gs_, r_b_], wr=[r_t_], out=t_[:], in0=gs_[:], in1=b_[:], op=ALU.mult)
                    self.I("pool", "tensor_tensor", rd=[r_tm[0], r_tm[1]], wr=[r_tm[0]], out=tm[0][:], in0=tm[0][:],
                           in1=tm[1][:], op=ALU.add)
                    self.I("pool", "tensor_tensor", rd=[r_tm[0], r_tm[2]], wr=[r_mT[dm]], out=mT[:, dm, :], in0=tm[0][:],
                           in1=tm[2][:], op=ALU.add)
                if b + 1 < NB:
                    ldb(b + 1)
                for i in range(4):
                    tt = b * 4 + i
                    R, r_R = Rt[tt % 2], r_Rt[tt % 2]
                    H, r_H = Ht[tt % 2], r_Ht[tt % 2]
                    for half in range(2):
                        n += 1
                        o_, r_o_ = po[n % 2], r_po[n % 2]
                        for dm in range(8):
                            self.mm(o_[:], mT[:, dm, i * 128:(i + 1) * 128], Wo[:, dm, half * 512:(half + 1) * 512],
                                    dm == 0, dm == 7, [r_mT[dm], r_Wo[dm // 4]], r_o_)
                        self.I("dve", "scalar_tensor_tensor", rd=[r_H, r_o_], wr=[r_R] if half == 0 else (),
                               pw=() if half == 0 else [r_R], out=R[:, half * 512:(half + 1) * 512],
                               in0=H[:, half * 512:(half + 1) * 512], scalar=ALPHA, in1=o_[:], op0=ALU.mult, op1=ALU.add)
                    if tt + 1 < self.TT:
                        ldH(tt + 1)
                    self.ln_tile(env, R, r_R, gB, bB, r_gb, tt, self.h_tok, self.r_h_tok, stage[b % 2], r_stage[b % 2],
                                 pt[tt % 2], r_pt[tt % 2])
                self.dma("sp", self.hT[:, :, b * 512:(b + 1) * 512].rearrange("c p t -> p c t"),
                         stage[b % 2][:], [r_stage[b % 2]], r_stage[b % 2], pw=[self.r_hT])
            self.S.flush()

    def phase_p4a(self, l):
        nc, NB = self.nc, self.NB
        BPS = self.SL // 512
        with contextlib.ExitStack() as ph:
            sb = lambda n, s_, d: ph.enter_context(nc.sbuf_tensor(self.un(n), s_, d))
            ps = lambda n, s_, d: ph.enter_context(nc.psum_tensor(self.un(n), s_, d))
            Wu = sb("Wu", [128, 8, 2 * D_FF], BF16)
            r_Wu = [Res("Wu%d" % i) for i in range(8)]
            self.load_w(Wu, self.w_up[l].rearrange("(kc p) n -> p kc n", p=128), r_Wu, 8)
            cw = sb("cw", [128, NCF, 4], F32)
            r_cw = Res("cw")
            for j in range(3):
                self.dma("sp", cw[:, :, j:j + 1], self.conv_w[l, j].rearrange("(c p o) -> p c o", p=128, o=1), [self.r_in], r_cw,
                         wr=[r_cw] if j == 0 else (), pw=() if j == 0 else [r_cw], allow_slow_non_contiguous=True)
            self.dma("sp", cw[:, :, 3:4], self.conv_b[l].rearrange("(c p o) -> p c o", p=128, o=1), [self.r_in], r_cw, pw=[r_cw],
                     allow_slow_non_contiguous=True)
            hTb = [sb("hTb%d" % i, [128, 8, 512], BF16) for i in range(2)]
            r_hTb = [Res("hTb0"), Res("hTb1")]
            aT = [sb("aT%d" % i, [128, NCF, 512], BF16) for i in range(2)]
            r_aT = [Res("aT0"), Res("aT1")]
            Gt = [sb("Gt%d" % i, [128, 514], F32) for i in range(3)]
            r_Gt = [Res("Gt%d" % i) for i in range(3)]
            cv = [sb("cv%d" % i, [128, 512], F32) for i in range(3)]
            r_cv = [Res("cv%d" % i) for i in range(3)]
            sl = [sb("sl%d" % i, [128, 512], F32) for i in range(3)]
            r_sl = [Res("sl%d" % i) for i in range(3)]
            halo = sb("halo", [128, NCF, 2], F32)
            r_halo = [Res("halo%d" % c) for c in range(NCF)]
            pg = [ps("pg%d" % i, [128, 512], F32) for i in range(3)]
            r_pg = [PRes("pg%d" % i) for i in range(3)]
            pv = [ps("pv%d" % i, [128, 512], F32) for i in range(3)]
            r_pv = [PRes("pv%d" % i) for i in range(3)]
            n = 0
            ldh = lambda b_: self.dma("sp", hTb[b_ % 2][:], self.hT[:, :, b_ * 512:(b_ + 1) * 512].rearrange("c p t -> p c t"),
                                      [self.r_hT], r_hTb[b_ % 2], wr=[r_hTb[b_ % 2]])
            ldh(0)
            for b in range(NB):
                hb_, r_hb_ = hTb[b % 2], r_hTb[b % 2]
                a_, r_a_ = aT[b % 2], r_aT[b % 2]
                if b + 1 < NB:
                    ldh(b + 1)
                for c in range(NCF):
                    n += 1
                    g_, r_g_ = pg[n % 3], r_pg[n % 3]
                    v_, r_v_ = pv[n % 3], r_pv[n % 3]
                    G_, r_G_ = Gt[n % 3], r_Gt[n % 3]
                    c_, r_c_ = cv[n % 3], r_cv[n % 3]
                    s_, r_s_ = sl[n % 3], r_sl[n % 3]
                    for kc in range(8):
                        self.mm(g_[:], Wu[:, kc, c * 128:(c + 1) * 128], hb_[:, kc, :], kc == 0, kc == 7, [r_Wu[kc], r_hb_], r_g_)
                    for kc in range(8):
                        self.mm(v_[:], Wu[:, kc, D_FF + c * 128:D_FF + (c + 1) * 128], hb_[:, kc, :], kc == 0, kc == 7,
                                [r_Wu[kc], r_hb_], r_v_)
                    self.I("act", "copy", rd=[r_g_], wr=[r_G_], out=G_[:, 2:514], in_=g_[:])
                    if b % BPS == 0:
                        self.I("pool", "memset", pw=[r_G_], ap=G_[:, 0:2], constant=0.0)
                    else:
                        self.I("pool", "tensor_copy", rd=[r_halo[c]], pw=[r_G_], out=G_[:, 0:2], in_=halo[:, c, :])
                    self.I("pool", "tensor_copy", rd=[r_G_], wr=[r_halo[c]], out=halo[:, c, :], in_=G_[:, 512:514])
                    self.I("dve", "tensor_scalar", rd=[r_G_, r_cw], wr=[r_c_], out=c_[:], in0=G_[:, 2:514],
                           scalar1=cw[:, c, 2:3], scalar2=cw[:, c, 3:4], op0=ALU.mult, op1=ALU.add)
                    self.I("dve", "scalar_tensor_tensor", rd=[r_G_, r_cw, r_c_], wr=[r_c_], out=c_[:], in0=G_[:, 1:513],
                           scalar=cw[:, c, 1:2], in1=c_[:], op0=ALU.mult, op1=ALU.add)
                    self.I("dve", "scalar_tensor_tensor", rd=[r_G_, r_cw, r_c_], wr=[r_c_], out=c_[:], in0=G_[:, 0:512],
                           scalar=cw[:, c, 0:1], in1=c_[:], op0=ALU.mult, op1=ALU.add)
                    self.I("act", "activation", rd=[r_c_], wr=[r_s_], out=s_[:], in_=c_[:], func=AF.Silu)
                    self.I("dve", "tensor_tensor", rd=[r_s_, r_v_], wr=[r_a_] if c == 0 else (), pw=() if c == 0 else [r_a_],
                           out=a_[:, c, :], in0=s_[:], in1=v_[:], op=ALU.mult)
                self.dma("sp", self.actT[:, :, b * 512:(b + 1) * 512].rearrange("c p t -> p c t"), a_[:], [r_a_], r_a_,
                         pw=[self.r_actT])
            self.S.flush()

    def phase_p4b(self, l, last):
        nc, NB = self.nc, self.NB
        with contextlib.ExitStack() as ph:
            sb = lambda n, s_, d: ph.enter_context(nc.sbuf_tensor(self.un(n), s_, d))
            ps = lambda n, s_, d: ph.enter_context(nc.psum_tensor(self.un(n), s_, d))
            Wd = sb("Wd", [128, NCF, D], BF16)
            r_Wd = [Res("Wd%d" % i) for i in range(11)]
            self.load_w(Wd, self.w_down[l].rearrange("(kc p) n -> p kc n", p=128), r_Wd, 11)
            env = self.ln_env(sb)
            gB, bB, r_gb = self.load_gb(sb, self.ln2_g[l], self.ln2_b[l], "2")
            aT = [sb("aT%d" % i, [128, NCF, 512], BF16) for i in range(2)]
            r_aT = [Res("aT0"), Res("aT1")]
            Rt = [sb("R%d" % i, [128, 1024], F32) for i in range(2)]
            r_Rt = [Res("R0"), Res("R1")]
            Ht = [sb("H%d" % i, [128, 1024], F32) for i in range(2)]
            r_Ht = [Res("H0"), Res("H1")]
            stage = [sb("hst%d" % i, [128, 8, 512], BF16) for i in range(2)]
            r_stage = [Res("hst0"), Res("hst1")]
            po = [ps("po%d" % i, [128, 512], F32) for i in range(3)]
            r_po = [PRes("po%d" % i) for i in range(3)]
            pt = [ps("pt%d" % i, [128, 1024], BF16) for i in range(2)]
            r_pt = [PRes("pt0"), PRes("pt1")]
            n = 0
            dst, r_dst = (self.out, self.r_out) if last else (self.h_tok, self.r_h_tok)
            lda = lambda b_: self.dma("sp", aT[b_ % 2][:], self.actT[:, :, b_ * 512:(b_ + 1) * 512].rearrange("c p t -> p c t"),
                                      [self.r_actT], r_aT[b_ % 2], wr=[r_aT[b_ % 2]])
            ldH = lambda t_: self.dma("sp", Ht[t_ % 2][:], self.h_tok[t_ * 128:(t_ + 1) * 128, :], [self.r_h_tok],
                                      r_Ht[t_ % 2], wr=[r_Ht[t_ % 2]])
            lda(0)
            ldH(0)
            for b in range(NB):
                a_, r_a_ = aT[b % 2], r_aT[b % 2]
                if b + 1 < NB:
                    lda(b + 1)
                for i in range(4):
                    tt = b * 4 + i
                    R, r_R = Rt[tt % 2], r_Rt[tt % 2]
                    H, r_H = Ht[tt % 2], r_Ht[tt % 2]
                    for half in range(2):
                        n += 1
                        o_, r_o_ = po[n % 3], r_po[n % 3]
                        for c in range(NCF):
                            self.mm(o_[:], a_[:, c, i * 128:(i + 1) * 128], Wd[:, c, half * 512:(half + 1) * 512],
                                    c == 0, c == NCF - 1, [r_a_, r_Wd[c // 2]], r_o_)
                        self.I("dve", "scalar_tensor_tensor", rd=[r_H, r_o_], wr=[r_R] if half == 0 else (),
                               pw=() if half == 0 else [r_R], out=R[:, half * 512:(half + 1) * 512],
                               in0=H[:, half * 512:(half + 1) * 512], scalar=ALPHA, in1=o_[:], op0=ALU.mult, op1=ALU.add)
                    if tt + 1 < self.TT:
                        ldH(tt + 1)
                    if last:
                        self.ln_tile(env, R, r_R, gB, bB, r_gb, tt, dst, r_dst, None, None, None, None)
                    else:
                        self.ln_tile(env, R, r_R, gB, bB, r_gb, tt, dst, r_dst, stage[b % 2], r_stage[b % 2],
                                     pt[tt % 2], r_pt[tt % 2])
                if not last:
                    self.dma("sp", self.hT[:, :, b * 512:(b + 1) * 512].rearrange("c p t -> p c t"),
                             stage[b % 2][:], [r_stage[b % 2]], r_stage[b % 2], pw=[self.r_hT])
            self.S.flush()


def host_consts():
    idx = np.arange(128)
    c = {}
    c["c_ident"] = np.eye(128, dtype=np.float32)
    c["c_trim"] = np.where(idx[:, None] > idx[None, :], NEG, 0.0).astype(np.float32)
    c["c_caus"] = np.where(idx[None, :] > idx[:, None], -1e30, 0.0).astype(np.float32)
    c["c_utri"] = (idx[:, None] <= idx[None, :]).astype(np.float32)
    e = np.zeros((128, 128), np.float32)
    e[127, :] = 1.0
    c["c_e127"] = e
    c["c_pw"] = (0.5 ** np.arange(1, NBIS + 1)).astype(np.float32)
    theta = np.float32(500000.0)
    f16 = theta ** (-np.arange(16, dtype=np.float32) / np.float32(16))
    f8 = theta ** (-np.arange(8, dtype=np.float32) / np.float32(8))
    c["c_invf"] = np.concatenate([f16, f8]).astype(np.float32)
    return c


WEIGHT_KEYS = ["ln_in_g", "ln_in_b", "w_in", "b_f", "kv_norm_g", "w_ukv", "lam_qk", "diff_norm_g", "w_gate",
               "w_br_a", "w_br_b", "w_br_c", "w_out", "ln1_g", "ln1_b", "w_up", "conv_w", "conv_b", "w_down",
               "ln2_g", "ln2_b"]


def make_in_maps(inputs, n_cores, nseq, nlayer):
    consts = host_consts()
    shared = {}
    for k in WEIGHT_KEYS:
        a = np.ascontiguousarray(np.asarray(inputs[k], dtype=np.float32))
        if a.ndim >= 2 or k in ("b_f",):
            pass
        if k not in ("ln_in_g", "ln_in_b"):
            a = a[:nlayer]
        if k == "lam_qk":
            a = a.reshape(a.shape[0], 256)
        shared[k] = np.ascontiguousarray(a)
    shared.update(consts)
    x = np.asarray(inputs["x"], dtype=np.float32)
    pos = np.asarray(inputs["positions"], dtype=np.int32)
    maps = []
    for c in range(n_cores):
        m = dict(shared)
        m["x"] = np.ascontiguousarray(x[c * nseq:(c + 1) * nseq].reshape(-1, D))
        m["positions"] = np.ascontiguousarray(pos[c * nseq:(c + 1) * nseq].reshape(-1))
        maps.append(m)
    return maps


def kernel(**inputs):
    x = np.asarray(inputs["x"])
    B, SL, _ = x.shape
    n_cores = 8
    nseq = B // n_cores
    bld = Builder(SL, nseq, DEPTH)
    nc = bld.build()
    maps = make_in_maps(inputs, n_cores, nseq, DEPTH)
    res = run_bass_kernel_spmd(nc, maps, core_ids=list(range(n_cores)))
    outs = [np.asarray(r["out"], dtype=np.float32).reshape(nseq, SL, D) for r in res.results]
    return np.concatenate(outs, axis=0)
```

```python
import contextlib
import math
import numpy as np
import concourse.bass as bass
import concourse.mybir as mybir
from concourse.bass_utils import run_bass_kernel_spmd

F32 = mybir.dt.float32
BF16 = mybir.dt.bfloat16
I32 = mybir.dt.int32
AF = mybir.ActivationFunctionType
ALU = mybir.AluOpType
AX = mybir.AxisListType

D = 1024
DEPTH = 4
D_FF = 2816
NCF = D_FF // 128
ALPHA = (2 * DEPTH) ** 0.25
LN_EPS = 1e-5
RMS_EPS = 1e-6
TOPK = 256
NEG = -30000.0
NBIS = 16
TWO_PI = 6.283185307179586
C1 = 6.28125
C2 = TWO_PI - C1

SRC = dict(aq=0, ak=512, av=1024, af=1536, bq=1540, bc=2052, biq=2308, bik=2820, biw=2884,
           cq=2892, ck=3404, cv=3916)
WID = dict(aq=512, ak=512, av=512, af=4, bq=512, bc=256, biq=512, bik=64, biw=8,
           cq=512, ck=512, cv=512)
ORDER = ["aq", "ak", "av", "bq", "biq", "cq", "ck", "cv", "bc", "bik", "af", "biw"]
DST = {}
_o = 0
for _k in ORDER:
    DST[_k] = _o
    _o += WID[_k]
D_IN = _o
CH = dict(aq=0, ak=4, bq=8, biq=12, cq=16, ck=20, kb=24, ki=25)
NCH = 26


class Res:
    __slots__ = ("name", "ws", "rs", "prs", "sem", "dummy", "excl")

    def __init__(self, name="", dummy=False):
        self.name = name
        self.ws = {}
        self.rs = {}
        self.prs = {}
        self.sem = None
        self.dummy = dummy
        self.excl = False


def PRes(name):
    r = Res(name)
    r.excl = True
    return r


def _merge(a, b):
    d = dict(a)
    for k, v in b.items():
        if d.get(k, -1) < v:
            d[k] = v
    return d


class Sch:
    COMPUTE = ("pe", "act", "dve", "pool")

    def __init__(self, nc, semstack):
        self.nc = nc
        self.ops = []
        self.base = 0
        self.eng = {"pe": nc.tensor, "act": nc.scalar, "dve": nc.vector,
                    "pool": nc.gpsimd, "sp": nc.sync}
        self.esem = {e: semstack.enter_context(nc.semaphore("e_" + e)) for e in self.COMPUTE}
        self.ecnt = {e: 0 for e in self.COMPUTE}
        self.pool = [[semstack.enter_context(nc.semaphore("d%d" % i)), 0] for i in range(72)]
        self.free = {"pool": list(range(0, 24)), "sp": list(range(24, 72))}
        self.semkind = {}
        self.used = []
        self.waited = {e: {} for e in self.eng}
        self.n_ins = 0
        self.n_wait = 0

    def _key(self, idx):
        o = self.ops[idx - self.base]
        return ("d", id(o[4])) if o[4] is not None else o[0]

    def op(self, eng, fn, reads=(), writes=(), pwrites=(), slot=None):
        idx = self.base + len(self.ops)
        raw = set()
        oth = set()
        reads = [r for r in reads if not r.dummy]
        writes = [w for w in writes if not w.dummy]
        pwrites = [w for w in pwrites if not w.dummy]
        for r in reads:
            raw.update(r.ws.values())
            if r.excl:
                oth.update(v for k, v in r.rs.items() if k != eng)
        for w in writes:
            oth.update(w.ws.values())
            oth.update(w.rs.values())
        for w in pwrites:
            if w.rs:
                oth.update(w.rs.values())
            else:
                oth.update(w.prs.values())
        key = ("d", id(slot)) if slot is not None else eng
        for r in reads:
            r.rs[key] = idx
        for w in writes:
            w.prs = _merge(w.ws, w.rs)
            w.ws = {key: idx}
            w.rs = {}
        for w in pwrites:
            if w.rs:
                w.prs = _merge(w.ws, w.rs)
                w.ws = {key: idx}
                w.rs = {}
            else:
                w.ws[key] = idx
        self.ops.append([eng, fn, raw, oth, slot, False, 0])
        return idx

    def flush(self):
        ops = self.ops
        base = self.base
        last = {}
        for i, o in enumerate(ops):
            eng = o[0]
            for d in o[2]:
                if d >= base:
                    od = ops[d - base]
                    if od[4] is None:
                        od[5] = True
            for d in o[3]:
                if d >= base:
                    od = ops[d - base]
                    if od[4] is None and (od[0] != eng or eng != "pe"):
                        od[5] = True
            if o[4] is None and eng in self.ecnt:
                last[eng] = o
        for o in last.values():
            o[5] = True
        for o in ops:
            if o[4] is not None:
                s = o[4]
                if s.sem is None:
                    kind = "pool" if o[0] == "pool" else "sp"
                    s.sem = self.free[kind].pop()
                    self.semkind[s.sem] = kind
                    self.used.append(s)
                else:
                    assert self.semkind[s.sem] == ("pool" if o[0] == "pool" else "sp"), s.name
                p = self.pool[s.sem]
                p[1] += 16
                o[6] = p[1]
            elif o[5]:
                self.ecnt[o[0]] += 1
                o[6] = self.ecnt[o[0]]
        waited = self.waited
        for o in ops:
            eng = o[0]
            need = {}
            for kind, deps in ((0, o[2]), (1, o[3])):
                for d in deps:
                    if d < base:
                        continue
                    od = ops[d - base]
                    if od[4] is not None:
                        sem = self.pool[od[4].sem][0]
                        k = ("d", od[4].sem)
                    else:
                        if od[0] == eng and kind == 1 and eng == "pe":
                            continue
                        sem = self.esem[od[0]]
                        k = od[0]
                    v = od[6]
                    if waited[eng].get(k, 0) < v and need.get(k, (None, 0))[1] < v:
                        need[k] = (sem, v)
            e = self.eng[eng]
            for k, (sem, v) in need.items():
                e.wait_ge(sem, v)
                waited[eng][k] = v
                self.n_wait += 1
            ins = o[1]()
            self.n_ins += 1
            if o[4] is not None:
                ins.then_inc(self.pool[o[4].sem][0], 16)
            elif o[5]:
                ins.then_inc(self.esem[eng], 1)
        for eng, e in self.eng.items():
            for f in self.COMPUTE:
                if f != eng and waited[eng].get(f, 0) < self.ecnt[f]:
                    e.wait_ge(self.esem[f], self.ecnt[f])
                    waited[eng][f] = self.ecnt[f]
            for s in self.used:
                k = ("d", s.sem)
                v = self.pool[s.sem][1]
                if waited[eng].get(k, 0) < v:
                    e.wait_ge(self.pool[s.sem][0], v)
                    waited[eng][k] = v
        for s in self.used:
            self.free[self.semkind[s.sem]].append(s.sem)
            s.sem = None
        self.used = []
        self.base += len(ops)
        self.ops = []


class Builder:
    def __init__(self, S_len, nseq, nlayer, dbg=()):
        self.SL = S_len
        self.NSEQ = nseq
        self.L = nlayer
        self.NT = S_len * nseq
        self.TT = self.NT // 128
        self.NB = self.NT // 512
        self.TPS = S_len // 128
        self.GPS = S_len // 512
        self.dbg = set(dbg)
        self.nc = bass.Bass("TRN2", target_bir_lowering=False)
        self.gs = contextlib.ExitStack()
        self.S = None

    def un(self, n):
        self._uid = getattr(self, "_uid", 0) + 1
        return "%s_%d" % (n, self._uid)

    def I(self, eng, meth, rd=(), wr=(), pw=(), **kw):
        f = getattr(self.S.eng[eng], meth)
        self.S.op(eng, lambda: f(**kw), reads=rd, writes=wr, pwrites=pw)

    def dma(self, eng, out, in_, rd, slot, wr=(), pw=(), **kw):
        f = self.S.eng[eng].dma_start
        self.S.op(eng, lambda: f(out=out, in_=in_, **kw), reads=rd, writes=wr, pwrites=pw, slot=slot)

    def mm(self, out, lhsT, rhs, start, stop, rd, wr):
        f = self.nc.tensor.matmul
        self.S.op("pe", lambda: f(out, lhsT=lhsT, rhs=rhs, start=start, stop=stop),
                  reads=rd, writes=[wr] if start else (), pwrites=() if start else [wr])

    def tr(self, out, in_, rd, wr, first):
        f = self.nc.tensor.transpose
        idt = self.ident[:]
        self.S.op("pe", lambda: f(out, in_, idt), reads=list(rd) + [self.r_const],
                  writes=[wr] if first else (), pwrites=() if first else [wr])

    def din(self, name, shape, dt):
        return self.nc.dram_tensor(name, list(shape), dt, kind="ExternalInput").ap()

    def dscr(self, name, shape, dt):
        kind = "ExternalOutput" if name in self.dbg else "Internal"
        return self.nc.dram_tensor(name, list(shape), dt, kind=kind).ap()

    def build(self):
        nc = self.nc
        L, NT, TT = self.L, self.NT, self.TT
        with self.gs as gs:
            self.S = Sch(nc, gs)
            sbg = lambda n, s, d: gs.enter_context(nc.sbuf_tensor(self.un(n), s, d))
            self.x = self.din("x", [NT, D], F32)
            self.pos = self.din("positions", [NT], I32)
            self.ln_in_g = self.din("ln_in_g", [D], F32)
            self.ln_in_b = self.din("ln_in_b", [D], F32)
            self.w_in = self.din("w_in", [L, D, 4428], F32)
            self.b_f = self.din("b_f", [L, 4], F32)
            self.kv_norm_g = self.din("kv_norm_g", [L, 256], F32)
            self.w_ukv = self.din("w_ukv", [L, 256, 256], F32)
            self.lam_qk = self.din("lam_qk", [L, 256], F32)
            self.diff_norm_g = self.din("diff_norm_g", [L, 128], F32)
            self.w_gate = self.din("w_gate", [L, D, 3 * D], F32)
            self.w_br = [self.din("w_br_" + c, [L, 512, D], F32) for c in "abc"]
            self.w_out = self.din("w_out", [L, D, D], F32)
            self.ln1_g = self.din("ln1_g", [L, D], F32)
            self.ln1_b = self.din("ln1_b", [L, D], F32)
            self.w_up = self.din("w_up", [L, D, 2 * D_FF], F32)
            self.conv_w = self.din("conv_w", [L, 3, D_FF], F32)
            self.conv_b = self.din("conv_b", [L, D_FF], F32)
            self.w_down = self.din("w_down", [L, D_FF, D], F32)
            self.ln2_g = self.din("ln2_g", [L, D], F32)
            self.ln2_b = self.din("ln2_b", [L, D], F32)
            self.c_ident = self.din("c_ident", [128, 128], F32)
            self.c_trim = self.din("c_trim", [128, 128], F32)
            self.c_caus = self.din("c_caus", [128, 128], F32)
            self.c_utri = self.din("c_utri", [128, 128], F32)
            self.c_e127 = self.din("c_e127", [128, 128], F32)
            self.c_pw = self.din("c_pw", [NBIS], F32)
            self.c_invf = self.din("c_invf", [24], F32)
            self.out = self.nc.dram_tensor("out", [NT, D], F32, kind="ExternalOutput").ap()
            self.h_tok = self.dscr("h_tok", [NT, D], F32)
            self.hT = self.dscr("hT", [8, 128, NT], BF16)
            self.qkT = self.dscr("qkT", [NCH, 128, NT], BF16)
            self.vtok = self.dscr("vtok", [NT, 1152], BF16)
            self.lfw = self.dscr("lfw", [128, TT, 12], F32)
            self.mb = self.dscr("mb", [NT, self.SL], BF16)
            self.oT = self.dscr("oT", [12, 128, NT], BF16)
            self.actT = self.dscr("actT", [NCF, 128, NT], BF16)
            dm_ = lambda n: Res(n, dummy=True)
            self.r_h_tok, self.r_hT, self.r_qkT, self.r_vtok = dm_("h_tok"), dm_("hT"), dm_("qkT"), dm_("vtok")
            self.r_lfw, self.r_mb, self.r_oT, self.r_actT, self.r_out = dm_("lfw"), dm_("mb"), dm_("oT"), dm_("actT"), dm_("out")
            self.r_in = dm_("inputs")
            self.ident = sbg("ident", [128, 128], BF16)
            self.trim = sbg("trim", [128, 128], BF16)
            self.caus = sbg("caus", [128, 128], F32)
            self.utri = sbg("utri", [128, 128], F32)
            self.e127 = sbg("e127", [128, 128], F32)
            self.pw = sbg("pw", [128, NBIS], F32)
            self.cosT = sbg("cosT", [128, TT, 24], F32)
            self.sinT = sbg("sinT", [128, TT, 24], F32)
            self.r_const = Res("const")
            self.r_rope = Res("rope")
            rc = [Res("c%d" % i) for i in range(6)]
            self.dma("pool", self.ident[:], self.c_ident[:, :], [], rc[0], wr=[rc[0]])
            self.dma("pool", self.trim[:], self.c_trim[:, :], [], rc[1], wr=[rc[1]])
            self.dma("sp", self.caus[:], self.c_caus[:, :], [], rc[2], wr=[rc[2]])
            self.dma("sp", self.utri[:], self.c_utri[:, :], [], rc[3], wr=[rc[3]])
            self.dma("sp", self.e127[:], self.c_e127[:, :], [], rc[4], wr=[rc[4]])
            self.dma("sp", self.pw[:], self.c_pw.partition_broadcast(128), [], rc[5], wr=[rc[5]])
            plist = [(self.phase_rope, ()), (self.phase_ln_in, ())]
            for l in range(L):
                plist.append((self.phase_p1, (l,)))
                plist += [(self.phase_attn_a, (l, s)) for s in range(self.NSEQ)]
                plist += [(self.phase_idx, (l, s)) for s in range(self.NSEQ)]
                plist += [(self.phase_attn_b, (l, s)) for s in range(self.NSEQ)]
                plist += [(self.phase_attn_c, (l, s)) for s in range(self.NSEQ)]
                plist += [(self.phase_p3, (l,)), (self.phase_p4a, (l,)), (self.phase_p4b, (l, l == L - 1))]
            for fn, args in plist[:getattr(self, "maxph", 10 ** 9)]:
                fn(*args)
            self.I("sp", "nop")
            self.S.flush()
        return nc

    def phase_rope(self):
        nc, TT = self.nc, self.TT
        with contextlib.ExitStack() as ph:
            sb = lambda n, s, d: ph.enter_context(nc.sbuf_tensor(self.un(n), s, d))
            posi = sb("posi", [128, TT], I32)
            posf = sb("posf", [128, TT], F32)
            invf = sb("invf", [128, 24], F32)
            ang = sb("ang", [128, TT, 24], F32)
            ki = sb("rki", [128, TT, 24], I32)
            kf = sb("rkf", [128, TT, 24], F32)
            r1 = sb("rr1", [128, TT, 24], F32)
            r_p, r_i, r_a, r_k, r_r = Res("posi"), Res("invf"), Res("ang"), Res("rk"), Res("rr")
            self.dma("sp", posi[:], self.pos.rearrange("(j p) -> p j", p=128), [], r_p, wr=[r_p],
                     allow_slow_non_contiguous=True)
            self.dma("sp", invf[:], self.c_invf.partition_broadcast(128), [], r_i, wr=[r_i])
            r_pf = Res("posf")
            self.I("dve", "tensor_copy", rd=[r_p], wr=[r_pf], out=posf[:], in_=posi[:])
            self.I("dve", "tensor_tensor", rd=[r_pf, r_i], wr=[r_a], out=ang[:],
                   in0=posf[:].unsqueeze(2).to_broadcast([128, TT, 24]),
                   in1=invf[:].unsqueeze(1).to_broadcast([128, TT, 24]), op=ALU.mult)
            for which, dst in ((0, self.sinT), (1, self.cosT)):
                src = ang
                if which == 1:
                    self.I("dve", "tensor_scalar", rd=[r_a], wr=[r_r], out=r1[:], in0=ang[:],
                           scalar1=math.pi / 2, scalar2=None, op0=ALU.add)
                    src = r1
                rs_ = r_a if which == 0 else r_r
                r_kf = Res("kf")
                self.I("dve", "tensor_scalar", rd=[rs_], wr=[r_kf], out=kf[:], in0=src[:],
                       scalar1=1.0 / TWO_PI, scalar2=None, op0=ALU.mult)
                self.I("dve", "tensor_copy", rd=[r_kf], wr=[r_k], out=ki[:], in_=kf[:])
                self.I("dve", "tensor_copy", rd=[r_k], wr=[r_kf], out=kf[:], in_=ki[:])
                self.I("dve", "scalar_tensor_tensor", rd=[r_kf, rs_], wr=[r_r], out=r1[:], in0=kf[:],
                       scalar=-C1, in1=src[:], op0=ALU.mult, op1=ALU.add)
                self.I("dve", "scalar_tensor_tensor", rd=[r_kf, r_r], wr=[r_r], out=r1[:], in0=kf[:],
                       scalar=-C2, in1=r1[:], op0=ALU.mult, op1=ALU.add)
                self.I("dve", "tensor_scalar", rd=[r_r], wr=[r_r], out=r1[:], in0=r1[:],
                       scalar1=math.pi, scalar2=-math.pi, op0=ALU.min, op1=ALU.max)
                self.I("act", "activation", rd=[r_r], wr=[self.r_rope], out=dst[:], in_=r1[:], func=AF.Sin)
            self.S.flush()

    def ln_tile(self, env, R, r_R, gB, bB, r_gb, tt, dst, r_dst, stage, r_stage, pt, r_pt):
        st, mv, hb = env["st"], env["mv"], env["hb"]
        k = env["k"] = env.get("k", 0) + 1
        b2 = k % 2
        r_st, r_mv, r_hb = env["r_st"][b2], env["r_mv"][b2], env["r_hb"][b2]
        st_, mv_, hb_ = st[b2], mv[b2], hb[b2]
        self.I("dve", "bn_stats", rd=[r_R], wr=[r_st], out=st_[:, 0:6], in_=R[:, 0:512])
        self.I("dve", "bn_stats", rd=[r_R], pw=[r_st], out=st_[:, 6:12], in_=R[:, 512:1024])
        self.I("dve", "bn_aggr", rd=[r_st], wr=[r_mv], out=mv_[:, 0:2], in_=st_[:, 0:12])
        self.I("dve", "tensor_scalar", rd=[r_mv], wr=[r_mv], out=mv_[:, 2:3], in0=mv_[:, 1:2],
               scalar1=LN_EPS, scalar2=None, op0=ALU.add)
        self.I("act", "activation", rd=[r_mv], wr=[r_mv], out=mv_[:, 3:4], in_=mv_[:, 2:3], func=AF.Sqrt)
        self.I("dve", "reciprocal", rd=[r_mv], wr=[r_mv], out=mv_[:, 4:5], in_=mv_[:, 3:4])
        self.I("dve", "scalar_tensor_tensor", rd=[r_R, r_mv] + list(r_gb), wr=[r_R], out=R[:], in0=R[:],
               scalar=mv_[:, 0:1], in1=gB[:], op0=ALU.subtract, op1=ALU.mult)
        self.I("dve", "scalar_tensor_tensor", rd=[r_R, r_mv] + list(r_gb), wr=[r_R], out=R[:], in0=R[:],
               scalar=mv_[:, 4:5], in1=bB[:], op0=ALU.mult, op1=ALU.add)
        self.dma("sp", dst[tt * 128:(tt + 1) * 128, :], R[:], [r_R], r_R, pw=[r_dst])
        if stage is None:
            return
        self.I("act", "copy", rd=[r_R], wr=[r_hb], out=hb_[:], in_=R[:])
        for c in range(8):
            self.tr(pt[:, c * 128:(c + 1) * 128], hb_[:, c * 128:(c + 1) * 128], [r_hb], r_pt, c == 0)
        i = tt % 4
        self.I("act", "copy", rd=[r_pt], pw=[r_stage], out=stage[:, :, i * 128:(i + 1) * 128],
               in_=pt[:].rearrange("p (c t) -> p c t", c=8))

    def ln_env(self, sb):
        env = dict(st=[sb("ln_st%d" % i, [128, 12], F32) for i in range(2)],
                   mv=[sb("ln_mv%d" % i, [128, 8], F32) for i in range(2)],
                   hb=[sb("ln_hb%d" % i, [128, 1024], BF16) for i in range(2)],
                   r_st=[Res("st0"), Res("st1")], r_mv=[Res("mv0"), Res("mv1")],
                   r_hb=[Res("hb0"), Res("hb1")])
        return env

    def load_gb(self, sb, g_ap, b_ap, tag):
        gB = sb("gB" + tag, [128, 1024], F32)
        bB = sb("bB" + tag, [128, 1024], F32)
        r_g, r_b = Res("gB"), Res("bB")
        self.dma("sp", gB[:], g_ap.partition_broadcast(128), [self.r_in], r_g, wr=[r_g])
        self.dma("sp", bB[:], b_ap.partition_broadcast(128), [self.r_in], r_b, wr=[r_b])
        return gB, bB, [r_g, r_b]

    def phase_ln_in(self):
        nc, TT = self.nc, self.TT
        with contextlib.ExitStack() as ph:
            sb = lambda n, s, d: ph.enter_context(nc.sbuf_tensor(self.un(n), s, d))
            env = self.ln_env(sb)
            gB, bB, r_gb = self.load_gb(sb, self.ln_in_g, self.ln_in_b, "i")
            Rt = [sb("R%d" % i, [128, 1024], F32) for i in range(3)]
            r_Rt = [Res("R%d" % i) for i in range(3)]
            stage = [sb("hst%d" % i, [128, 8, 512], BF16) for i in range(2)]
            r_stage = [Res("hst0"), Res("hst1")]
            pt = [ph.enter_context(nc.psum_tensor("pt%d" % i, [128, 1024], BF16)) for i in range(2)]
            r_pt = [PRes("pt0"), PRes("pt1")]
            ldx = lambda t_: self.dma("sp", Rt[t_ % 3][:], self.x[t_ * 128:(t_ + 1) * 128, :], [self.r_in],
                                      r_Rt[t_ % 3], wr=[r_Rt[t_ % 3]])
            ldx(0)
            for tt in range(TT):
                R, r_R = Rt[tt % 3], r_Rt[tt % 3]
                b = tt // 4
                if tt + 1 < TT:
                    ldx(tt + 1)
                self.ln_tile(env, R, r_R, gB, bB, r_gb[0:2], tt, self.h_tok, self.r_h_tok,
                             stage[b % 2], r_stage[b % 2], pt[tt % 2], r_pt[tt % 2])
                if tt % 4 == 3:
                    self.dma("sp", self.hT[:, :, b * 512:(b + 1) * 512].rearrange("c p t -> p c t"),
                             stage[b % 2][:], [r_stage[b % 2]], r_stage[b % 2], pw=[self.r_hT])
            self.S.flush()

    def rope(self, X, Y, H, Dh, half, off, tt, tmp, r_tmp, r_X, r_Y):
        Xv = X.rearrange("p (h d) -> p h d", h=H)
        Yv = Y.rearrange("p (h d) -> p h d", h=H)
        cos = self.cosT[:, tt, off:off + half].unsqueeze(1).to_broadcast([128, H, half])
        sin = self.sinT[:, tt, off:off + half].unsqueeze(1).to_broadcast([128, H, half])
        x1, x2 = Xv[:, :, 0:half], Xv[:, :, half:2 * half]
        t = [tmp[i][:, 0:H * half].rearrange("p (h d) -> p h d", h=H) for i in range(4)]
        rr = [self.r_rope]
        TTm = lambda o, a, b, op, rd, wr=(), pw=(): self.I("dve", "tensor_tensor", rd=rd, wr=wr, pw=pw, out=o, in0=a, in1=b, op=op)
        RP = 9
        RO = 4
        TTm(t[0], x1, cos, ALU.mult, [r_X, r_Y] + rr, wr=[r_tmp[0]])
        if RO >= 2:
            TTm(t[1], x2, sin, ALU.mult, [r_X] + rr, wr=[r_tmp[1]])
        if RO >= 3:
            TTm(t[2], x2, cos, ALU.mult, [r_X] + rr, wr=[r_tmp[2]])
        if RO >= 4:
            TTm(t[3], x1, sin, ALU.mult, [r_X] + rr, wr=[r_tmp[3]])
        if RP == 1:
            return
        TTm(Yv[:, :, 0:half], t[0], t[1], ALU.subtract, [r_tmp[0], r_tmp[1], r_Y], pw=[r_Y])
        TTm(Yv[:, :, half:2 * half], t[2], t[3], ALU.add, [r_tmp[2], r_tmp[3], r_Y], pw=[r_Y])

    def phase_p1(self, l):
        nc, TT, NB = self.nc, self.TT, self.NB
        with contextlib.ExitStack() as ph:
            sb = lambda n, s, d: ph.enter_context(nc.sbuf_tensor(self.un(n), s, d))
            ps = lambda n, s, d: ph.enter_context(nc.psum_tensor(self.un(n), s, d))
            W = sb("W_in", [128, 8, D_IN], BF16)
            r_W = [Res("W%d" % k) for k in range(8)]
            Wk = sb("W_ukv", [128, 2, 256], BF16)
            r_Wk = Res("Wukv")
            wsrc = self.w_in[l].rearrange("(kc p) n -> p kc n", p=128)
            for kc in range(8):
                for j, name in enumerate(ORDER):
                    self.dma("pool", W[:, kc, DST[name]:DST[name] + WID[name]],
                             wsrc[:, kc, SRC[name]:SRC[name] + WID[name]], [self.r_in], r_W[kc],
                             wr=[r_W[kc]] if j == 0 else (), pw=() if j == 0 else [r_W[kc]])
            self.dma("pool", Wk[:], self.w_ukv[l].rearrange("(kc p) n -> p kc n", p=128), [self.r_in], r_Wk, wr=[r_Wk])
            kvg = sb("kvg", [128, 256], F32)
            bfb = sb("bfb", [128, 4], F32)
            r_kvg = Res("kvg")
            self.dma("sp", kvg[:], self.kv_norm_g[l].partition_broadcast(128), [self.r_in], r_kvg, wr=[r_kvg])
            r_bfb = Res("bfb")
            self.dma("sp", bfb[:], self.b_f[l].partition_broadcast(128), [self.r_in], r_bfb, wr=[r_bfb])
            hTb = [sb("hTb%d" % i, [128, 8, 512], BF16) for i in range(2)]
            r_hTb = [Res("hTb0"), Res("hTb1")]
            Y = [sb("Y%d" % i, [128, D_IN], BF16) for i in range(2)]
            r_Y = [Res("Y0"), Res("Y1")]
            VB = [sb("VB%d" % i, [128, 128], BF16) for i in range(2)]
            r_VB = [Res("VB0"), Res("VB1")]
            ST = [sb("ST%d" % i, [128, NCH, 512], BF16) for i in range(2)]
            r_ST = [Res("ST0"), Res("ST1")]
            LFW = sb("LFW", [128, TT, 12], F32)
            r_LFW = Res("LFW")
            tmp = [sb("rt%d" % i, [128, 64], F32) for i in range(8)]
            r_tmp = [Res("rt%d" % i) for i in range(8)]
            sm = [sb("sm%d" % i, [128, 8], F32) for i in range(2)]
            r_sm = [Res("sm0"), Res("sm1")]
            junk = sb("junk", [128, 256], F32)
            r_junk = Res("junk")
            bcn = [sb("bcn%d" % i, [128, 256], BF16) for i in range(2)]
            r_bcn = [Res("bcn0"), Res("bcn1")]
            bcnT = [sb("bcnT%d" % i, [128, 256], BF16) for i in range(2)]
            r_bcnT = [Res("bcnT0"), Res("bcnT1")]
            kid = [sb("kid%d" % i, [128, 256], BF16) for i in range(2)]
            r_kid = [Res("kid0"), Res("kid1")]
            pc = [ps("pc%d" % i, [128, 512], F32) for i in range(4)]
            r_pc = [PRes("pc%d" % i) for i in range(4)]
            ptr = [ps("ptr%d" % i, [128, 1024], BF16) for i in range(2)]
            r_ptr = [PRes("ptr0"), PRes("ptr1")]
            pkv = ps("pkv", [128, 256], F32)
            r_pkv = PRes("pkv")
            npc = 0
            ntr = 0
            LV = 9
            ldh = lambda b_: self.dma("sp", hTb[b_ % 2][:], self.hT[:, :, b_ * 512:(b_ + 1) * 512].rearrange("c p t -> p c t"),
                                      [self.r_hT], r_hTb[b_ % 2], wr=[r_hTb[b_ % 2]])
            ldh(0)
            for b in range(NB if LV > 0 else 0):
                hb_, r_hb_ = hTb[b % 2], r_hTb[b % 2]
                if b + 1 < NB:
                    ldh(b + 1)
                st_, r_st_ = ST[b % 2], r_ST[b % 2]
                for i in range(4):
                    tt = b * 4 + i
                    y, r_y = Y[tt % 2], r_Y[tt % 2]
                    t4 = tmp[0:4] if tt % 2 == 0 else tmp[4:8]
                    r_t4 = r_tmp[0:4] if tt % 2 == 0 else r_tmp[4:8]
                    sm_, r_sm_ = sm[tt % 2], r_sm[tt % 2]
                    for c in range(9):
                        n = 512 if c < 8 else D_IN - 4096
                        p, r_p = pc[npc % 4], r_pc[npc % 4]
                        npc += 1
                        for kc in range(8):
                            self.mm(p[:, 0:n], hb_[:, kc, i * 128:(i + 1) * 128], W[:, kc, c * 512:c * 512 + n],
                                    kc == 0, kc == 7, [r_hb_, r_W[kc]], r_p)
                        if c < 8:
                            self.I("act", "copy", rd=[r_p], wr=[r_y] if c == 0 else (), pw=() if c == 0 else [r_y],
                                   out=y[:, c * 512:(c + 1) * 512], in_=p[:, 0:512])
                            if LV < 2:
                                pass
                            elif c == 3:
                                self.rope(p[:, 0:512], y[:, 1536:2048], 4, 128, 16, 0, tt, t4, r_t4, r_p, r_y)
                            elif c in (4, 5, 6):
                                self.rope(p[:, 0:512], y[:, c * 512:(c + 1) * 512], 8, 64, 8, 16, tt, t4, r_t4, r_p, r_y)
                        elif LV >= 3:
                            bn_, r_bn_ = bcn[tt % 2], r_bcn[tt % 2]
                            kd_, r_kd_ = kid[tt % 2], r_kid[tt % 2]
                            self.I("dve", "memset", wr=[r_sm_], ap=sm_[:], constant=0.0)
                            self.I("act", "activation", rd=[r_p, r_sm_], wr=[r_junk], pw=[r_sm_], out=junk[:],
                                   in_=p[:, 0:256], func=AF.Square, accum_out=sm_[:, 0:1])
                            self.I("dve", "tensor_scalar", rd=[r_sm_], wr=[r_sm_], out=sm_[:, 1:2], in0=sm_[:, 0:1],
                                   scalar1=1.0 / 256, scalar2=RMS_EPS, op0=ALU.mult, op1=ALU.add)
                            self.I("act", "activation", rd=[r_sm_], wr=[r_sm_], out=sm_[:, 2:3], in_=sm_[:, 1:2], func=AF.Sqrt)
                            self.I("dve", "reciprocal", rd=[r_sm_], wr=[r_sm_], out=sm_[:, 3:4], in_=sm_[:, 2:3])
                            self.I("dve", "scalar_tensor_tensor", rd=[r_p, r_sm_, r_kvg], wr=[r_bn_], out=bn_[:],
                                   in0=p[:, 0:256], scalar=sm_[:, 3:4], in1=kvg[:], op0=ALU.mult, op1=ALU.mult)
                            self.I("act", "copy", rd=[r_p], wr=[r_kd_], out=kd_[:, 128:192], in_=p[:, 256:320])
                            self.rope(p[:, 256:320], kd_[:, 128:192], 1, 64, 8, 16, tt, t4, r_t4, r_p, r_kd_)
                            self.I("dve", "tensor_copy", rd=[r_kd_], pw=[r_kd_], out=kd_[:, 192:256], in_=kd_[:, 128:192])
                            lf = LFW[:, tt, 0:4]
                            self.I("dve", "tensor_tensor", rd=[r_p, r_bfb], wr=[r_sm_], out=sm_[:, 4:8], in0=p[:, 320:324],
                                   in1=bfb[:], op=ALU.add)
                            self.I("act", "activation", rd=[r_sm_], wr=[r_sm_], out=sm_[:, 4:8], in_=sm_[:, 4:8],
                                   func=AF.Exp, scale=-1.0)
                            self.I("act", "activation", rd=[r_sm_], wr=[r_sm_], out=sm_[:, 4:8], in_=sm_[:, 4:8],
                                   func=AF.Ln, bias=1.0, scale=1.0)
                            self.I("dve", "tensor_scalar", rd=[r_sm_], pw=[r_LFW], out=lf, in0=sm_[:, 4:8],
                                   scalar1=-1.0, scalar2=None, op0=ALU.mult)
                            self.I("act", "copy", rd=[r_p], pw=[r_LFW], out=LFW[:, tt, 4:12], in_=p[:, 324:332])
                            pt_, r_pt_ = ptr[ntr % 2], r_ptr[ntr % 2]
                            ntr += 1
                            bT, r_bT = bcnT[tt % 2], r_bcnT[tt % 2]
                            self.tr(pt_[:, 0:128], bn_[:, 0:128], [r_bn_], r_pt_, True)
                            self.tr(pt_[:, 128:256], bn_[:, 128:256], [r_bn_], r_pt_, False)
                            self.I("dve", "tensor_copy", rd=[r_pt_], wr=[r_bT], out=bT[:], in_=pt_[:, 0:256])
                            self.mm(pkv[:], bT[:, 0:128], Wk[:, 0, :], True, False, [r_bT, r_Wk], r_pkv)
                            self.mm(pkv[:], bT[:, 128:256], Wk[:, 1, :], False, True, [r_bT, r_Wk], r_pkv)
                            vb_, r_vb_ = VB[tt % 2], r_VB[tt % 2]
                            self.I("act", "copy", rd=[r_pkv], wr=[r_vb_], out=vb_[:], in_=pkv[:, 128:256])
                            self.I("act", "copy", rd=[r_pkv], pw=[r_kd_], out=kd_[:, 0:128], in_=pkv[:, 0:128])
                            self.rope(pkv[:, 0:128], kd_[:, 0:128], 1, 128, 16, 0, tt, t4, r_t4, r_pkv, r_kd_)
                    if LV < 4:
                        continue
                    groups = [("aq", 0), ("ak", 512), ("bq", 1536), ("biq", 2048), ("cq", 2560), ("ck", 3072)]
                    for gi in range(0, 6, 2):
                        pt_, r_pt_ = ptr[ntr % 2], r_ptr[ntr % 2]
                        ntr += 1
                        for g2 in range(2):
                            name, yoff = groups[gi + g2]
                            for j in range(4):
                                self.tr(pt_[:, (g2 * 4 + j) * 128:(g2 * 4 + j + 1) * 128],
                                        y[:, yoff + j * 128: yoff + (j + 1) * 128], [r_y], r_pt_, g2 == 0 and j == 0)
                        for g2 in range(2):
                            name, yoff = groups[gi + g2]
                            eng = "act" if g2 == 0 else "dve"
                            meth = "copy" if g2 == 0 else "tensor_copy"
                            self.I(eng, meth, rd=[r_pt_], pw=[r_st_],
                                   out=st_[:, CH[name]:CH[name] + 4, i * 128:(i + 1) * 128],
                                   in_=pt_[:, g2 * 512:(g2 + 1) * 512].rearrange("p (c t) -> p c t", c=4))
                    pt_, r_pt_ = ptr[ntr % 2], r_ptr[ntr % 2]
                    ntr += 1
                    kd_, r_kd_ = kid[tt % 2], r_kid[tt % 2]
                    self.tr(pt_[:, 0:128], kd_[:, 0:128], [r_kd_], r_pt_, True)
                    self.tr(pt_[:, 128:256], kd_[:, 128:256], [r_kd_], r_pt_, False)
                    self.I("dve", "tensor_copy", rd=[r_pt_], pw=[r_st_],
                           out=st_[:, CH["kb"]:CH["kb"] + 2, i * 128:(i + 1) * 128],
                           in_=pt_[:, 0:256].rearrange("p (c t) -> p c t", c=2))
                    if LV < 5:
                        continue
                    rows = slice(tt * 128, (tt + 1) * 128)
                    self.dma("sp", self.vtok[rows, 0:512], y[:, 1024:1536], [r_y], r_y, pw=[self.r_vtok])
                    self.dma("sp", self.vtok[rows, 640:1152], y[:, 3584:4096], [r_y], r_y, pw=[self.r_vtok])
                    self.dma("sp", self.vtok[rows, 512:640], VB[tt % 2][:], [r_VB[tt % 2]], r_VB[tt % 2], pw=[self.r_vtok])
                if LV >= 6:
                    self.dma("sp", self.qkT[:, :, b * 512:(b + 1) * 512].rearrange("c p t -> p c t"), st_[:],
                             [r_st_], r_st_, pw=[self.r_qkT])
            if LV >= 6:
                self.dma("sp", self.lfw[:, :, :], LFW[:], [r_LFW], r_LFW, pw=[self.r_lfw])
            self.S.flush()

    def attn_core(self, env, qT, kT, V, rd_q, rd_k, rd_v, scale, bias_fn, rd_bias, mask, fin, Gs=None):
        GPS = self.GPS
        pss, r_pss = env["pss"], env["r_pss"]
        oa, r_oa = env["OA"], env["r_OA"]
        PT, r_PT = env["PT"], env["r_PT"]
        q = env.setdefault("queue", [])
        NP = len(pss)
        NPT = len(PT)
        for G in (range(GPS) if Gs is None else Gs):
            for J in range(4 * G + 4):
                i0 = max(J - 4 * G, 0)
                n = env["n"] = env.get("n", 0) + 1
                p, r_p = pss[n % NP], r_pss[n % NP]
                pt, r_pt = PT[n % NPT], r_PT[n % NPT]
                c0 = i0 * 128
                diag = J >= 4 * G
                self.mm(p[:, c0:512], kT[:, J * 128:(J + 1) * 128], qT[:, G * 512 + c0:(G + 1) * 512], True,
                        (mask is None and not diag), rd_k + rd_q, r_p)
                if mask is not None:
                    MB, r_MB = mask
                    for i in range(i0, 4):
                        self.mm(p[:, i * 128:(i + 1) * 128], MB[:, i, J * 128:(J + 1) * 128], self.ident[:],
                                False, i == 3, [r_MB, self.r_const], r_p)
                elif diag:
                    self.mm(p[:, c0:c0 + 128], self.ident[:], self.trim[:], False, True, [self.r_const], r_p)
                if bias_fn is None:
                    self.I("act", "activation", rd=[r_p], wr=[r_pt], out=pt[:, c0:512], in_=p[:, c0:512],
                           func=AF.Exp, scale=scale)
                else:
                    first = True
                    for ip in range(2):
                        lo, hi = max(c0, 256 * ip), 256 * (ip + 1)
                        if lo >= hi:
                            continue
                        self.I("act", "activation", rd=[r_p] + rd_bias, wr=[r_pt] if first else (),
                               pw=() if first else [r_pt], out=pt[:, lo:hi], in_=p[:, lo:hi], func=AF.Exp,
                               scale=scale, bias=bias_fn(4 * G + 2 * ip + 1, J))
                        first = False

                def stage2(G=G, J=J, i0=i0, pt=pt, r_pt=r_pt, V=V, rd_v=rd_v, fin=fin):
                    for i in range(i0, 4):
                        self.mm(oa[i][:, 0:129], pt[:, i * 128:(i + 1) * 128], V[:, J, :],
                                J == 0, J == 4 * G + i, [r_pt] + rd_v, r_oa[i])
                    if J == 4 * G + 3:
                        for i in range(4):
                            fin(G, i, oa[i][:, 0:129], r_oa[i])
                q.append(stage2)
                if len(q) > 2:
                    q.pop(0)()

    def attn_drain(self, env):
        q = env.setdefault("queue", [])
        while q:
            q.pop(0)()

    def attn_env(self, sb, ps):
        env = dict(pss=[ps("pss%d" % i, [128, 512], F32) for i in range(3)], r_pss=[PRes("pss%d" % i) for i in range(3)],
                   OA=[ps("OA%d" % i, [128, 512], F32) for i in range(4)], r_OA=[PRes("OA%d" % i) for i in range(4)],
                   PT=[sb("PT%d" % i, [128, 512], BF16) for i in range(4)], r_PT=[Res("PT%d" % i) for i in range(4)],
                   ptr=ps("aptr", [128, 1024], BF16), r_ptr=PRes("aptr"),
                   ob=[sb("ob%d" % i, [128, 512], BF16) for i in range(2)], r_ob=[Res("ob0"), Res("ob1")],
                   ost=[sb("ost%d" % i, [128, 512], BF16) for i in range(2)], r_ost=[Res("ost0"), Res("ost1")],
                   rec=[sb("rec%d" % i, [128, 8], F32) for i in range(4)], r_rec=[Res("rec%d" % i) for i in range(4)])
        return env

    def attn_store(self, env, ob, r_ob, chunk, tok0):
        k = env["k"] = env.get("k", 0) + 1
        ptr, r_ptr = env["ptr"], env["r_ptr"]
        ost, r_ost = env["ost"][k % 2], env["r_ost"][k % 2]
        for i in range(4):
            self.tr(ptr[:, i * 128:(i + 1) * 128], ob[:, i * 128:(i + 1) * 128], [r_ob], r_ptr, i == 0)
        self.I("dve", "tensor_copy", rd=[r_ptr], wr=[r_ost], out=ost[:], in_=ptr[:, 0:512])
        self.dma("sp", self.oT[chunk, :, tok0:tok0 + 512], ost[:], [r_ost], r_ost, pw=[self.r_oT])

    def load_v(self, Vt, r_V, s, col0):
        self.dma("sp", Vt[:, :, 0:128],
                 self.vtok[s * self.SL:(s + 1) * self.SL, col0:col0 + 128].rearrange("(j p) d -> p j d", p=128),
                 [self.r_vtok], r_V, wr=[r_V])

    def phase_attn_a(self, l, s):
        nc, SL, TPS = self.nc, self.SL, self.TPS
        t0 = s * SL
        with contextlib.ExitStack() as ph:
            sb = lambda n, s_, d: ph.enter_context(nc.sbuf_tensor(self.un(n), s_, d))
            ps = lambda n, s_, d: ph.enter_context(nc.psum_tensor(self.un(n), s_, d))
            LF = sb("LF", [128, TPS, 12], F32)
            r_LF = Res("LF")
            self.dma("sp", LF[:], self.lfw[:, s * TPS:(s + 1) * TPS, :], [self.r_lfw], r_LF, wr=[r_LF])
            lf = sb("lf4", [128, 4, TPS], F32)
            r_lf = Res("lf4")
            self.I("dve", "tensor_copy", rd=[r_LF], wr=[r_lf], out=lf[:], in_=LF[:, :, 0:4].rearrange("p j h -> p h j"))
            ph2 = contextlib.ExitStack()
            pcs = ph2.enter_context(nc.psum_tensor(self.un("pcs"), [128, 4 * TPS], F32))
            r_pcs = PRes("pcs")
            lf2 = lf[:].rearrange("p h j -> p (h j)")
            self.mm(pcs[:], self.utri[:], lf2, True, True, [r_lf, self.r_const], r_pcs)
            cs = sb("cs", [128, 4, TPS], F32)
            r_cs = Res("cs")
            self.I("dve", "tensor_copy", rd=[r_pcs], wr=[r_cs], out=cs[:].rearrange("p h j -> p (h j)"), in_=pcs[:])
            ex = sb("ex", [128, 4, TPS], F32)
            r_ex = Res("ex")
            self.I("dve", "memset", wr=[r_ex], ap=ex[:, :, 0:1], constant=0.0)
            for j in range(1, TPS):
                self.I("dve", "tensor_tensor", rd=[r_ex, r_cs], wr=[r_ex], out=ex[:, :, j:j + 1], in0=ex[:, :, j - 1:j],
                       in1=cs[:, :, j - 1:j], op=ALU.add)
            self.mm(pcs[:], self.e127[:], ex[:].rearrange("p h j -> p (h j)"), True, True, [r_ex, self.r_const], r_pcs)
            cum = sb("cum", [128, 4, TPS], F32)
            r_cum = Res("cum")
            self.I("dve", "tensor_tensor", rd=[r_pcs, r_cs], wr=[r_cum], out=cum[:].rearrange("p h j -> p (h j)"),
                   in0=pcs[:], in1=cs[:].rearrange("p h j -> p (h j)"), op=ALU.add)
            self.mm(pcs[:], self.e127[:], cum[:].rearrange("p h j -> p (h j)"), True, True, [r_cum, self.r_const], r_pcs)
            cend = sb("cend", [128, 4, TPS], F32)
            r_cend = Res("cend")
            self.I("dve", "tensor_copy", rd=[r_pcs], wr=[r_cend], out=cend[:].rearrange("p h j -> p (h j)"), in_=pcs[:])
            bias = sb("biasA", [128, 4, TPS, TPS], F32)
            r_bias = Res("biasA")
            for h in range(4):
                for I_ in range(TPS):
                    self.I("dve", "tensor_scalar", rd=[r_cum, r_cend], pw=[r_bias], out=bias[:, h, I_, 0:I_ + 1],
                           in0=cum[:, h, 0:I_ + 1], scalar1=-1.0, scalar2=cend[:, h, I_:I_ + 1], op0=ALU.mult, op1=ALU.add)
            self.S.flush()
            ph2.close()
            env = self.attn_env(sb, ps)
            qT = [sb("qTa%d" % i, [128, SL], BF16) for i in range(2)]
            kT = [sb("kTa%d" % i, [128, SL], BF16) for i in range(2)]
            Vt = [sb("Va%d" % i, [128, TPS, 129], BF16) for i in range(2)]
            r_q, r_k, r_v = [Res("q0"), Res("q1")], [Res("k0"), Res("k1")], [Res("v0"), Res("v1")]
            r_v1 = [Res("v10"), Res("v11")]
            for i in range(2):
                self.I("pool", "memset", wr=[r_v1[i]], ap=Vt[i][:, :, 128:129], constant=1.0)
            for h in range(4):
                b2 = h % 2
                self.dma("sp", qT[b2][:], self.qkT[CH["aq"] + h, :, t0:t0 + SL], [self.r_qkT], r_q[b2], wr=[r_q[b2]])
                self.dma("sp", kT[b2][:], self.qkT[CH["ak"] + h, :, t0:t0 + SL], [self.r_qkT], r_k[b2], wr=[r_k[b2]])
                self.load_v(Vt[b2], r_v[b2], s, h * 128)

                def fin(G, i, O, r_O, h=h):
                    k = env["fk"] = env.get("fk", 0) + 1
                    rec, r_rec = env["rec"][k % 4], env["r_rec"][k % 4]
                    ob, r_ob = env["ob"][(k - 1) // 4 % 2], env["r_ob"][(k - 1) // 4 % 2]
                    self.I("dve", "reciprocal", rd=[r_O], wr=[r_rec], out=rec[:, 0:1], in_=O[:, 128:129])
                    self.I("dve", "tensor_scalar", rd=[r_O, r_rec], wr=[r_ob] if i == 0 else (), pw=() if i == 0 else [r_ob],
                               out=ob[:, i * 128:(i + 1) * 128], in0=O[:, 0:128], scalar1=rec[:, 0:1], scalar2=None, op0=ALU.mult)
                    if i == 3:
                        self.attn_store(env, ob, r_ob, h, t0 + G * 512)

                self.attn_core(env, qT[b2][:], kT[b2][:], Vt[b2], [r_q[b2]], [r_k[b2]], [r_v[b2], r_v1[b2]],
                               128 ** -0.5, lambda I_, J, h=h: bias[:, h, I_, J:J + 1], [r_bias], None, fin)
            self.attn_drain(env)
            self.S.flush()

    def phase_attn_c(self, l, s):
        nc, SL, TPS = self.nc, self.SL, self.TPS
        t0 = s * SL
        lam_init = 0.8 - 0.6 * math.exp(-0.3 * l)
        with contextlib.ExitStack() as ph:
            sb = lambda n, s_, d: ph.enter_context(nc.sbuf_tensor(self.un(n), s_, d))
            ps = lambda n, s_, d: ph.enter_context(nc.psum_tensor(self.un(n), s_, d))
            env = self.attn_env(sb, ps)
            lq = sb("lq", [128, 256], F32)
            r_lq = Res("lq")
            self.dma("sp", lq[:], self.lam_qk[l].partition_broadcast(128), [self.r_in], r_lq, wr=[r_lq])
            lt = sb("lt", [128, 128], F32)
            r_lt = Res("lt")
            lv = sb("lv", [128, 8], F32)
            r_lv = Res("lv")
            self.I("dve", "memset", wr=[r_lv], ap=lv[:], constant=0.0)
            self.I("dve", "tensor_tensor", rd=[r_lq], wr=[r_lt], out=lt[:, 0:64], in0=lq[:, 0:64], in1=lq[:, 64:128], op=ALU.mult)
            self.I("dve", "tensor_tensor", rd=[r_lq], pw=[r_lt], out=lt[:, 64:128], in0=lq[:, 128:192], in1=lq[:, 192:256], op=ALU.mult)
            self.I("dve", "reduce_sum", rd=[r_lt], wr=[r_lv], out=lv[:, 0:1], in_=lt[:, 0:64], axis=AX.X)
            self.I("dve", "reduce_sum", rd=[r_lt, r_lv], wr=[r_lv], out=lv[:, 1:2], in_=lt[:, 64:128], axis=AX.X)
            self.I("act", "activation", rd=[r_lv], wr=[r_lv], out=lv[:, 2:4], in_=lv[:, 0:2], func=AF.Exp)
            self.I("dve", "tensor_tensor", rd=[r_lv], wr=[r_lv], out=lv[:, 4:5], in0=lv[:, 3:4], in1=lv[:, 2:3], op=ALU.subtract)
            self.I("dve", "tensor_scalar", rd=[r_lv], wr=[r_lv], out=lv[:, 5:6], in0=lv[:, 4:5], scalar1=-lam_init,
                   scalar2=None, op0=ALU.add)
            dg = sb("dg", [128, 128], F32)
            r_dg = Res("dg")
            self.dma("sp", dg[:], self.diff_norm_g[l].partition_broadcast(128), [self.r_in], r_dg, wr=[r_dg])
            self.I("dve", "tensor_scalar", rd=[r_dg], wr=[r_dg], out=dg[:], in0=dg[:], scalar1=1.0 - lam_init,
                   scalar2=None, op0=ALU.mult)
            qT = [sb("qTc%d" % i, [128, SL], BF16) for i in range(2)]
            kz = [[sb("kz%d_%d" % (c, i), [128, SL], BF16) for i in range(2)] for c in range(2)]
            r_kz = [[Res("kz"), Res("kz")] for c in range(2)]
            r_kzz = [[Res("kzz"), Res("kzz")] for c in range(2)]
            for c in range(2):
                for i in range(2):
                    for j0 in range(0, SL, 1024):
                        self.I("dve", "memset", wr=[r_kzz[c][i]] if j0 == 0 else (), pw=() if j0 == 0 else [r_kzz[c][i]],
                               ap=kz[c][i][64 * (1 - c):64 * (1 - c) + 64, j0:j0 + 1024], constant=0.0)
            Vt = [sb("Vc%d" % i, [128, TPS, 129], BF16) for i in range(2)]
            r_q, r_v = [Res("q0"), Res("q1")], [Res("v0"), Res("v1")]
            r_v1 = [Res("v10"), Res("v11")]
            on0 = [sb("on0_%d" % i, [128, 4, 128], F32) for i in range(2)]
            r_on0 = [Res("on0_0"), Res("on0_1")]
            dd = [sb("dd%d" % i, [128, 128], F32) for i in range(2)]
            r_dd = [Res("dd0"), Res("dd1")]
            jk = sb("jkc", [128, 128], F32)
            r_jk = Res("jkc")
            for i in range(2):
                self.I("pool", "memset", wr=[r_v1[i]], ap=Vt[i][:, :, 128:129], constant=1.0)
            for h in range(4):
                b2 = h % 2
                self.dma("sp", qT[b2][:], self.qkT[CH["cq"] + h, :, t0:t0 + SL], [self.r_qkT], r_q[b2], wr=[r_q[b2]])
                for c in range(2):
                    self.dma("sp", kz[c][b2][64 * c:64 * c + 64, :], self.qkT[CH["ck"] + h, 64 * c:64 * c + 64, t0:t0 + SL],
                             [self.r_qkT], r_kz[c][b2], wr=[r_kz[c][b2]])
                self.load_v(Vt[b2], r_v[b2], s, 640 + h * 128)
                for G in range(self.GPS):
                    gk = env["gk"] = env.get("gk", 0) + 1
                    o0, r_o0 = on0[gk % 2], r_on0[gk % 2]

                    def fin0(G, i, O, r_O, o0=o0, r_o0=r_o0):
                        k = env["fk"] = env.get("fk", 0) + 1
                        rec, r_rec = env["rec"][k % 4], env["r_rec"][k % 4]
                        self.I("dve", "reciprocal", rd=[r_O], wr=[r_rec], out=rec[:, 0:1], in_=O[:, 128:129])
                        self.I("dve", "tensor_scalar", rd=[r_O, r_rec], wr=[r_o0] if i == 0 else (), pw=() if i == 0 else [r_o0],
                               out=o0[:, i, :], in0=O[:, 0:128], scalar1=rec[:, 0:1], scalar2=None, op0=ALU.mult)

                    def fin1(G, i, O, r_O, o0=o0, r_o0=r_o0, h=h):
                        k = env["fk"] = env.get("fk", 0) + 1
                        rec, r_rec = env["rec"][k % 4], env["r_rec"][k % 4]
                        d_, r_d = dd[k % 2], r_dd[k % 2]
                        kk = env["ck"] = env.get("ck", 0) + 1
                        ob, r_ob = env["ob"][(kk - 1) // 4 % 2], env["r_ob"][(kk - 1) // 4 % 2]
                        self.I("dve", "memset", wr=[r_rec], ap=rec[:], constant=0.0)
                        self.I("dve", "reciprocal", rd=[r_O, r_rec], wr=[r_rec], out=rec[:, 0:1], in_=O[:, 128:129])
                        self.I("dve", "tensor_tensor", rd=[r_rec, r_lv], wr=[r_rec], out=rec[:, 1:2], in0=rec[:, 0:1],
                               in1=lv[:, 5:6], op=ALU.mult)
                        self.I("dve", "scalar_tensor_tensor", rd=[r_O, r_rec, r_o0], wr=[r_d], out=d_[:], in0=O[:, 0:128],
                               scalar=rec[:, 1:2], in1=o0[:, i, :], op0=ALU.mult, op1=ALU.add)
                        self.I("dve", "tensor_tensor", rd=[r_d], wr=[r_jk], out=jk[:], in0=d_[:], in1=d_[:], op=ALU.mult)
                        self.I("dve", "reduce_sum", rd=[r_jk, r_rec], wr=[r_rec], out=rec[:, 2:3], in_=jk[:], axis=AX.X)
                        self.I("dve", "tensor_scalar", rd=[r_rec], wr=[r_rec], out=rec[:, 3:4], in0=rec[:, 2:3],
                               scalar1=1.0 / 128, scalar2=RMS_EPS, op0=ALU.mult, op1=ALU.add)
                        self.I("act", "activation", rd=[r_rec], wr=[r_rec], out=rec[:, 4:5], in_=rec[:, 3:4], func=AF.Sqrt)
                        self.I("dve", "reciprocal", rd=[r_rec], wr=[r_rec], out=rec[:, 5:6], in_=rec[:, 4:5])
                        self.I("dve", "scalar_tensor_tensor", rd=[r_d, r_rec, r_dg], wr=[r_ob] if i == 0 else (),
                               pw=() if i == 0 else [r_ob], out=ob[:, i * 128:(i + 1) * 128], in0=d_[:],
                               scalar=rec[:, 5:6], in1=dg[:], op0=ALU.mult, op1=ALU.mult)
                        if i == 3:
                            self.attn_store(env, ob, r_ob, 8 + h, t0 + G * 512)

                    for c in range(2):
                        self.attn_core(env, qT[b2][:], kz[c][b2][:], Vt[b2],
                                       [r_q[b2]], [r_kz[c][b2], r_kzz[c][b2]], [r_v[b2], r_v1[b2]], 64 ** -0.5, None, [], None,
                                       fin0 if c == 0 else fin1, Gs=[G])
            self.attn_drain(env)
            self.S.flush()

    def phase_idx(self, l, s):
        nc, SL, TPS = self.nc, self.SL, self.TPS
        t0 = s * SL
        with contextlib.ExitStack() as ph:
            sb = lambda n, s_, d: ph.enter_context(nc.sbuf_tensor(self.un(n), s_, d))
            ps = lambda n, s_, d: ph.enter_context(nc.psum_tensor(self.un(n), s_, d))
            kiT = sb("kiT", [128, SL], BF16)
            qiT = sb("qiT", [128, 4, SL], BF16)
            wi = sb("wi", [128, TPS, 12], F32)
            r_ki, r_qi, r_wi = Res("kiT"), Res("qiT"), Res("wi")
            self.dma("sp", kiT[:], self.qkT[CH["ki"], :, t0:t0 + SL], [self.r_qkT], r_ki, wr=[r_ki])
            self.dma("sp", qiT[:], self.qkT[CH["biq"]:CH["biq"] + 4, :, t0:t0 + SL].rearrange("c p t -> p c t"),
                     [self.r_qkT], r_qi, wr=[r_qi])
            self.dma("sp", wi[:], self.lfw[:, s * TPS:(s + 1) * TPS, :], [self.r_lfw], r_wi, wr=[r_wi])
            SC = [sb("SC%d" % i, [128, SL], F32) for i in range(2)]
            r_SC = [Res("SC0"), Res("SC1")]
            MBt = [sb("MBt%d" % i, [128, SL], BF16) for i in range(2)]
            r_MBt = [Res("MBt0"), Res("MBt1")]
            Rl = [sb("Rl%d" % i, [128, 1024], F32) for i in range(3)]
            r_Rl = [Res("Rl%d" % i) for i in range(3)]
            jk = sb("jki", [128, SL], BF16)
            r_jk = Res("jki")
            bs = [sb("bs%d" % i, [128, 8 + 2 * NBIS], F32) for i in range(2)]
            r_bs = [Res("bs0"), Res("bs1")]
            bsA = [sb("bsA%d" % i, [128, NBIS], F32) for i in range(2)]
            r_bsA = [Res("bsA0"), Res("bsA1")]
            bsD = [sb("bsD%d" % i, [128, NBIS], F32) for i in range(2)]
            r_bsD = [Res("bsD0"), Res("bsD1")]
            jk2 = sb("jki2", [128, SL], BF16)
            r_jk2 = Res("jki2")
            pp = [ps("pi%d" % i, [128, 1024], F32) for i in range(3)]
            r_pp = [PRes("pi%d" % i) for i in range(3)]
            cnt = {"n": 0}

            def units(I_):
                q1 = (I_ + 1) * 128
                return [(I_, c, hh, min(1024, q1 - c * 1024)) for c in range((q1 + 1023) // 1024) for hh in range(8)]

            def unit(I_, c, hh, w):
                sc, r_sc = SC[I_ % 2], r_SC[I_ % 2]
                n = cnt["n"] = cnt["n"] + 1
                p, r_p = pp[n % 3], r_pp[n % 3]
                r0 = 64 * (hh % 2)
                for j0 in range(0, w, 512):
                    w2 = min(512, w - j0)
                    self.mm(p[:, j0:j0 + w2], qiT[r0:r0 + 64, hh // 2, I_ * 128:(I_ + 1) * 128],
                            kiT[r0:r0 + 64, c * 1024 + j0:c * 1024 + j0 + w2], True, True, [r_qi, r_ki], r_p)
                if hh == 0:
                    self.I("dve", "tensor_scalar", rd=[r_p, r_wi], wr=[r_sc] if c == 0 else (),
                           pw=() if c == 0 else [r_sc], out=sc[:, c * 1024:c * 1024 + w], in0=p[:, 0:w],
                           scalar1=0.0, scalar2=wi[:, I_, 4:5], op0=ALU.max, op1=ALU.mult)
                else:
                    rl, r_rl = Rl[n % 3], r_Rl[n % 3]
                    self.I("act", "activation", rd=[r_p], wr=[r_rl], out=rl[:, 0:w], in_=p[:, 0:w], func=AF.Relu)
                    self.I("dve", "scalar_tensor_tensor", rd=[r_rl, r_wi, r_sc], pw=[r_sc],
                           out=sc[:, c * 1024:c * 1024 + w], in0=rl[:, 0:w], scalar=wi[:, I_, 4 + hh:5 + hh],
                           in1=sc[:, c * 1024:c * 1024 + w], op0=ALU.mult, op1=ALU.add)

            def final(I_):
                sc, r_sc = SC[I_ % 2], r_SC[I_ % 2]
                q1 = (I_ + 1) * 128
                self.I("dve", "tensor_tensor", rd=[r_sc, self.r_const], wr=[r_sc], out=sc[:, I_ * 128:q1],
                       in0=sc[:, I_ * 128:q1], in1=self.caus[:], op=ALU.add)

            def mb_out(I_):
                sc, r_sc = SC[I_ % 2], r_SC[I_ % 2]
                b_, r_b = bs[I_ % 2], r_bs[I_ % 2]
                q1 = (I_ + 1) * 128
                mbt, r_mbt = MBt[I_ % 2], r_MBt[I_ % 2]
                self.I("dve", "tensor_scalar", rd=[r_sc, r_b], wr=[r_mbt], out=mbt[:, 0:q1], in0=sc[:, 0:q1],
                       scalar1=b_[:, 5:6], scalar2=NEG, op0=ALU.is_lt, op1=ALU.mult)
                self.dma("sp", self.mb[t0 + I_ * 128:t0 + q1, 0:q1], mbt[:, 0:q1], [r_mbt], r_mbt, pw=[self.r_mb])

            for I_ in range(min(2, TPS)):
                for u in units(I_):
                    unit(*u)
                final(I_)
                self.I("dve", "memset", wr=[r_bs[I_ % 2]], ap=bs[I_ % 2][:, 5:6], constant=-1e29)
                mb_out(I_)
            if TPS > 2:
                for u in units(2):
                    unit(*u)
                final(2)
            for I_ in range(2, TPS):
                q1 = (I_ + 1) * 128
                sc, r_sc = SC[I_ % 2], r_SC[I_ % 2]
                b_, r_b = bs[I_ % 2], r_bs[I_ % 2]
                nxt = units(I_ + 1) if I_ + 1 < TPS else []
                bA, r_bA = bsA[I_ % 2], r_bsA[I_ % 2]
                bD, r_bD = bsD[I_ % 2], r_bsD[I_ % 2]
                self.I("dve", "memset", wr=[r_b], ap=b_[:], constant=0.0)
                self.I("dve", "memset", wr=[r_bA], ap=bA[:], constant=0.0)
                self.I("dve", "memset", wr=[r_bD], ap=bD[:], constant=0.0)
                self.I("dve", "reduce_max", rd=[r_sc, r_b], wr=[r_b], out=b_[:, 0:1], in_=sc[:, 0:q1], axis=AX.X)
                self.I("dve", "tensor_reduce", rd=[r_sc, r_b], wr=[r_b], out=b_[:, 1:2], in_=sc[:, 0:I_ * 128],
                       axis=AX.X, op=ALU.min)
                self.I("dve", "tensor_tensor", rd=[r_b], wr=[r_b], out=b_[:, 2:3], in0=b_[:, 0:1], in1=b_[:, 1:2],
                       op=ALU.subtract)
                self.I("dve", "tensor_scalar", rd=[r_b, self.r_const], wr=[r_b], out=b_[:, 8:8 + NBIS], in0=self.pw[:],
                       scalar1=b_[:, 2:3], scalar2=None, op0=ALU.mult)
                self.I("dve", "scalar_tensor_tensor", rd=[r_b], wr=[r_b], out=b_[:, 3:4], in0=b_[:, 2:3], scalar=0.5,
                       in1=b_[:, 1:2], op0=ALU.mult, op1=ALU.add)
                done = 0
                a = min(q1, max(128, int(0.85 * q1 / 128) * 128))
                for k in range(NBIS):
                    cc = 8 + NBIS + k
                    self.I("act", "activation", rd=[r_sc, r_b], wr=[r_jk], pw=[r_bA], out=jk[:, 0:a], in_=sc[:, 0:a],
                           func=AF.Sign, scale=-1.0, bias=b_[:, 3:4], accum_out=bA[:, k:k + 1])
                    if a < q1:
                        self.I("dve", "tensor_scalar", rd=[r_sc, r_b], wr=[r_jk2], pw=[r_bD], out=jk2[:, a:q1], in0=sc[:, a:q1],
                               scalar1=b_[:, 3:4], scalar2=0.0, op0=ALU.is_ge, op1=ALU.add, accum_out=bD[:, k:k + 1])
                    self.I("dve", "scalar_tensor_tensor", rd=[r_bA, r_bD], wr=[r_b], out=b_[:, 6:7], in0=bD[:, k:k + 1],
                           scalar=2.0, in1=bA[:, k:k + 1], op0=ALU.mult, op1=ALU.subtract)
                    self.I("dve", "tensor_scalar", rd=[r_b], wr=[r_b], out=b_[:, 4:5], in0=b_[:, 6:7],
                           scalar1=float(511.5 - a), scalar2=0.5, op0=ALU.is_ge, op1=ALU.subtract)
                    self.I("dve", "scalar_tensor_tensor", rd=[r_b], wr=[r_b], out=b_[:, 3:4], in0=b_[:, 4:5],
                           scalar=b_[:, 8 + k:9 + k], in1=b_[:, 3:4], op0=ALU.mult, op1=ALU.add)
                    upto = (len(nxt) * (k + 1)) // NBIS
                    for u in nxt[done:upto]:
                        unit(*u)
                    done = upto
                self.I("dve", "tensor_tensor", rd=[r_b], wr=[r_b], out=b_[:, 5:6], in0=b_[:, 3:4],
                       in1=b_[:, 8 + NBIS - 1:8 + NBIS], op=ALU.subtract)
                if I_ + 1 < TPS:
                    final(I_ + 1)
                mb_out(I_)
            self.S.flush()

    def phase_attn_b(self, l, s):
        nc, SL, TPS = self.nc, self.SL, self.TPS
        t0 = s * SL
        with contextlib.ExitStack() as ph:
            sb = lambda n, s_, d: ph.enter_context(nc.sbuf_tensor(self.un(n), s_, d))
            ps = lambda n, s_, d: ph.enter_context(nc.psum_tensor(self.un(n), s_, d))
            env = self.attn_env(sb, ps)
            qT = sb("qTb", [128, 4, SL], BF16)
            kT = sb("kTb", [128, SL], BF16)
            Vt = sb("Vb", [128, TPS, 129], BF16)
            r_q, r_k, r_v, r_v1 = Res("qb"), Res("kb"), Res("vb"), Res("vb1")
            self.I("pool", "memset", wr=[r_v1], ap=Vt[:, :, 128:129], constant=1.0)
            self.dma("sp", qT[:], self.qkT[CH["bq"]:CH["bq"] + 4, :, t0:t0 + SL].rearrange("c p t -> p c t"),
                     [self.r_qkT], r_q, wr=[r_q])
            self.dma("sp", kT[:], self.qkT[CH["kb"], :, t0:t0 + SL], [self.r_qkT], r_k, wr=[r_k])
            self.load_v(Vt, r_v, s, 512)
            MB = [sb("MB%d" % i, [128, 4, SL], BF16) for i in range(2)]
            r_MB = [Res("MB0"), Res("MB1")]
            for G in range(self.GPS):
                mbt, r_mbt = MB[G % 2], r_MB[G % 2]
                for i in range(4):
                    q1 = (4 * G + i + 1) * 128
                    self.dma("sp", mbt[:, i, 0:q1], self.mb[t0 + q1 - 128:t0 + q1, 0:q1],
                             [self.r_mb], r_mbt, wr=[r_mbt] if i == 0 else (), pw=() if i == 0 else [r_mbt])
                for h in range(4):
                    def fin(G, i, O, r_O, h=h):
                        k = env["fk"] = env.get("fk", 0) + 1
                        rec, r_rec = env["rec"][k % 4], env["r_rec"][k % 4]
                        ob, r_ob = env["ob"][(k - 1) // 4 % 2], env["r_ob"][(k - 1) // 4 % 2]
                        self.I("dve", "reciprocal", rd=[r_O], wr=[r_rec], out=rec[:, 0:1], in_=O[:, 128:129])
                        self.I("dve", "tensor_scalar", rd=[r_O, r_rec], wr=[r_ob] if i == 0 else (), pw=() if i == 0 else [r_ob],
                               out=ob[:, i * 128:(i + 1) * 128], in0=O[:, 0:128], scalar1=rec[:, 0:1], scalar2=None, op0=ALU.mult)
                        if i == 3:
                            self.attn_store(env, ob, r_ob, 4 + h, t0 + G * 512)
                    self.attn_core(env, qT[:, h, :], kT[:], Vt, [r_q], [r_k], [r_v, r_v1], 128 ** -0.5, None, [],
                                   (mbt, r_mbt), fin, Gs=[G])
            self.attn_drain(env)
            self.S.flush()

    def load_w(self, dst, src, r, nparts=1):
        n1 = dst.shape[1]
        step = (n1 + nparts - 1) // nparts
        for j, a in enumerate(range(0, n1, step)):
            b = min(n1, a + step)
            self.dma("pool", dst[:, a:b], src[:, a:b], [self.r_in], r[j], wr=[r[j]])

    def phase_p3(self, l):
        nc, NB = self.nc, self.NB
        with contextlib.ExitStack() as ph:
            sb = lambda n, s_, d: ph.enter_context(nc.sbuf_tensor(self.un(n), s_, d))
            ps = lambda n, s_, d: ph.enter_context(nc.psum_tensor(self.un(n), s_, d))
            Wg = sb("Wg", [128, 8, 3 * D], BF16)
            r_Wg = [Res("Wg%d" % i) for i in range(8)]
            self.load_w(Wg, self.w_gate[l].rearrange("(kc p) n -> p kc n", p=128), r_Wg, 8)
            Wb = [sb("Wb%d" % i, [128, 4, D], BF16) for i in range(3)]
            r_Wb = [[Res("Wb%d" % i)] for i in range(3)]
            for i in range(3):
                self.load_w(Wb[i], self.w_br[i][l].rearrange("(kc p) n -> p kc n", p=128), r_Wb[i], 1)
            Wo = sb("Wo", [128, 8, D], BF16)
            r_Wo = [Res("Wo0"), Res("Wo1")]
            self.load_w(Wo, self.w_out[l].rearrange("(kc p) n -> p kc n", p=128), r_Wo, 2)
            env = self.ln_env(sb)
            gB, bB, r_gb = self.load_gb(sb, self.ln1_g[l], self.ln1_b[l], "1")
            hTb = [sb("hTb0", [128, 8, 512], BF16)]
            r_hTb = [Res("hTb0")]
            oTb = [sb("oTb0", [128, 12, 512], BF16)]
            r_oTb = [Res("oTb0")]
            gx = [sb("gx%d" % i, [128, 512], F32) for i in range(3)]
            r_gx = [Res("gx%d" % i) for i in range(3)]
            tm = [sb("tm%d" % i, [128, 512], F32) for i in range(3)]
            r_tm = [Res("tm%d" % i) for i in range(3)]
            mT = sb("mT", [128, 8, 512], BF16)
            r_mT = [Res("mT%d" % i) for i in range(8)]
            Rt = [sb("R%d" % i, [128, 1024], F32) for i in range(2)]
            r_Rt = [Res("R0"), Res("R1")]
            Ht = [sb("H%d" % i, [128, 1024], F32) for i in range(2)]
            r_Ht = [Res("H0"), Res("H1")]
            stage = [sb("hst%d" % i, [128, 8, 512], BF16) for i in range(2)]
            r_stage = [Res("hst0"), Res("hst1")]
            pg = [ps("pg%d" % i, [128, 512], F32) for i in range(2)]
            r_pg = [PRes("pg0"), PRes("pg1")]
            pb = [ps("pb%d" % i, [128, 512], F32) for i in range(2)]
            r_pb = [PRes("pb0"), PRes("pb1")]
            po = [ps("po%d" % i, [128, 512], F32) for i in range(2)]
            r_po = [PRes("po0"), PRes("po1")]
            pt = [ps("pt%d" % i, [128, 1024], BF16) for i in range(2)]
            r_pt = [PRes("pt0"), PRes("pt1")]
            n = 0

            def ldb(b_):
                self.dma("sp", hTb[0][:], self.hT[:, :, b_ * 512:(b_ + 1) * 512].rearrange("c p t -> p c t"),
                         [self.r_hT], r_hTb[0], wr=[r_hTb[0]])
                self.dma("sp", oTb[0][:], self.oT[:, :, b_ * 512:(b_ + 1) * 512].rearrange("c p t -> p c t"),
                         [self.r_oT], r_oTb[0], wr=[r_oTb[0]])
            ldH = lambda t_: self.dma("sp", Ht[t_ % 2][:], self.h_tok[t_ * 128:(t_ + 1) * 128, :], [self.r_h_tok],
                                      r_Ht[t_ % 2], wr=[r_Ht[t_ % 2]])
            ldb(0)
            ldH(0)
            for b in range(NB):
                hb_, r_hb_ = hTb[0], r_hTb[0]
                ob_, r_ob_ = oTb[0], r_oTb[0]
                for dm in range(8):
                    for x in range(3):
                        n += 1
                        g_, r_g_ = pg[n % 2], r_pg[n % 2]
                        b_, r_b_ = pb[n % 2], r_pb[n % 2]
                        gs_, r_gs_ = gx[n % 3], r_gx[n % 3]
                        for kc in range(8):
                            self.mm(g_[:], Wg[:, kc, x * D + dm * 128:x * D + (dm + 1) * 128], hb_[:, kc, :],
                                    kc == 0, kc == 7, [r_Wg[kc], r_hb_], r_g_)
                        self.I("act", "activation", rd=[r_g_], wr=[r_gs_], out=gs_[:], in_=g_[:], func=AF.Sigmoid)
                        for kc in range(4):
                            self.mm(b_[:], Wb[x][:, kc, dm * 128:(dm + 1) * 128], ob_[:, 4 * x + kc, :],
                                    kc == 0, kc == 3, [r_Wb[x][0], r_ob_], r_b_)
                        t_, r_t_ = tm[x], r_tm[x]
                        self.I("dve", "tensor_tensor", rd=[r_gs_, r_b_], wr=[r_t_], out=t_[:], in0=gs_[:], in1=b_[:], op=ALU.mult)
                    self.I("pool", "tensor_tensor", rd=[r_tm[0], r_tm[1]], wr=[r_tm[0]], out=tm[0][:], in0=tm[0][:],
                           in1=tm[1][:], op=ALU.add)
                    self.I("pool", "tensor_tensor", rd=[r_tm[0], r_tm[2]], wr=[r_mT[dm]], out=mT[:, dm, :], in0=tm[0][:],
                           in1=tm[2][:], op=ALU.add)
                if b + 1 < NB:
                    ldb(b + 1)
                for i in range(4):
                    tt = b * 4 + i
                    R, r_R = Rt[tt % 2], r_Rt[tt % 2]
                    H, r_H = Ht[tt % 2], r_Ht[tt % 2]
                    for half in range(2):
                        n += 1
                        o_, r_o_ = po[n % 2], r_po[n % 2]
                        for dm in range(8):
                            self.mm(o_[:], mT[:, dm, i * 128:(i + 1) * 128], Wo[:, dm, half * 512:(half + 1) * 512],
                                    dm == 0, dm == 7, [r_mT[dm], r_Wo[dm // 4]], r_o_)
                        self.I("dve", "scalar_tensor_tensor", rd=[r_H, r_o_], wr=[r_R] if half == 0 else (),
                               pw=() if half == 0 else [r_R], out=R[:, half * 512:(half + 1) * 512],
                               in0=H[:, half * 512:(half + 1) * 512], scalar=ALPHA, in1=o_[:], op0=ALU.mult, op1=ALU.add)
                    if tt + 1 < self.TT:
                        ldH(tt + 1)
                    self.ln_tile(env, R, r_R, gB, bB, r_gb, tt, self.h_tok, self.r_h_tok, stage[b % 2], r_stage[b % 2],
                                 pt[tt % 2], r_pt[tt % 2])
                self.dma("sp", self.hT[:, :, b * 512:(b + 1) * 512].rearrange("c p t -> p c t"),
                         stage[b % 2][:], [r_stage[b % 2]], r_stage[b % 2], pw=[self.r_hT])
            self.S.flush()

    def phase_p4a(self, l):
        nc, NB = self.nc, self.NB
        BPS = self.SL // 512
        with contextlib.ExitStack() as ph:
            sb = lambda n, s_, d: ph.enter_context(nc.sbuf_tensor(self.un(n), s_, d))
            ps = lambda n, s_, d: ph.enter_context(nc.psum_tensor(self.un(n), s_, d))
            Wu = sb("Wu", [128, 8, 2 * D_FF], BF16)
            r_Wu = [Res("Wu%d" % i) for i in range(8)]
            self.load_w(Wu, self.w_up[l].rearrange("(kc p) n -> p kc n", p=128), r_Wu, 8)
            cw = sb("cw", [128, NCF, 4], F32)
            r_cw = Res("cw")
            for j in range(3):
                self.dma("sp", cw[:, :, j:j + 1], self.conv_w[l, j].rearrange("(c p o) -> p c o", p=128, o=1), [self.r_in], r_cw,
                         wr=[r_cw] if j == 0 else (), pw=() if j == 0 else [r_cw], allow_slow_non_contiguous=True)
            self.dma("sp", cw[:, :, 3:4], self.conv_b[l].rearrange("(c p o) -> p c o", p=128, o=1), [self.r_in], r_cw, pw=[r_cw],
                     allow_slow_non_contiguous=True)
            hTb = [sb("hTb%d" % i, [128, 8, 512], BF16) for i in range(2)]
            r_hTb = [Res("hTb0"), Res("hTb1")]
            aT = [sb("aT%d" % i, [128, NCF, 512], BF16) for i in range(2)]
            r_aT = [Res("aT0"), Res("aT1")]
            Gt = [sb("Gt%d" % i, [128, 514], F32) for i in range(3)]
            r_Gt = [Res("Gt%d" % i) for i in range(3)]
            cv = [sb("cv%d" % i, [128, 512], F32) for i in range(3)]
            r_cv = [Res("cv%d" % i) for i in range(3)]
            sl = [sb("sl%d" % i, [128, 512], F32) for i in range(3)]
            r_sl = [Res("sl%d" % i) for i in range(3)]
            halo = sb("halo", [128, NCF, 2], F32)
            r_halo = [Res("halo%d" % c) for c in range(NCF)]
            pg = [ps("pg%d" % i, [128, 512], F32) for i in range(3)]
            r_pg = [PRes("pg%d" % i) for i in range(3)]
            pv = [ps("pv%d" % i, [128, 512], F32) for i in range(3)]
            r_pv = [PRes("pv%d" % i) for i in range(3)]
            n = 0
            ldh = lambda b_: self.dma("sp", hTb[b_ % 2][:], self.hT[:, :, b_ * 512:(b_ + 1) * 512].rearrange("c p t -> p c t"),
                                      [self.r_hT], r_hTb[b_ % 2], wr=[r_hTb[b_ % 2]])
            ldh(0)
            for b in range(NB):
                hb_, r_hb_ = hTb[b % 2], r_hTb[b % 2]
                a_, r_a_ = aT[b % 2], r_aT[b % 2]
                if b + 1 < NB:
                    ldh(b + 1)
                for c in range(NCF):
                    n += 1
                    g_, r_g_ = pg[n % 3], r_pg[n % 3]
                    v_, r_v_ = pv[n % 3], r_pv[n % 3]
                    G_, r_G_ = Gt[n % 3], r_Gt[n % 3]
                    c_, r_c_ = cv[n % 3], r_cv[n % 3]
                    s_, r_s_ = sl[n % 3], r_sl[n % 3]
                    for kc in range(8):
                        self.mm(g_[:], Wu[:, kc, c * 128:(c + 1) * 128], hb_[:, kc, :], kc == 0, kc == 7, [r_Wu[kc], r_hb_], r_g_)
                    for kc in range(8):
                        self.mm(v_[:], Wu[:, kc, D_FF + c * 128:D_FF + (c + 1) * 128], hb_[:, kc, :], kc == 0, kc == 7,
                                [r_Wu[kc], r_hb_], r_v_)
                    self.I("act", "copy", rd=[r_g_], wr=[r_G_], out=G_[:, 2:514], in_=g_[:])
                    if b % BPS == 0:
                        self.I("pool", "memset", pw=[r_G_], ap=G_[:, 0:2], constant=0.0)
                    else:
                        self.I("pool", "tensor_copy", rd=[r_halo[c]], pw=[r_G_], out=G_[:, 0:2], in_=halo[:, c, :])
                    self.I("pool", "tensor_copy", rd=[r_G_], wr=[r_halo[c]], out=halo[:, c, :], in_=G_[:, 512:514])
                    self.I("dve", "tensor_scalar", rd=[r_G_, r_cw], wr=[r_c_], out=c_[:], in0=G_[:, 2:514],
                           scalar1=cw[:, c, 2:3], scalar2=cw[:, c, 3:4], op0=ALU.mult, op1=ALU.add)
                    self.I("dve", "scalar_tensor_tensor", rd=[r_G_, r_cw, r_c_], wr=[r_c_], out=c_[:], in0=G_[:, 1:513],
                           scalar=cw[:, c, 1:2], in1=c_[:], op0=ALU.mult, op1=ALU.add)
                    self.I("dve", "scalar_tensor_tensor", rd=[r_G_, r_cw, r_c_], wr=[r_c_], out=c_[:], in0=G_[:, 0:512],
                           scalar=cw[:, c, 0:1], in1=c_[:], op0=ALU.mult, op1=ALU.add)
                    self.I("act", "activation", rd=[r_c_], wr=[r_s_], out=s_[:], in_=c_[:], func=AF.Silu)
                    self.I("dve", "tensor_tensor", rd=[r_s_, r_v_], wr=[r_a_] if c == 0 else (), pw=() if c == 0 else [r_a_],
                           out=a_[:, c, :], in0=s_[:], in1=v_[:], op=ALU.mult)
                self.dma("sp", self.actT[:, :, b * 512:(b + 1) * 512].rearrange("c p t -> p c t"), a_[:], [r_a_], r_a_,
                         pw=[self.r_actT])
            self.S.flush()

    def phase_p4b(self, l, last):
        nc, NB = self.nc, self.NB
        with contextlib.ExitStack() as ph:
            sb = lambda n, s_, d: ph.enter_context(nc.sbuf_tensor(self.un(n), s_, d))
            ps = lambda n, s_, d: ph.enter_context(nc.psum_tensor(self.un(n), s_, d))
            Wd = sb("Wd", [128, NCF, D], BF16)
            r_Wd = [Res("Wd%d" % i) for i in range(11)]
            self.load_w(Wd, self.w_down[l].rearrange("(kc p) n -> p kc n", p=128), r_Wd, 11)
            env = self.ln_env(sb)
            gB, bB, r_gb = self.load_gb(sb, self.ln2_g[l], self.ln2_b[l], "2")
            aT = [sb("aT%d" % i, [128, NCF, 512], BF16) for i in range(2)]
            r_aT = [Res("aT0"), Res("aT1")]
            Rt = [sb("R%d" % i, [128, 1024], F32) for i in range(2)]
            r_Rt = [Res("R0"), Res("R1")]
            Ht = [sb("H%d" % i, [128, 1024], F32) for i in range(2)]
            r_Ht = [Res("H0"), Res("H1")]
            stage = [sb("hst%d" % i, [128, 8, 512], BF16) for i in range(2)]
            r_stage = [Res("hst0"), Res("hst1")]
            po = [ps("po%d" % i, [128, 512], F32) for i in range(3)]
            r_po = [PRes("po%d" % i) for i in range(3)]
            pt = [ps("pt%d" % i, [128, 1024], BF16) for i in range(2)]
            r_pt = [PRes("pt0"), PRes("pt1")]
            n = 0
            dst, r_dst = (self.out, self.r_out) if last else (self.h_tok, self.r_h_tok)
            lda = lambda b_: self.dma("sp", aT[b_ % 2][:], self.actT[:, :, b_ * 512:(b_ + 1) * 512].rearrange("c p t -> p c t"),
                                      [self.r_actT], r_aT[b_ % 2], wr=[r_aT[b_ % 2]])
            ldH = lambda t_: self.dma("sp", Ht[t_ % 2][:], self.h_tok[t_ * 128:(t_ + 1) * 128, :], [self.r_h_tok],
                                      r_Ht[t_ % 2], wr=[r_Ht[t_ % 2]])
            lda(0)
            ldH(0)
            for b in range(NB):
                a_, r_a_ = aT[b % 2], r_aT[b % 2]
                if b + 1 < NB:
                    lda(b + 1)
                for i in range(4):
                    tt = b * 4 + i
                    R, r_R = Rt[tt % 2], r_Rt[tt % 2]
                    H, r_H = Ht[tt % 2], r_Ht[tt % 2]
                    for half in range(2):
                        n += 1
                        o_, r_o_ = po[n % 3], r_po[n % 3]
                        for c in range(NCF):
                            self.mm(o_[:], a_[:, c, i * 128:(i + 1) * 128], Wd[:, c, half * 512:(half + 1) * 512],
                                    c == 0, c == NCF - 1, [r_a_, r_Wd[c // 2]], r_o_)
                        self.I("dve", "scalar_tensor_tensor", rd=[r_H, r_o_], wr=[r_R] if half == 0 else (),
                               pw=() if half == 0 else [r_R], out=R[:, half * 512:(half + 1) * 512],
                               in0=H[:, half * 512:(half + 1) * 512], scalar=ALPHA, in1=o_[:], op0=ALU.mult, op1=ALU.add)
                    if tt + 1 < self.TT:
                        ldH(tt + 1)
                    if last:
                        self.ln_tile(env, R, r_R, gB, bB, r_gb, tt, dst, r_dst, None, None, None, None)
                    else:
                        self.ln_tile(env, R, r_R, gB, bB, r_gb, tt, dst, r_dst, stage[b % 2], r_stage[b % 2],
                                     pt[tt % 2], r_pt[tt % 2])
                if not last:
                    self.dma("sp", self.hT[:, :, b * 512:(b + 1) * 512].rearrange("c p t -> p c t"),
                             stage[b % 2][:], [r_stage[b % 2]], r_stage[b % 2], pw=[self.r_hT])
            self.S.flush()


def host_consts():
    idx = np.arange(128)
    c = {}
    c["c_ident"] = np.eye(128, dtype=np.float32)
    c["c_trim"] = np.where(idx[:, None] > idx[None, :], NEG, 0.0).astype(np.float32)
    c["c_caus"] = np.where(idx[None, :] > idx[:, None], -1e30, 0.0).astype(np.float32)
    c["c_utri"] = (idx[:, None] <= idx[None, :]).astype(np.float32)
    e = np.zeros((128, 128), np.float32)
    e[127, :] = 1.0
    c["c_e127"] = e
    c["c_pw"] = (0.5 ** np.arange(1, NBIS + 1)).astype(np.float32)
    theta = np.float32(500000.0)
    f16 = theta ** (-np.arange(16, dtype=np.float32) / np.float32(16))
    f8 = theta ** (-np.arange(8, dtype=np.float32) / np.float32(8))
    c["c_invf"] = np.concatenate([f16, f8]).astype(np.float32)
    return c


WEIGHT_KEYS = ["ln_in_g", "ln_in_b", "w_in", "b_f", "kv_norm_g", "w_ukv", "lam_qk", "diff_norm_g", "w_gate",
               "w_br_a", "w_br_b", "w_br_c", "w_out", "ln1_g", "ln1_b", "w_up", "conv_w", "conv_b", "w_down",
               "ln2_g", "ln2_b"]


def make_in_maps(inputs, n_cores, nseq, nlayer):
    consts = host_consts()
    shared = {}
    for k in WEIGHT_KEYS:
        a = np.ascontiguousarray(np.asarray(inputs[k], dtype=np.float32))
        if a.ndim >= 2 or k in ("b_f",):
            pass
        if k not in ("ln_in_g", "ln_in_b"):
            a = a[:nlayer]
        if k == "lam_qk":
            a = a.reshape(a.shape[0], 256)
        shared[k] = np.ascontiguousarray(a)
    shared.update(consts)
    x = np.asarray(inputs["x"], dtype=np.float32)
    pos = np.asarray(inputs["positions"], dtype=np.int32)
    maps = []
    for c in range(n_cores):
        m = dict(shared)
        m["x"] = np.ascontiguousarray(x[c * nseq:(c + 1) * nseq].reshape(-1, D))
        m["positions"] = np.ascontiguousarray(pos[c * nseq:(c + 1) * nseq].reshape(-1))
        maps.append(m)
    return maps


def kernel(**inputs):
    x = np.asarray(inputs["x"])
    B, SL, _ = x.shape
    n_cores = 8
    nseq = B // n_cores
    bld = Builder(SL, nseq, DEPTH)
    nc = bld.build()
    maps = make_in_maps(inputs, n_cores, nseq, DEPTH)
    res = run_bass_kernel_spmd(nc, maps, core_ids=list(range(n_cores)))
    outs = [np.asarray(r["out"], dtype=np.float32).reshape(nseq, SL, D) for r in res.results]
    return np.concatenate(outs, axis=0)
```
